# Optimizing a Trainium2 kernel written in Bass

```python
import math
import jax, jax.numpy as jnp
from jax import lax
import numpy as np

D_MODEL = 1024
BATCH = 8
SEQ = 4096
DEPTH = 1
DEC_BATCH = 4
DEC_SEQ = 4096
PAST_LEN = 128

GRID_W = 64
P_DIM = 256
GLA_HEADS = 4
GLA_DK = 64
GLA_DV = 128
GLA_GATE_RANK = 16
GLA_TAU = 16.0
GLA_CHUNK = 64
ATTN_HEADS = 8
ATTN_KV_HEADS = 2
ATTN_DH = 64
Q_BLOCK = 128
ROPE_THETA = 10000.0
D_FF = 2816
LN_EPS = 1e-5
QK_EPS = 1e-6
GN_EPS = 1e-5
DEEPNORM_ALPHA = (2.0 * DEPTH) ** 0.25
DEEPNORM_BETA = (8.0 * DEPTH) ** -0.25

GLA_QK_W = GLA_HEADS * GLA_DK
GLA_V_W = GLA_HEADS * GLA_DV
ATTN_Q_W = ATTN_HEADS * ATTN_DH
ATTN_KV_W = ATTN_KV_HEADS * ATTN_DH
IN_SPLITS = (GLA_QK_W, GLA_QK_W, GLA_V_W, GLA_V_W, GLA_GATE_RANK, GLA_GATE_RANK, ATTN_Q_W, ATTN_KV_W, ATTN_KV_W)
D_IN = sum(IN_SPLITS)
MIX_W = GLA_V_W + ATTN_Q_W

kernel_name = "hymba_gla_axialrope_gqa_macaron_deepnorm_encoder"


def layer_norm(x, g, b):
    xf = x.astype(jnp.float32)
    mu = jnp.mean(xf, axis=-1, keepdims=True)
    var = jnp.mean(jnp.square(xf - mu), axis=-1, keepdims=True)
    y = (xf - mu) * lax.rsqrt(var + LN_EPS)
    return (y * g.astype(jnp.float32) + b.astype(jnp.float32)).astype(x.dtype)


def rms_norm(x, g, eps):
    xf = x.astype(jnp.float32)
    y = xf * lax.rsqrt(jnp.mean(jnp.square(xf), axis=-1, keepdims=True) + eps)
    return (y * g.astype(jnp.float32)).astype(x.dtype)


def swiglu(x, w_g, w_u, w_d):
    return (jax.nn.silu(x @ w_g) * (x @ w_u)) @ w_d


def gla_direction(q, k, v, log_a, inclusive):
    bsz, n, h, dk = q.shape
    dv = v.shape[-1]
    c = GLA_CHUNK
    nc = n // c
    q = q.reshape(bsz, nc, c, h, dk)
    k = k.reshape(bsz, nc, c, h, dk)
    v = v.reshape(bsz, nc, c, h, dv)
    cum = jnp.cumsum(log_a.reshape(bsz, nc, c, h, dk), axis=2)
    mid = cum[:, :, c // 2 - 1:c // 2]
    last = cum[:, :, c - 1:]
    scores = jnp.einsum('bnihd,bnjhd->bnhij', q * jnp.exp(cum - mid), k * jnp.exp(mid - cum))
    mask = jnp.tril(jnp.ones((c, c), dtype=bool), 0 if inclusive else -1)
    scores = jnp.where(mask, scores, 0.0)
    o_intra = jnp.einsum('bnhij,bnjhe->bnihe', scores, v)
    upd = jnp.einsum('bnjhd,bnjhe->bnhde', k * jnp.exp(last - cum), v)
    chunk_decay = jnp.exp(last[:, :, 0])

    def step(state, inp):
        dec, u = inp
        return dec[..., None] * state + u, state

    s0 = jnp.zeros((bsz, h, dk, dv), q.dtype)
    _, states = lax.scan(step, s0, (jnp.moveaxis(chunk_decay, 1, 0), jnp.moveaxis(upd, 1, 0)))
    states = jnp.moveaxis(states, 0, 1)
    o_inter = jnp.einsum('bnihd,bnhde->bnihe', q * jnp.exp(cum), states)
    return (o_intra + o_inter).reshape(bsz, n, h, dv)


def gla_mixer(gq, gk, gv, gr, zf, zb, w2f, b2f, w2b, b2b, gn_g):
    bsz, n, _ = gq.shape
    dt = gq.dtype
    f32 = jnp.float32
    q = gq.astype(f32).reshape(bsz, n, GLA_HEADS, GLA_DK) * (GLA_DK ** -0.5)
    k = gk.astype(f32).reshape(bsz, n, GLA_HEADS, GLA_DK)
    v = gv.astype(f32).reshape(bsz, n, GLA_HEADS, GLA_DV)
    log_af = (jax.nn.log_sigmoid((zf @ w2f + b2f).astype(f32)) / GLA_TAU).reshape(bsz, n, GLA_HEADS, GLA_DK)
    log_ab = (jax.nn.log_sigmoid((zb @ w2b + b2b).astype(f32)) / GLA_TAU).reshape(bsz, n, GLA_HEADS, GLA_DK)
    o_f = gla_direction(q, k, v, log_af, True)
    o_b = gla_direction(q[:, ::-1], k[:, ::-1], v[:, ::-1], log_ab[:, ::-1], False)[:, ::-1]
    o = rms_norm(o_f + o_b, gn_g, GN_EPS).reshape(bsz, n, GLA_V_W)
    return (o * jax.nn.silu(gr.astype(f32))).astype(dt)


def axial_rope_tables(n):
    rows = n // GRID_W
    row = jnp.repeat(jnp.arange(rows, dtype=jnp.float32), GRID_W)
    col = jnp.tile(jnp.arange(GRID_W, dtype=jnp.float32), rows)
    axis_dim = ATTN_DH // 2
    inv_freq = ROPE_THETA ** (-jnp.arange(0, axis_dim, 2, dtype=jnp.float32) / axis_dim)
    ang_r = (row[:, None] * inv_freq)[:, None, :]
    ang_c = (col[:, None] * inv_freq)[:, None, :]
    return jnp.cos(ang_r), jnp.sin(ang_r), jnp.cos(ang_c), jnp.sin(ang_c)


def rotate(x, cos, sin):
    half = x.shape[-1] // 2
    x1, x2 = x[..., :half], x[..., half:]
    return jnp.concatenate([x1 * cos - x2 * sin, x2 * cos + x1 * sin], axis=-1)


def apply_axial_rope(x, tabs):
    cos_r, sin_r, cos_c, sin_c = tabs
    axis_dim = ATTN_DH // 2
    xr = rotate(x[..., :axis_dim].astype(jnp.float32), cos_r, sin_r)
    xc = rotate(x[..., axis_dim:].astype(jnp.float32), cos_c, sin_c)
    return jnp.concatenate([xr, xc], axis=-1).astype(x.dtype)


def gqa_mixer(aq, ak, av, qn_g, kn_g):
    bsz, n, _ = aq.shape
    grp = ATTN_HEADS // ATTN_KV_HEADS
    q = rms_norm(aq.reshape(bsz, n, ATTN_HEADS, ATTN_DH), qn_g, QK_EPS)
    k = rms_norm(ak.reshape(bsz, n, ATTN_KV_HEADS, ATTN_DH), kn_g, QK_EPS)
    v = av.reshape(bsz, n, ATTN_KV_HEADS, ATTN_DH)
    tabs = axial_rope_tables(n)
    q = apply_axial_rope(q, tabs)
    k = apply_axial_rope(k, tabs)
    q = q.reshape(bsz, n // Q_BLOCK, Q_BLOCK, ATTN_KV_HEADS, grp, ATTN_DH)
    q = jnp.moveaxis(q, 1, 0)
    scale = ATTN_DH ** -0.5

    def block(qb):
        s = jnp.einsum('bqkgd,bskd->bkgqs', qb, k, preferred_element_type=jnp.float32) * scale
        p = jax.nn.softmax(s, axis=-1)
        return jnp.einsum('bkgqs,bskd->bqkgd', p.astype(v.dtype), v)

    o = lax.map(block, q)
    return jnp.moveaxis(o, 0, 1).reshape(bsz, n, ATTN_Q_W)


def encoder_layer(x, p, ffn1_wg, ffn1_wu, ffn1_wd, ln1_g, ln1_b, w_in,
                  gla_w2f, gla_b2f, gla_w2b, gla_b2b, gla_gn_g, q_norm_g, k_norm_g,
                  w_out, ln2_g, ln2_b, ffn2_wg, ffn2_wu, ffn2_wd, ln3_g, ln3_b,
                  w_pg, b_pg, w_pe):
    a = DEEPNORM_ALPHA
    h = layer_norm(a * x + 0.5 * swiglu(x, ffn1_wg, ffn1_wu, ffn1_wd), ln1_g, ln1_b)
    proj = h @ w_in
    offs = [int(o) for o in np.cumsum(IN_SPLITS)[:-1]]
    gq, gk, gv, gr, zf, zb, aq, ak, av = jnp.split(proj, offs, axis=-1)
    o_gla = gla_mixer(gq, gk, gv, gr, zf, zb, gla_w2f, gla_b2f, gla_w2b, gla_b2b, gla_gn_g)
    o_att = gqa_mixer(aq, ak, av, q_norm_g, k_norm_g)
    mix = jnp.concatenate([o_gla, o_att], axis=-1) @ w_out
    h = layer_norm(a * h + mix, ln2_g, ln2_b)
    h = layer_norm(a * h + 0.5 * swiglu(h, ffn2_wg, ffn2_wu, ffn2_wd), ln3_g, ln3_b)
    return h + jax.nn.sigmoid(h @ w_pg + b_pg) * (p @ w_pe)


def setup_inputs(seed: int = 0) -> dict:
    key = jax.random.key(seed)
    ks = jax.random.split(key, 40)
    f32 = jnp.float32
    nrm = lambda k, shape, s: jax.random.normal(k, shape, f32) * s
    gain = lambda k, shape: 1.0 + 0.02 * jax.random.normal(k, shape, f32)
    beta = DEEPNORM_BETA
    L = DEPTH
    return {
        "x_prompt": nrm(ks[0], (BATCH, SEQ, D_MODEL), 1.0),
        "x_sample": nrm(ks[1], (DEC_BATCH, DEC_SEQ, D_MODEL), 1.0),
        "p_prompt": nrm(ks[2], (DEPTH, BATCH, SEQ, P_DIM), 1.0),
        "p_sample": nrm(ks[3], (DEPTH, DEC_BATCH, DEC_SEQ, P_DIM), 1.0),
        "ffn1_wg": nrm(ks[4], (L, D_MODEL, D_FF), D_MODEL ** -0.5),
        "ffn1_wu": nrm(ks[5], (L, D_MODEL, D_FF), D_MODEL ** -0.5),
        "ffn1_wd": nrm(ks[6], (L, D_FF, D_MODEL), beta * D_FF ** -0.5),
        "ln1_g": gain(ks[7], (L, D_MODEL)),
        "ln1_b": nrm(ks[8], (L, D_MODEL), 0.02),
        "w_in": nrm(ks[9], (L, D_MODEL, D_IN), D_MODEL ** -0.5),
        "gla_w2f": nrm(ks[10], (L, GLA_GATE_RANK, GLA_QK_W), GLA_GATE_RANK ** -0.5),
        "gla_b2f": nrm(ks[11], (L, GLA_QK_W), 0.1),
        "gla_w2b": nrm(ks[12], (L, GLA_GATE_RANK, GLA_QK_W), GLA_GATE_RANK ** -0.5),
        "gla_b2b": nrm(ks[13], (L, GLA_QK_W), 0.1),
        "gla_gn_g": gain(ks[14], (L, GLA_DV)),
        "q_norm_g": gain(ks[15], (L, ATTN_DH)),
        "k_norm_g": gain(ks[16], (L, ATTN_DH)),
        "w_out": nrm(ks[17], (L, MIX_W, D_MODEL), beta * MIX_W ** -0.5),
        "ln2_g": gain(ks[18], (L, D_MODEL)),
        "ln2_b": nrm(ks[19], (L, D_MODEL), 0.02),
        "ffn2_wg": nrm(ks[20], (L, D_MODEL, D_FF), D_MODEL ** -0.5),
        "ffn2_wu": nrm(ks[21], (L, D_MODEL, D_FF), D_MODEL ** -0.5),
        "ffn2_wd": nrm(ks[22], (L, D_FF, D_MODEL), beta * D_FF ** -0.5),
        "ln3_g": gain(ks[23], (L, D_MODEL)),
        "ln3_b": nrm(ks[24], (L, D_MODEL), 0.02),
        "w_pg": nrm(ks[25], (L, D_MODEL, D_MODEL), D_MODEL ** -0.5),
        "b_pg": nrm(ks[26], (L, D_MODEL), 0.02),
        "w_pe": nrm(ks[27], (L, P_DIM, D_MODEL), beta * P_DIM ** -0.5),
    }


def reference(x_prompt, x_sample, p_prompt, p_sample, ffn1_wg, ffn1_wu, ffn1_wd, ln1_g, ln1_b,
              w_in, gla_w2f, gla_b2f, gla_w2b, gla_b2b, gla_gn_g, q_norm_g, k_norm_g, w_out,
              ln2_g, ln2_b, ffn2_wg, ffn2_wu, ffn2_wd, ln3_g, ln3_b, w_pg, b_pg, w_pe):
    y_prompt = x_prompt
    y_sample = x_sample
    for i in range(DEPTH):
        layer_w = (ffn1_wg[i], ffn1_wu[i], ffn1_wd[i], ln1_g[i], ln1_b[i], w_in[i],
                   gla_w2f[i], gla_b2f[i], gla_w2b[i], gla_b2b[i], gla_gn_g[i],
                   q_norm_g[i], k_norm_g[i], w_out[i], ln2_g[i], ln2_b[i],
                   ffn2_wg[i], ffn2_wu[i], ffn2_wd[i], ln3_g[i], ln3_b[i],
                   w_pg[i], b_pg[i], w_pe[i])
        y_prompt = encoder_layer(y_prompt, p_prompt[i], *layer_w)
        y_sample = encoder_layer(y_sample, p_sample[i], *layer_w)
    return (y_prompt, y_sample)
```

```python
import contextlib
import math
import numpy as np
import ml_dtypes
import concourse.bass as bass
import concourse.mybir as mybir
from concourse.bass_utils import run_bass_kernel_spmd

F32 = mybir.dt.float32
BF16 = mybir.dt.bfloat16
AF = mybir.ActivationFunctionType
ALU = mybir.AluOpType
AX = mybir.AxisListType

D = 1024
DFF = 2816
NF = DFF // 128
PD = 256
DIN = 2336
ALPHA = 2.0 ** 0.25
LN_EPS = 1e-5
ENGS = ("pe", "act", "dve", "pool", "sp")


class _Op:
    __slots__ = ("eng", "fn", "deps", "is_dma", "key", "signal", "seq", "waits", "idx")

    def __init__(self, eng, fn, is_dma=False, key=None):
        self.eng = eng
        self.fn = fn
        self.deps = set()
        self.is_dma = is_dma
        self.key = key
        self.signal = False
        self.seq = None
        self.waits = None


class Prog:
    NSLOT = 40
    NPOOL = 10

    def __init__(self, nc, stack):
        self.nc = nc
        self.sem_eng = {e: stack.enter_context(nc.semaphore("s_" + e)) for e in ENGS}
        self.sem_slot = [stack.enter_context(nc.semaphore("d_%d" % i)) for i in range(self.NSLOT)]
        self.eng_cnt = {e: 0 for e in ENGS}
        self.slot_cnt = [0] * self.NSLOT
        self.known = {e: {} for e in ENGS}
        self.reset()

    def reset(self):
        self.ops = []
        self.last_w = {}
        self.readers = {}

    def _add(self, op, reads, writes):
        idx = len(self.ops)
        op.idx = idx
        for r in reads:
            w = self.last_w.get(r)
            if w is not None:
                op.deps.add(w)
        for t in writes:
            w = self.last_w.get(t)
            if w is not None:
                op.deps.add(w)
            for rd in self.readers.get(t, ()):
                op.deps.add(rd)
        op.deps.discard(idx)
        for r in reads:
            self.readers.setdefault(r, []).append(idx)
        for t in writes:
            self.last_w[t] = idx
            self.readers[t] = []
        self.ops.append(op)
        return idx

    def op(self, eng, fn, reads=(), writes=()):
        return self._add(_Op(eng, fn), reads, writes)

    def dma(self, queue, out, in_, reads=(), writes=(), key=None, **kw):
        assert key is not None
        o = _Op(queue, lambda e: e.dma_start(out=out, in_=in_, **kw), is_dma=True, key=key)
        o.signal = True
        return self._add(o, reads, writes)

    def lower(self):
        nc = self.nc
        ops = self.ops
        for o in ops:
            for d in o.deps:
                p = ops[d]
                if p.eng == "pe" and o.eng == "pe" and not p.is_dma and not o.is_dma:
                    continue
                p.signal = True
        streams = {e: [] for e in ENGS}
        for o in ops:
            streams[o.eng].append(o)
        for e in ENGS:
            lst = [o for o in streams[e] if not o.is_dma]
            if lst:
                lst[-1].signal = True
        slot_of = {}
        npool = [0]
        nsp = [0]
        for o in ops:
            if not o.signal:
                continue
            if o.is_dma:
                if o.key not in slot_of:
                    if o.eng == "pool":
                        slot_of[o.key] = npool[0]
                        npool[0] += 1
                        assert npool[0] <= self.NPOOL, "too many pool dma keys"
                    else:
                        slot_of[o.key] = self.NPOOL + nsp[0]
                        nsp[0] += 1
                        assert self.NPOOL + nsp[0] <= self.NSLOT, "too many sp dma keys"
                s = slot_of[o.key]
                self.slot_cnt[s] += 16
                o.seq = self.slot_cnt[s]
            else:
                self.eng_cnt[o.eng] += 1
                o.seq = self.eng_cnt[o.eng]

        def sem_of(p):
            return self.sem_slot[slot_of[p.key]] if p.is_dma else self.sem_eng[p.eng]

        for e in ENGS:
            known = self.known[e]
            for o in streams[e]:
                need = {}
                for d in o.deps:
                    p = ops[d]
                    if p.eng == "pe" and o.eng == "pe" and not p.is_dma and not o.is_dma:
                        continue
                    s = ("k", slot_of[p.key]) if p.is_dma else ("e", p.eng)
                    if p.seq > need.get(s, 0):
                        need[s] = p.seq
                w = []
                for s, v in need.items():
                    if known.get(s, 0) >= v:
                        continue
                    known[s] = v
                    w.append((self.sem_slot[s[1]] if s[0] == "k" else self.sem_eng[s[1]], v))
                o.waits = w
        fin = [(self.sem_slot[i], self.slot_cnt[i]) for i in range(self.NSLOT) if self.slot_cnt[i] > 0]
        fin += [(self.sem_eng[e], self.eng_cnt[e]) for e in ENGS if self.eng_cnt[e] > 0]

        with nc.Block(no_gpsimd_drain=True) as block:
            def run(e, engobj):
                for o in streams[e]:
                    for (s, v) in o.waits:
                        engobj.wait_ge(s, v)
                    ins = o.fn(engobj)
                    if o.signal:
                        ins.then_inc(sem_of(o), 16 if o.is_dma else 1)
                for (s, v) in fin:
                    engobj.wait_ge(s, v)

            @block.sync
            def _(eng):
                run("sp", eng)

            @block.tensor
            def _(eng):
                run("pe", eng)

            @block.scalar
            def _(eng):
                run("act", eng)

            @block.vector
            def _(eng):
                run("dve", eng)

            @block.gpsimd
            def _(eng):
                run("pool", eng)
        self.reset()


_CNT = [0]


class _Rec:
    def __init__(self):
        self.calls = []

    def op(self, *a, **k):
        self.calls.append(("op", a, k))

    def dma(self, *a, **k):
        self.calls.append(("dma", a, k))


def _replay(P, calls):
    for kind, a, k in calls:
        getattr(P, kind)(*a, **k)


class Phase:
    def __init__(self, nc, P):
        self.nc = nc
        self.P = P
        self.es = contextlib.ExitStack()
        self.n = 0

    def sb(self, shape, dt, name=None):
        _CNT[0] += 1
        return self.es.enter_context(self.nc.sbuf_tensor(name or ("t%d" % _CNT[0]), list(shape), dt))

    def ps(self, shape, dt, name=None):
        _CNT[0] += 1
        return self.es.enter_context(self.nc.psum_tensor(name or ("p%d" % _CNT[0]), list(shape), dt))

    def close(self):
        self.P.lower()
        self.es.close()


def bcast_rows(ap_row, n):
    return ap_row.unsqueeze(0).broadcast_to([n, ap_row.shape[0]])


def ffn_phase(nc, P, tag, NT, src32, srcT, wg_d, wu_d, wd_d, g_d, b_d, ident_d, dst32, dstT, cres, eps):
    TT = 256
    NS = TT // 128
    ph = Phase(nc, P)
    wg = ph.sb([128, 8, DFF], BF16)
    wu = ph.sb([128, 8, DFF], BF16)
    wd = ph.sb([128, NF, D], BF16)
    lng = ph.sb([128, D], F32)
    lnb = ph.sb([128, D], F32)
    idt = ph.sb([128, 128], BF16)
    xb = [ph.sb([128, NS, D], BF16) for _ in range(2)] if srcT is None else None
    xT = [ph.sb([128, 8, TT], BF16) for _ in range(2)]
    hT = ph.sb([128, NF, TT], BF16)
    sg = [ph.sb([128, TT], F32) for _ in range(2)]
    x32 = [ph.sb([128, D], F32) for _ in range(2)]
    z = [ph.sb([128, D], F32) for _ in range(2)]
    hb = [ph.sb([128, D], BF16) for _ in range(2)]
    hTt = [ph.sb([128, 8, TT], BF16) for _ in range(2)]
    stats = ph.sb([128, 2, 6], F32)
    mv = ph.sb([128, 8], F32)
    pgu = [ph.ps([128, 512], F32) for _ in range(2)]
    py = [ph.ps([128, 1024], F32) for _ in range(2)]
    pT = [ph.ps([128, 8, 128], BF16) for _ in range(2)]

    HALF = DFF // 2
    ntile = NT // TT
    P.dma("sp", idt[:], ident_d, writes=["idt"], key="idt")
    WGUH = [["wg%d_%d" % (k, h) for k in range(8)] + ["wu%d_%d" % (k, h) for k in range(8)] for h in range(2)]
    WD = ["wd%d" % f for f in range(NF)]

    def load_weights():
        for h in range(2):
            for k in range(8):
                P.dma("pool", wg[:, k, h * HALF:(h + 1) * HALF], wg_d[k * 128:(k + 1) * 128, h * HALF:(h + 1) * HALF],
                      writes=["wg%d_%d" % (k, h)], key="wg%d" % h)
                P.dma("pool", wu[:, k, h * HALF:(h + 1) * HALF], wu_d[k * 128:(k + 1) * 128, h * HALF:(h + 1) * HALF],
                      writes=["wu%d_%d" % (k, h)], key="wu%d" % h)
        for f in range(NF):
            P.dma("pool", wd[:, f, :], wd_d[f * 128:(f + 1) * 128, :], writes=["wd%d" % f], key="wd")
        P.dma("sp", lng[:], bcast_rows(g_d, 128), writes=["lng"], key="lng")
        P.dma("sp", lnb[:], bcast_rows(b_d, 128), writes=["lnb"], key="lnb")

    def load_x(ti):
        t0 = ti * TT
        tb = ti % 2
        if srcT is None:
            for s in range(NS):
                P.dma("pool", xb[tb][:, s, :], src32[t0 + s * 128:t0 + (s + 1) * 128, :], writes=["xb%d_%d" % (tb, s)], key="xb%d_%d" % (tb, s))
        else:
            P.dma("sp", xT[tb][:], srcT[:, :, t0:t0 + TT], writes=["xT%d" % tb], key="xT%d" % tb)

    def xtrans(ti):
        if srcT is not None:
            return
        tb = ti % 2
        for s in range(NS):
            pTc = pT[s % 2]
            for k in range(8):
                P.op("pe", lambda e, s=s, k=k, pTc=pTc: e.transpose(pTc[:, k, :], xb[tb][:, s, k * 128:(k + 1) * 128], idt[:]),
                     reads=["xb%d_%d" % (tb, s), "idt"], writes=["pT%d_%d" % (s % 2, k)])
            P.op("act", lambda e, s=s, pTc=pTc: e.copy(xT[tb][:, :, s * 128:(s + 1) * 128], pTc[:]),
                 reads=["pT%d_%d" % (s % 2, k) for k in range(8)], writes=["xT%d_%d" % (tb, s)])

    def xT_reads(ti):
        tb = ti % 2
        return ["xT%d_%d" % (tb, s) for s in range(NS)] if srcT is None else ["xT%d" % tb]

    def up(ti, f):
        b = f % 2
        xTc = xT[ti % 2]

        def upm(e):
            for k in range(8):
                e.matmul(pgu[b][:, 0:TT], wg[:, k, f * 128:(f + 1) * 128], xTc[:, k, :], start=(k == 0), stop=(k == 7))
            for k in range(8):
                ins = e.matmul(pgu[b][:, TT:2 * TT], wu[:, k, f * 128:(f + 1) * 128], xTc[:, k, :], start=(k == 0), stop=(k == 7))
            return ins
        P.op("pe", upm, reads=xT_reads(ti) + WGUH[0 if f < NF // 2 else 1], writes=["pgu%d" % b])
        P.op("act", lambda e: e.activation(sg[b][:], pgu[b][:, 0:TT], AF.Silu), reads=["pgu%d" % b], writes=["sg%d" % b])
        P.op("dve", lambda e: e.tensor_tensor(hT[:, f, :], sg[b][:], pgu[b][:, TT:2 * TT], ALU.mult),
             reads=["sg%d" % b, "pgu%d" % b], writes=["hT%d" % f])

    def down(ti, s):
        gi = ti * NS + s
        r0 = ti * TT + s * 128
        q = gi % 2

        def dn(e):
            for h in range(2):
                for f in range(NF):
                    ins = e.matmul(py[q][:, h * 512:(h + 1) * 512], hT[:, f, s * 128:(s + 1) * 128], wd[:, f, h * 512:(h + 1) * 512],
                                   start=(f == 0), stop=(f == NF - 1))
            return ins
        P.op("pe", dn, reads=["hT%d" % f for f in range(NF)] + WD, writes=["py%d" % q])
        P.dma("sp", x32[q][:], src32[r0:r0 + 128, :], writes=["x32_%d" % q], key="x32_%d" % q)
        P.op("dve", lambda e: e.scalar_tensor_tensor(z[q][:], py[q][:], float(cres), x32[q][:], ALU.mult, ALU.add),
             reads=["py%d" % q, "x32_%d" % q], writes=["z%d" % q])
        layer_norm(P, z[q], "z%d" % q, stats, mv, lng, lnb, eps, x32[q], "x32_%d" % q,
                   out_bf=(hb[q] if dstT is not None else None), out_bfn="hb%d" % q)
        P.dma("sp", dst32[r0:r0 + 128, :], x32[q][:], reads=["x32_%d" % q], writes=["dst32_%d" % gi], key="st32_%d" % q)

    def htrans(ti, s):
        if dstT is None:
            return
        gi = ti * NS + s
        q = gi % 2
        tb = ti % 2
        pTc = pT[q]
        for k in range(8):
            P.op("pe", lambda e, k=k: e.transpose(pTc[:, k, :], hb[q][:, k * 128:(k + 1) * 128], idt[:]),
                 reads=["hb%d" % q, "idt"], writes=["pT%d_%d" % (q, k)])
        P.op("act", lambda e: e.copy(hTt[tb][:, :, s * 128:(s + 1) * 128], pTc[:]),
             reads=["pT%d_%d" % (q, k) for k in range(8)], writes=["hTt%d_%d" % (tb, s)])
        if s == NS - 1:
            P.dma("sp", dstT[:, :, ti * TT:(ti + 1) * TT], hTt[tb][:], reads=["hTt%d_%d" % (tb, ss) for ss in range(NS)],
                  writes=["dstT_%d" % ti], key="stT_%d" % tb)

    load_x(0)
    load_weights()
    xtrans(0)
    for ti in range(ntile):
        if ti + 1 < ntile:
            load_x(ti + 1)
        for f in range(NF):
            up(ti, f)
            if f == 10 and ti > 0:
                for s in range(NS):
                    htrans(ti - 1, s)
        if ti + 1 < ntile:
            xtrans(ti + 1)
        for s in range(NS):
            down(ti, s)
    for s in range(NS):
        htrans(ntile - 1, s)
    ph.close()


def layer_norm(P, zt, zn, stats, mv, lng, lnb, eps, out, outn, out_bf=None, out_bfn=None):
    for c in range(2):
        P.op("dve", lambda e, c=c: e.bn_stats(stats[:, c, :], zt[:, c * 512:(c + 1) * 512]), reads=[zn], writes=["stats%d" % c])
    P.op("dve", lambda e: e.bn_aggr(mv[:, 0:2], stats[:]), reads=["stats0", "stats1"], writes=["mv01"])
    P.op("dve", lambda e: e.tensor_scalar(mv[:, 2:3], mv[:, 1:2], float(eps), None, ALU.add), reads=["mv01"], writes=["mv2"])
    P.op("act", lambda e: e.activation(mv[:, 3:4], mv[:, 2:3], AF.Sqrt), reads=["mv2"], writes=["mv3"])
    P.op("dve", lambda e: e.reciprocal(mv[:, 4:5], mv[:, 3:4]), reads=["mv3"], writes=["mv4"])
    P.op("dve", lambda e: e.scalar_tensor_tensor(zt[:], zt[:], mv[:, 0:1], lng[:], ALU.subtract, ALU.mult),
         reads=[zn, "mv01", "lng"], writes=[zn])
    P.op("dve", lambda e: e.scalar_tensor_tensor(out[:], zt[:], mv[:, 4:5], lnb[:], ALU.mult, ALU.add),
         reads=[zn, "mv4", "lnb"], writes=[outn])
    if out_bf is not None:
        P.op("dve", lambda e: e.scalar_tensor_tensor(out_bf[:], zt[:], mv[:, 4:5], lnb[:], ALU.mult, ALU.add),
             reads=[zn, "mv4", "lnb"], writes=[out_bfn])


def attn_phase(nc, P, T, seq, h1T, W, CN, mixA, nqb=None):
    NCH = T // 128
    outer = Phase(nc, P)
    QT = outer.sb([128, 4, T], BF16)
    KT = outer.sb([128, 2, 2, T], BF16)
    VA = outer.sb([128, NCH, 2, 65], BF16)
    VAo = outer.sb([128, NCH, 2, 128], BF16)
    negm = outer.sb([128, 1], F32)
    ones32 = outer.sb([128, 128], BF16)
    ph = Phase(nc, P)
    wq = ph.sb([128, 8, 768], BF16)
    gq = ph.sb([128, 10, 64], F32)
    gm = ph.sb([128, 4], F32)
    rC = ph.sb([128, NCH, 64], F32)
    rS = ph.sb([128, NCH, 64], F32)
    idt = ph.sb([128, 128], BF16)
    hTt = [ph.sb([128, 8, 512], BF16) for _ in range(2)]
    sq2 = [ph.sb([128, 640], F32) for _ in range(2)]
    ss2 = [ph.sb([128, 40], F32) for _ in range(2)]
    qn2 = [ph.sb([128, 10, 64], F32) for _ in range(2)]
    t12 = [ph.sb([128, 10, 64], F32) for _ in range(2)]
    t22 = [ph.sb([128, 10, 64], F32) for _ in range(2)]
    qr = [ph.sb([128, 10, 64], BF16) for _ in range(2)]
    krs = [ph.sb([128, 2, 64], BF16) for _ in range(2)]
    pq = [ph.ps([128, 1024], F32) for _ in range(2)]
    pTq = [ph.ps([128, 4, 128], BF16) for _ in range(2)]
    pTk = [ph.ps([128, 2, 128], BF16) for _ in range(2)]
    for k in range(8):
        P.dma("pool", wq[:, k, :], W["w_in"][k * 128:(k + 1) * 128, 1568:2336], writes=["wq%d" % k], key="wq")
    WQ = ["wq%d" % k for k in range(8)]
    for h in range(10):
        P.dma("sp", gq[:, h, :], bcast_rows(W["q_norm_g"] if h < 8 else W["k_norm_g"], 128), writes=["gq%d" % h], key="gq")
    GQ = ["gq%d" % h for h in range(10)]
    P.dma("sp", rC[:], CN["ropeC"].rearrange("(n p) f -> p n f", p=128), writes=["rC"], key="rC")
    P.dma("sp", rS[:], CN["ropeS"].rearrange("(n p) f -> p n f", p=128), writes=["rS"], key="rS")
    P.dma("sp", idt[:], CN["ident"], writes=["idt"], key="idt")
    P.op("pool", lambda e: e.memset(VA[:, :, :, 64:65], 1.0), writes=["VAones"])

    P.op("pool", lambda e: e.memset(VAo[:, :, :, 0:64], 0.0), writes=["VAoones"])
    P.op("pool", lambda e: e.memset(VAo[:, :, :, 0:1], 1.0), writes=["VAoones"])
    P.op("pool", lambda e: e.memset(ones32[:], 1.0), writes=["ones32"])
    P.op("pool", lambda e: e.memset(KT[:], 0.0), writes=["KTz"])
    P.op("dve", lambda e: e.tensor_reduce(gm[:, 0:1], gq[:, 0, :], AX.X, ALU.max, apply_absolute_value=True), reads=GQ, writes=["gm0"])
    P.op("dve", lambda e: e.tensor_reduce(gm[:, 1:2], gq[:, 8, :], AX.X, ALU.max, apply_absolute_value=True), reads=GQ, writes=["gm1"])
    P.op("dve", lambda e: e.tensor_tensor(gm[:, 2:3], gm[:, 0:1], gm[:, 1:2], ALU.mult), reads=["gm0", "gm1"], writes=["gm2"])
    P.op("dve", lambda e: e.tensor_scalar(negm[:], gm[:, 2:3], -8.0, None, ALU.mult), reads=["gm2"], writes=["negm"])
    loads, recs = [], []
    for ti in range(T // 512):
        hc = hTt[ti % 2]
        hn = "hTt%d" % (ti % 2)
        PP = _Rec()
        PP.dma("sp", hc[:], h1T[:, :, seq * T + ti * 512: seq * T + (ti + 1) * 512], writes=[hn], key=hn)
        loads.append(PP.calls)
        for s in range(4):
            PP = _Rec()
            recs.append(PP.calls)
            ci = ti * 4 + s
            b = ci % 2
            sq, ss, qn, t1, t2 = sq2[b], ss2[b], qn2[b], t12[b], t22[b]
            tok = slice(ci * 128, (ci + 1) * 128)

            def proj(e, s=s, b=b, hc=hc):
                for k in range(8):
                    e.matmul(pq[b][:, 0:512], hc[:, k, s * 128:(s + 1) * 128], wq[:, k, 0:512], start=(k == 0), stop=(k == 7))
                for k in range(8):
                    ins = e.matmul(pq[b][:, 512:768], hc[:, k, s * 128:(s + 1) * 128], wq[:, k, 512:768], start=(k == 0), stop=(k == 7))
                return ins
            PP.op("pe", proj, reads=[hn] + WQ, writes=["pq%d" % b])
            PP.op("act", lambda e, b=b, sq=sq, ss=ss, qn=qn, t1=t1, t2=t2: e.activation(sq[:], pq[b][:, 0:640], AF.Square), reads=["pq%d" % b], writes=["sq_" + str(b)])
            PP.op("dve", lambda e, sq=sq, ss=ss, qn=qn, t1=t1, t2=t2: e.tensor_reduce(ss[:, 0:10], sq[:].rearrange("p (h d) -> p h d", d=64), AX.X, ALU.add), reads=["sq_" + str(b)], writes=["ss0_" + str(b)])
            PP.op("dve", lambda e, sq=sq, ss=ss, qn=qn, t1=t1, t2=t2: e.tensor_scalar(ss[:, 10:20], ss[:, 0:10], 1.0 / 64, 1e-6, ALU.mult, ALU.add), reads=["ss0_" + str(b)], writes=["ss1_" + str(b)])
            PP.op("act", lambda e, sq=sq, ss=ss, qn=qn, t1=t1, t2=t2: e.activation(ss[:, 20:30], ss[:, 10:20], AF.Sqrt), reads=["ss1_" + str(b)], writes=["ss2_" + str(b)])
            PP.op("dve", lambda e, sq=sq, ss=ss, qn=qn, t1=t1, t2=t2: e.reciprocal(ss[:, 30:40], ss[:, 20:30]), reads=["ss2_" + str(b)], writes=["ss3_" + str(b)])
            PP.op("dve", lambda e, b=b, sq=sq, ss=ss, qn=qn, t1=t1, t2=t2: e.tensor_tensor(qn[:], pq[b][:, 0:640].rearrange("p (h d) -> p h d", d=64),
                                                       ss[:, 30:40].unsqueeze(2).broadcast_to([128, 10, 64]), ALU.mult),
                 reads=["pq%d" % b, "ss3_" + str(b)], writes=["qn_" + str(b)])
            PP.op("act", lambda e, b=b, ci=ci, sq=sq, ss=ss, qn=qn, t1=t1, t2=t2: e.copy(VA[:, ci, :, 0:64], pq[b][:, 640:768].rearrange("p (h d) -> p h d", d=64)),
                 reads=["pq%d" % b], writes=["VA%d" % ci])
            PP.op("act", lambda e, b=b, ci=ci: e.copy(VAo[:, ci, :, 64:128], pq[b][:, 640:768].rearrange("p (h d) -> p h d", d=64)),
                 reads=["pq%d" % b, "VAoones"], writes=["VAo%d" % ci])
            PP.op("pool", lambda e, sq=sq, ss=ss, qn=qn, t1=t1, t2=t2: e.tensor_tensor(qn[:], qn[:], gq[:], ALU.mult), reads=["qn_" + str(b)] + GQ, writes=["qn_" + str(b)])
            PP.op("pool", lambda e, ci=ci, sq=sq, ss=ss, qn=qn, t1=t1, t2=t2: e.tensor_tensor(t1[:], qn[:], rC[:, ci, :].unsqueeze(1).broadcast_to([128, 10, 64]), ALU.mult),
                 reads=["qn_" + str(b), "rC"], writes=["t1_" + str(b)])
            qv = qn[:].rearrange("p h (a c f) -> p h a c f", a=2, c=2)
            tv = t2[:].rearrange("p h (a c f) -> p h a c f", a=2, c=2)

            def rot(e, ci=ci, qv=qv, tv=tv):
                sv = rS[:, ci, :].rearrange("p (a c f) -> p a c f", a=2, c=2)
                e.tensor_tensor(tv[:, :, :, 0, :], qv[:, :, :, 1, :], sv[:, :, 0, :].unsqueeze(1).broadcast_to([128, 10, 2, 16]), ALU.mult)
                return e.tensor_tensor(tv[:, :, :, 1, :], qv[:, :, :, 0, :], sv[:, :, 1, :].unsqueeze(1).broadcast_to([128, 10, 2, 16]), ALU.mult)
            PP.op("dve", rot, reads=["qn_" + str(b), "rS"], writes=["t2_" + str(b)])
            PP.op("dve", lambda e, b=b, sq=sq, ss=ss, qn=qn, t1=t1, t2=t2: e.tensor_tensor(qr[b][:], t1[:], t2[:], ALU.add), reads=["t1_" + str(b), "t2_" + str(b)], writes=["qr%d" % b])

            def kswap(e, b=b):
                e.tensor_copy(krs[b][:, 0, :], qr[b][:, 9, :])
                return e.tensor_copy(krs[b][:, 1, :], qr[b][:, 8, :])
            PP.op("pool", kswap, reads=["qr%d" % b], writes=["krs%d" % b])

            def tq(e, b=b):
                for pr in range(4):
                    ins = e.transpose(pTq[b][:, pr, :], qr[b][:, 2 * pr:2 * pr + 2, :].rearrange("p h d -> p (h d)"), idt[:])
                return ins
            PP.op("pe", tq, reads=["qr%d" % b, "idt"], writes=["pTq%d" % b])

            def tk(e, b=b):
                e.transpose(pTk[b][:, 0, :], qr[b][:, 8:10, :].rearrange("p h d -> p (h d)"), idt[:])
                return e.transpose(pTk[b][:, 1, :], krs[b][:].rearrange("p h d -> p (h d)"), idt[:])
            PP.op("pe", tk, reads=["qr%d" % b, "krs%d" % b, "idt"], writes=["pTk%d" % b])
            PP.op("act", lambda e, b=b, tok=tok, sq=sq, ss=ss, qn=qn, t1=t1, t2=t2: e.copy(QT[:, :, tok], pTq[b][:]), reads=["pTq%d" % b], writes=["QT%d" % ci])

            def kcp(e, b=b, tok=tok):
                e.tensor_copy(KT[0:64, :, 0, tok], pTk[b][0:64, :, :])
                e.tensor_copy(KT[64:128, 0, 1, tok], pTk[b][64:128, 1, :])
                return e.tensor_copy(KT[64:128, 1, 1, tok], pTk[b][64:128, 0, :])
            PP.op("dve", kcp, reads=["pTk%d" % b, "KTz"], writes=["KT%d" % ci])
    def _split(calls):
        k = next(i for i, c in enumerate(calls) if any(str(w).startswith("krs") for w in c[2].get("writes", ())))
        return calls[:1], calls[1:k], calls[k:]
    parts = [_split(c) for c in recs]
    NR = len(parts)
    _replay(P, loads[0])
    if len(loads) > 1:
        _replay(P, loads[1])
    _replay(P, parts[0][0])
    if NR > 1:
        _replay(P, parts[1][0])
    _replay(P, parts[0][1])
    for ci in range(NR):
        nt = (ci + 2) // 4 + 1
        if (ci + 2) % 4 == 0 and nt < len(loads):
            _replay(P, loads[nt])
        if ci + 2 < NR:
            _replay(P, parts[ci + 2][0])
        if ci + 1 < NR:
            _replay(P, parts[ci + 1][1])
        _replay(P, parts[ci][2])
    ph.close()
    ph = Phase(nc, P)
    PT = [ph.sb([128, 1024], BF16) for _ in range(2)]
    rd = [ph.sb([128, 512], F32) for _ in range(2)]
    rres = [ph.sb([128, 512], F32) for _ in range(2)]
    rb16 = [ph.sb([128, 2, 512], BF16) for _ in range(2)]
    o32 = [ph.sb([128, 512], F32) for _ in range(2)]
    ob = [ph.sb([128, 4, 512], BF16) for _ in range(2)]
    pS = [ph.ps([128, 1024], F32) for _ in range(2)]
    po = [ph.ps([128, 512], F32) for _ in range(2)]
    pb = ph.ps([128, 512], F32)
    nsg = NCH // 2
    groups = [(qb, h, sgi) for qb in range(nqb if nqb is not None else T // 512) for h in range(8) for sgi in range(nsg)]

    def emit_qk_exp(i):
        qb, h, sgi = groups[i]
        b = i % 2
        pr, base, kv = h // 2, (h % 2) * 64, h // 4
        qsl = slice(qb * 512, (qb + 1) * 512)

        def qk(e):
            for j in range(2):
                c = 2 * sgi + j
                ins = e.matmul(pS[b][:, j * 512:(j + 1) * 512], KT[:, kv, h % 2, c * 128:(c + 1) * 128],
                               QT[:, pr, qsl], start=True, stop=True)
            return ins
        P.op("pe", qk, reads=[], writes=["pS%d" % b])
        P.op("act", lambda e: e.activation(PT[b][:], pS[b][:], AF.Exp, bias=negm[:, 0:1], scale=0.125),
             reads=["pS%d" % b], writes=["PT%d" % b])

    pending = []

    def emit_pv(i):
        qb, h, sgi = groups[i]
        b = i % 2
        kv = h // 4
        hb_ = (qb * 8 + h) % 2
        poc = po[hb_]
        pon = "po%d" % hb_

        odd = h % 2
        pr = h // 2
        MO = 128 if odd else 65
        dp = 0 if odd else 64
        lo = 64 if odd else 0

        def pv(e):
            for j in range(2):
                c = 2 * sgi + j
                ins = e.matmul(poc[0:MO, :], (VAo if odd else VA)[:, c, kv, :], PT[b][:, j * 512:(j + 1) * 512],
                               start=(sgi == 0 and j == 0), stop=(sgi == nsg - 1 and j == 1))
            return ins
        P.op("pe", pv, reads=["PT%d" % b], writes=[pon] if sgi in (0, nsg - 1) else [])
        if sgi == nsg - 1:
            obc = ob[qb % 2]
            obn = "ob%d" % (qb % 2)
            rdc, oc = rd[hb_], o32[hb_]
            rrc, rbc = rres[hb_], rb16[hb_]

            def dfin(e):
                e.tensor_copy(oc[lo:lo + 64, :], poc[lo:lo + 64, :])
                return e.tensor_copy(rbc[dp:dp + 1, 0, :], poc[dp:dp + 1, :])
            P.op("dve", dfin, reads=[pon], writes=["rd%d" % hb_, "o32_%d" % hb_])
            P.op("dve", lambda e: e.tensor_tensor(rrc[dp:dp + 1, :], poc[dp:dp + 1, :], rbc[dp:dp + 1, 0, :], ALU.subtract),
                 reads=["rd%d" % hb_, pon], writes=["rres%d" % hb_])
            P.op("dve", lambda e: e.tensor_copy(rbc[dp:dp + 1, 1, :], rrc[dp:dp + 1, :]), reads=["rres%d" % hb_], writes=["rb%d" % hb_])

            def fin():
                MB = lo + 64

                def bc(e):
                    e.matmul(pb[0:MB, :], ones32[dp:dp + 1, 0:MB], rbc[dp:dp + 1, 0, :], start=True, stop=False)
                    return e.matmul(pb[0:MB, :], ones32[dp:dp + 1, 0:MB], rbc[dp:dp + 1, 1, :], start=False, stop=True)
                P.op("pe", bc, reads=["rd%d" % hb_, "rb%d" % hb_], writes=["pb"])
                P.op("dve", lambda e: e.reciprocal(rrc[lo:lo + 64, :], pb[lo:lo + 64, :]), reads=["pb", "rb%d" % hb_], writes=["rec%d" % hb_])
                P.op("dve", lambda e: e.tensor_tensor(obc[lo:lo + 64, pr, :], oc[lo:lo + 64, :], rrc[lo:lo + 64, :], ALU.mult),
                     reads=["o32_%d" % hb_, "rec%d" % hb_], writes=[obn + "_%d" % h])
                if h == 7:
                    P.dma("sp", mixA[:, :, seq * T + qb * 512: seq * T + (qb + 1) * 512], obc[:],
                          reads=[obn + "_%d" % hh for hh in range(8)], writes=["mixA%d" % qb], key=obn)
            pending.append((i + min(6, nsg), fin))

    n = len(groups)
    for i in range(n + 1):
        if i < n:
            emit_qk_exp(i)
        while pending and pending[0][0] <= i:
            pending.pop(0)[1]()
        if i >= 1:
            emit_pv(i - 1)
    while pending:
        pending.pop(0)[1]()
    ph.close()
    outer.es.close()


def gla_phase(nc, P, T, seq, h1T, W, CN, SC, half=False):
    NCH = T // 128
    base = seq * T
    gp_d, la_d, of_d, mixG = SC["gp"], SC["la"], SC["of"], SC["mixG"]
    ph = Phase(nc, P)
    wq = ph.sb([128, 8, 1568], BF16)
    w2 = ph.sb([16, 2, 256], BF16)
    b2 = ph.sb([128, 512], F32)
    one1 = ph.sb([128, 1], F32)
    idt = ph.sb([128, 128], BF16)
    hTt = [ph.sb([128, 8, 512], BF16) for _ in range(2)]
    gpb = [ph.sb([128, 1536], BF16) for _ in range(2)]
    zb16 = ph.sb([128, 32], BF16)
    thb = ph.sb([128, 512], F32)
    zT = ph.sb([16, 2, 128], BF16)
    xs = ph.sb([128, 512], F32)
    e1 = ph.sb([128, 512], F32)
    la = [ph.sb([128, 512], F32) for _ in range(2)]
    pq = ph.ps([128, 2048], F32)
    pz = ph.ps([128, 2, 128], BF16)
    px = ph.ps([128, 512], F32)
    for k in range(8):
        P.dma("pool", wq[:, k, :], W["w_in"][k * 128:(k + 1) * 128, 0:1568], writes=["wq%d" % k], key="wq")
    WQ = ["wq%d" % k for k in range(8)]
    P.dma("pool", w2[:, 0, :], W["gla_w2f"], writes=["w2f"], key="w2")
    P.dma("pool", w2[:, 1, :], W["gla_w2b"], writes=["w2b"], key="w2")
    P.dma("sp", b2[:, 0:256], bcast_rows(W["gla_b2f"], 128), writes=["b2f"], key="b2")
    P.dma("sp", b2[:, 256:512], bcast_rows(W["gla_b2b"], 128), writes=["b2b"], key="b2")
    P.dma("sp", idt[:], CN["ident"], writes=["idt"], key="idt")
    P.op("pool", lambda e: e.memset(one1[:], 1.0), writes=["one1"])
    def g0_load(ti):
        P.dma("sp", hTt[ti % 2][:], h1T[:, :, base + ti * 512: base + (ti + 1) * 512], writes=["hTt%d" % (ti % 2)], key="hTt%d" % (ti % 2))
    g0_load(0)
    for ti in range(T // 512):
        hc = hTt[ti % 2]
        hn = "hTt%d" % (ti % 2)
        if ti + 1 < T // 512:
            g0_load(ti + 1)
        for s in range(4):
            ci = ti * 4 + s
            b = ci % 2
            r0 = base + ci * 128

            def proj(e, s=s, hc=hc):
                for (c0, c1) in ((0, 512), (512, 1024), (1024, 1536), (1536, 1568)):
                    for k in range(8):
                        ins = e.matmul(pq[:, c0:c1], hc[:, k, s * 128:(s + 1) * 128], wq[:, k, c0:c1], start=(k == 0), stop=(k == 7))
                return ins
            P.op("pe", proj, reads=[hn] + WQ, writes=["pq"])
            P.op("dve", lambda e: e.tensor_copy(zb16[:], pq[:, 1536:1568]), reads=["pq"], writes=["zb16"])
            P.op("act", lambda e, b=b: e.copy(gpb[b][:], pq[:, 0:1536]), reads=["pq"], writes=["gpbA%d" % b])

            def tz(e):
                e.transpose(pz[0:16, 0, :], zb16[:, 0:16], idt[:])
                return e.transpose(pz[0:16, 1, :], zb16[:, 16:32], idt[:])
            P.op("pe", tz, reads=["zb16", "idt"], writes=["pz"])
            P.op("dve", lambda e: e.tensor_copy(zT[:], pz[0:16, :, :]), reads=["pz"], writes=["zT"])

            def xm(e):
                e.matmul(px[:, 0:256], zT[:, 0, :], w2[:, 0, :], start=True, stop=True)
                return e.matmul(px[:, 256:512], zT[:, 1, :], w2[:, 1, :], start=True, stop=True)
            P.op("pe", xm, reads=["zT", "w2f", "w2b"], writes=["px"])
            P.op("dve", lambda e: e.tensor_tensor(xs[:], px[:], b2[:], ALU.add), reads=["px", "b2f", "b2b"], writes=["xs"])
            P.op("act", lambda e: e.activation(e1[:], xs[:], AF.Exp, scale=-1.0), reads=["xs"], writes=["e1"])
            P.op("act", lambda e, b=b: e.activation(thb[:], gpb[b][:, 1024:1536], AF.Tanh, scale=0.5), reads=["gpbA%d" % b], writes=["thb"])
            P.op("act", lambda e, b=b: e.activation(la[b][:], e1[:], AF.Ln, bias=one1[:, 0:1]), reads=["e1", "one1"], writes=["la%d" % b])
            P.op("dve", lambda e: e.tensor_scalar(thb[:], thb[:], 0.5, 0.5, ALU.mult, ALU.add), reads=["thb"], writes=["thb"])
            P.op("dve", lambda e, b=b: e.tensor_tensor(gpb[b][:, 1024:1536], thb[:], gpb[b][:, 1024:1536], ALU.mult),
                 reads=["thb", "gpbA%d" % b], writes=["gpb%d" % b])
            P.dma("sp", gp_d[r0:r0 + 128, :], gpb[b][:], reads=["gpb%d" % b, "gpbA%d" % b], writes=["gp%d" % ci], key="gpb%d" % b)
            P.dma("sp", la_d[r0:r0 + 128, :], la[b][:], reads=["la%d" % b], writes=["lad%d" % ci], key="la%d" % b)
    ph.close()
    for dirn in range(2):
        ph = Phase(nc, P)
        idt = ph.sb([128, 128], BF16)
        Mc = ph.sb([128, 3, 128], BF16)
        Ind = ph.sb([128, 16], BF16)
        lhi = ph.sb([128, 256], BF16)
        llo = ph.sb([128, 256], BF16)
        lres = ph.sb([128, 256], F32)
        mskT = ph.sb([128, 128], F32)
        lq = ph.sb([128, 1], F32)
        gng = ph.sb([128, 4, 128], F32)
        gpt = [ph.sb([128, 1536], BF16) for _ in range(3)]
        lat = [ph.sb([128, 256], F32) for _ in range(2)]
        E = ph.sb([128, 4, 256], F32)
        dec = [ph.sb([64, 4, 16], F32) for _ in range(2)]
        prod = [ph.sb([128, 4, 256], BF16) for _ in range(2)]
        qdT = ph.sb([64, 4, 128], BF16)
        kdT = ph.sb([64, 4, 128], BF16)
        qeT = [ph.sb([64, 4, 2, 128], BF16) for _ in range(2)]
        scT = [ph.sb([128, 4, 128], BF16) for _ in range(2)]
        S = ph.sb([64, 4, 128], F32)
        Sb = [ph.sb([64, 4, 128], BF16) for _ in range(2)]
        of32 = [ph.sb([128, 512], F32) for _ in range(3)]
        pc = ph.ps([128, 1024], F32)
        pT = ph.ps([128, 3, 4, 128], F32)
        pSc = ph.ps([128, 4, 128], F32)
        po = ph.ps([128, 4, 128], F32)
        pU = ph.ps([128, 4, 128], F32)
        if dirn == 1:
            sq = ph.sb([128, 512], F32)
            ss = ph.sb([128, 16], F32)
            er = ph.sb([128, 512], F32)
            ob16 = [ph.sb([128, 512], BF16) for _ in range(2)]
            mT = [ph.sb([128, 4, 128], BF16) for _ in range(2)]
            for h in range(4):
                P.dma("sp", gng[:, h, :], bcast_rows(W["gla_gn_g"], 128), writes=["gng%d" % h], key="gng")
        GNG = ["gng%d" % h for h in range(4)]
        P.dma("sp", idt[:], CN["ident"], writes=["idt"], key="idt")
        for m in range(3):
            P.dma("pool", Mc[:, m, :], CN["glaM"][dirn, m], writes=["Mc%d" % m], key="Mc")
        P.dma("pool", Ind[:], CN["glaInd"], writes=["Ind"], key="Ind")
        P.dma("sp", mskT[:], CN["glaMask"][dirn], writes=["mskT"], key="mskT")
        Q = [P]
        P.op("pool", lambda e: e.memset(lq[:], math.log(0.125)), writes=["lq"])
        P.op("pool", lambda e: e.memset(qeT[0][:], 0.0), writes=["qeT0"])
        P.op("pool", lambda e: e.memset(qeT[1][:], 0.0), writes=["qeT1"])
        P.op("pool", lambda e: e.memset(S[:], 0.0), writes=["S"])
        P.op("pool", lambda e: e.memset(Sb[0][:], 0.0), writes=["Sb0"])
        NOUT = NCH // 2 if half else NCH
        order = list(range(NOUT)) if dirn == 0 else list(range(NCH - 1, -1, -1))
        NU = len(order)
        c0, c1 = (0, 1) if dirn == 0 else (1, 0)

        def prep(n):
            ci = order[n]
            b = n % 2
            b3 = n % 3
            r0 = base + ci * 128
            gt = gpt[b3]
            gtn = "gpt%d" % b3
            Q[0].dma("sp", gt[:], gp_d[r0:r0 + 128, :], writes=[gtn], key=gtn)
            Q[0].dma("sp", lat[b][:], la_d[r0:r0 + 128, dirn * 256:(dirn + 1) * 256], writes=["lat%d" % b], key="lat%d" % b)
            if dirn == 1 and ci < NOUT:
                Q[0].dma("sp", of32[b3][:], of_d[r0:r0 + 128, :], writes=["of32_%d" % b3], key="ofl%d" % b3)
            Q[0].op("dve", lambda e: e.tensor_copy(lhi[:], lat[b][:]), reads=["lat%d" % b], writes=["lhi"])
            Q[0].op("dve", lambda e: e.tensor_tensor(lres[:], lat[b][:], lhi[:], ALU.subtract), reads=["lat%d" % b, "lhi"], writes=["lres"])
            Q[0].op("dve", lambda e: e.tensor_copy(llo[:], lres[:]), reads=["lres"], writes=["llo"])

            def cm(e):
                for m in range(3):
                    e.matmul(pc[:, m * 256:(m + 1) * 256], Mc[:, m, :], lhi[:], start=True, stop=False)
                    ins = e.matmul(pc[:, m * 256:(m + 1) * 256], Mc[:, m, :], llo[:], start=False, stop=True)
                return ins
            Q[0].op("pe", cm, reads=["lhi", "llo", "Mc0", "Mc1", "Mc2"], writes=["pc"])

            def dl(e):
                for h in range(4):
                    e.matmul(pc[0:64, 768 + h * 16:768 + (h + 1) * 16], lhi[:, h * 64:(h + 1) * 64], Ind[:], start=True, stop=False)
                    ins = e.matmul(pc[0:64, 768 + h * 16:768 + (h + 1) * 16], llo[:, h * 64:(h + 1) * 64], Ind[:], start=False, stop=True)
                return ins
            Q[0].op("pe", dl, reads=["lhi", "llo", "Ind"], writes=["pdl"])
            Q[0].op("act", lambda e: e.activation(E[:, 0, :], pc[:, 0:256], AF.Exp, bias=lq[:, 0:1]), reads=["pc", "pdl", "lq"], writes=["E0"])
            Q[0].op("act", lambda e: e.activation(E[:, 1, :], pc[:, 0:256], AF.Exp, scale=-1.0), reads=["pc", "pdl"], writes=["E1"])
            Q[0].op("act", lambda e: e.activation(E[:, 2, :], pc[:, 512:768], AF.Exp, bias=lq[:, 0:1]), reads=["pc", "pdl", "lq"], writes=["E2"])
            Q[0].op("act", lambda e: e.activation(E[:, 3, :], pc[:, 256:512], AF.Exp), reads=["pc", "pdl"], writes=["E3"])
            Q[0].op("act", lambda e: e.activation(dec[b][:].rearrange("p h c -> p (h c)"), pc[0:64, 768:832], AF.Exp), reads=["pdl", "pc"], writes=["dec%d" % b])
            pr = prod[b]
            Q[0].op("dve", lambda e: e.tensor_tensor(pr[:, 0, :], gt[:, 0:256], E[:, 0, :], ALU.mult), reads=[gtn, "E0"], writes=["qd%d" % b])
            Q[0].op("dve", lambda e: e.tensor_tensor(pr[:, 1, :], gt[:, 256:512], E[:, 1, :], ALU.mult), reads=[gtn, "E1"], writes=["kd%d" % b])
            Q[0].op("dve", lambda e: e.tensor_tensor(pr[:, 2, :], gt[:, 0:256], E[:, 2, :], ALU.mult), reads=[gtn, "E2"], writes=["qe%d" % b])
            Q[0].op("dve", lambda e: e.tensor_tensor(pr[:, 3, :], gt[:, 256:512], E[:, 3, :], ALU.mult), reads=[gtn, "E3"], writes=["ku%d" % b])

            def tr(e):
                for m in range(3):
                    for h in range(4):
                        ins = e.matmul(pT[0:64, m, h, :], pr[:, m, h * 64:(h + 1) * 64], idt[:], start=True, stop=True)
                return ins
            Q[0].op("pe", tr, reads=["qd%d" % b, "kd%d" % b, "qe%d" % b, "idt"], writes=["pT"])
            Q[0].op("act", lambda e: e.copy(qdT[:], pT[0:64, 0, :, :]), reads=["pT"], writes=["qdT"])
            Q[0].op("act", lambda e: e.copy(kdT[:], pT[0:64, 1, :, :]), reads=["pT"], writes=["kdT"])

            def qecp(e):
                e.copy(qeT[b][:, :, 0, 0:64], pT[0:64, 2, :, 0:64])
                return e.copy(qeT[b][:, :, 1, 64:128], pT[0:64, 2, :, 64:128])
            Q[0].op("act", qecp, reads=["pT"], writes=["qeT%d" % b])

            def sc(e):
                for h in range(4):
                    ins = e.matmul(pSc[:, h, :], kdT[:, h, :], qdT[:, h, :], start=True, stop=True)
                return ins
            Q[0].op("pe", sc, reads=["qdT", "kdT"], writes=["pSc"])
            Q[0].op("dve", lambda e: e.tensor_tensor(scT[b][:], pSc[:], mskT[:].unsqueeze(1).broadcast_to([128, 4, 128]), ALU.mult),
                 reads=["pSc", "mskT"], writes=["scT%d" % b])

        def state(n):
            ci = order[n]
            b = n % 2
            b3 = n % 3
            r0 = base + ci * 128
            gt = gpt[b3]
            gtn = "gpt%d" % b3
            pr = prod[b]

            def upd(c):
                def f(e):
                    for h in range(4):
                        ins = e.matmul(pU[0:64, h, :], pr[c * 64:(c + 1) * 64, 3, h * 64:(h + 1) * 64],
                                       gt[c * 64:(c + 1) * 64, 512 + h * 128:512 + (h + 1) * 128], start=True, stop=True)
                    return ins
                return f

            def supd(c, dst, dstn):
                def sfused(e):
                    for h in range(4):
                        ins = e.scalar_tensor_tensor(S[:, h, :], S[:, h, :], dec[b][:, h, c:c + 1], pU[0:64, h, :], ALU.mult, ALU.add)
                    return ins
                Q[0].op("dve", sfused, reads=["S", "dec%d" % b, "pU"], writes=["S"])
                Q[0].op("act", lambda e: e.copy(dst[:], S[:]), reads=["S"], writes=[dstn])
            Q[0].op("pe", upd(c0), reads=["ku%d" % b, gtn], writes=["pU"])
            supd(c0, Sb[1], "Sb1")
            if ci >= NOUT:
                Q[0].op("pe", upd(c1), reads=["ku%d" % b, gtn], writes=["pU"])
                supd(c1, Sb[0], "Sb0")
                return

            def outm(e):
                for h in range(4):
                    e.matmul(po[:, h, :], scT[b][:, h, :], gt[:, 512 + h * 128:512 + (h + 1) * 128], start=True, stop=False)
                    e.matmul(po[:, h, :], qeT[b][:, h, c0, :], Sb[0][:, h, :], start=False, stop=False)
                    ins = e.matmul(po[:, h, :], qeT[b][:, h, c1, :], Sb[1][:, h, :], start=False, stop=True)
                return ins
            Q[0].op("pe", outm, reads=["scT%d" % b, gtn, "qeT%d" % b, "Sb0", "Sb1"], writes=["po"])
            Q[0].op("pe", upd(c1), reads=["ku%d" % b, gtn], writes=["pU"])
            supd(c1, Sb[0], "Sb0")
            o = of32[b3]
            on = "of32_%d" % b3
            if dirn == 0:
                Q[0].op("act", lambda e: e.copy(o[:], po[:].rearrange("p h e -> p (h e)")), reads=["po"], writes=[on])
                Q[0].dma("sp", of_d[r0:r0 + 128, :], o[:], reads=[on], writes=["ofd%d" % ci], key="ofs%d" % b3)
            else:
                ov = o[:].rearrange("p (h e) -> p h e", e=128)
                Q[0].op("dve", lambda e: e.tensor_tensor(o[:], o[:], po[:].rearrange("p h e -> p (h e)"), ALU.add), reads=[on, "po"], writes=[on])
                Q[0].op("act", lambda e: e.activation(sq[:], o[:], AF.Square), reads=[on], writes=["sq"])
                Q[0].op("dve", lambda e: e.tensor_reduce(ss[:, 0:4], sq[:].rearrange("p (h e) -> p h e", e=128), AX.X, ALU.add), reads=["sq"], writes=["ss0"])
                Q[0].op("dve", lambda e: e.tensor_scalar(ss[:, 4:8], ss[:, 0:4], 1.0 / 128, 1e-5, ALU.mult, ALU.add), reads=["ss0"], writes=["ss1"])
                Q[0].op("act", lambda e: e.activation(ss[:, 8:12], ss[:, 4:8], AF.Sqrt), reads=["ss1"], writes=["ss2"])
                Q[0].op("dve", lambda e: e.reciprocal(ss[:, 12:16], ss[:, 8:12]), reads=["ss2"], writes=["ss3"])
                Q[0].op("dve", lambda e: e.tensor_tensor(ov, ov, ss[:, 12:16].unsqueeze(2).broadcast_to([128, 4, 128]), ALU.mult), reads=[on, "ss3"], writes=[on])
                Q[0].op("pool", lambda e: e.tensor_tensor(o[:], o[:], gng[:].rearrange("p h e -> p (h e)"), ALU.mult), reads=[on] + GNG, writes=[on])
                Q[0].op("dve", lambda e: e.tensor_tensor(ob16[b][:], o[:], gt[:, 1024:1536], ALU.mult), reads=[on, gtn], writes=["ob16_%d" % b])

        def tail(n):
            if dirn == 0 or order[n] >= NOUT:
                return
            ci = order[n]
            b = n % 2
            r0 = base + ci * 128

            def to(e):
                for c in range(4):
                    ins = e.matmul(pT[:, 0, c, :], ob16[b][:, c * 128:(c + 1) * 128], idt[:], start=True, stop=True)
                return ins
            Q[0].op("pe", to, reads=["ob16_%d" % b, "idt"], writes=["pT"])
            Q[0].op("act", lambda e: e.copy(mT[b][:], pT[:, 0, :, :]), reads=["pT"], writes=["mT%d" % b])
            Q[0].dma("sp", mixG[:, :, r0:r0 + 128], mT[b][:], reads=["mT%d" % b], writes=["mixG%d" % ci], key="mT%d" % b)

        def rec(fn, n):
            r = _Rec()
            Q[0] = r
            fn(n)
            Q[0] = P
            return r.calls

        def interleave(A, B):
            out, ia, ib, na, nb = [], 0, 0, len(A), len(B)
            while ia < na or ib < nb:
                if ib >= nb or (ia < na and ia * nb <= ib * na):
                    out.append(A[ia])
                    ia += 1
                else:
                    out.append(B[ib])
                    ib += 1
            return out

        preps = [rec(prep, n) for n in range(NU)]
        loads = [[c for c in calls if c[0] == "dma"] for calls in preps]
        comps = [[c for c in calls if c[0] != "dma"] for calls in preps]
        _replay(P, loads[0])
        if NU > 1:
            _replay(P, loads[1])
        _replay(P, comps[0])
        for n in range(NU):
            if n + 2 < NU:
                _replay(P, loads[n + 2])
            _replay(P, interleave(comps[n + 1] if n + 1 < NU else [], rec(state, n)))
            if n >= 1:
                tail(n - 1)
        tail(NU - 1)
        ph.close()


def wout_phase(nc, P, NT, W, CN, SC, h1, h2, h2T):
    mixG, mixA = SC["mixG"], SC["mixA"]
    ph = Phase(nc, P)
    wog = ph.sb([128, 4, D], BF16)
    woa = ph.sb([128, 4, D], BF16)
    lng = ph.sb([128, D], F32)
    lnb = ph.sb([128, D], F32)
    idt = ph.sb([128, 128], BF16)
    mG = [ph.sb([128, 4, 512], BF16) for _ in range(2)]
    mA = [ph.sb([128, 4, 512], BF16) for _ in range(2)]
    x32 = [ph.sb([128, D], F32) for _ in range(3)]
    z = [ph.sb([128, D], F32) for _ in range(2)]
    hb = [ph.sb([128, D], BF16) for _ in range(3)]
    hTt = [ph.sb([128, 8, 512], BF16) for _ in range(2)]
    stats = ph.sb([128, 2, 6], F32)
    mv = ph.sb([128, 8], F32)
    py = [ph.ps([128, 1024], F32) for _ in range(2)]
    pT = [ph.ps([128, 8, 128], BF16) for _ in range(2)]
    for c in range(4):
        P.dma("pool", wog[:, c, :], W["w_out"][c * 128:(c + 1) * 128, :], writes=["wog%d" % c], key="wog")
    for c in range(4):
        P.dma("pool", woa[:, c, :], W["w_out"][512 + c * 128:512 + (c + 1) * 128, :], writes=["woa%d" % c], key="woa")
    WO = ["wog%d" % c for c in range(4)] + ["woa%d" % c for c in range(4)]
    P.dma("sp", lng[:], bcast_rows(W["ln2_g"], 128), writes=["lng"], key="lng")
    P.dma("sp", lnb[:], bcast_rows(W["ln2_b"], 128), writes=["lnb"], key="lnb")
    P.dma("sp", idt[:], CN["ident"], writes=["idt"], key="idt")
    def load_tile(ti):
        t0 = ti * 512
        tb = ti % 2
        P.dma("sp", mG[tb][:], mixG[:, :, t0:t0 + 512], writes=["mG%d" % tb], key="mG%d" % tb)
        P.dma("sp", mA[tb][:], mixA[:, :, t0:t0 + 512], writes=["mA%d" % tb], key="mA%d" % tb)

    def main(ti, s):
        t0 = ti * 512
        tb = ti % 2
        gi = ti * 4 + s
        q = gi % 2
        r0 = t0 + s * 128

        def om(e):
            for hf in range(2):
                cs = slice(hf * 512, (hf + 1) * 512)
                for c in range(4):
                    e.matmul(py[q][:, cs], mG[tb][:, c, s * 128:(s + 1) * 128], wog[:, c, cs], start=(c == 0), stop=False)
                for c in range(4):
                    ins = e.matmul(py[q][:, cs], mA[tb][:, c, s * 128:(s + 1) * 128], woa[:, c, cs], start=False, stop=(c == 3))
            return ins
        P.op("pe", om, reads=["mG%d" % tb, "mA%d" % tb] + WO, writes=["py%d" % q])
        q3 = gi % 3
        xq = x32[q3]
        P.op("dve", lambda e: e.scalar_tensor_tensor(z[q][:], py[q][:], 1.0 / ALPHA, xq[:], ALU.mult, ALU.add),
             reads=["py%d" % q, "x32_%d" % q3], writes=["z%d" % q])
        layer_norm(P, z[q], "z%d" % q, stats, mv, lng, lnb, LN_EPS / ALPHA ** 2, xq, "x32_%d" % q3)
        P.dma("sp", h2[r0:r0 + 128, :], xq[:], reads=["x32_%d" % q3], writes=["h2_%d" % gi], key="st32_%d" % q3)
        P.op("act", lambda e: e.copy(hb[q3][:], xq[:]), reads=["x32_%d" % q3], writes=["hb%d" % q3])

    def load_x(ti, s):
        gi = ti * 4 + s
        r0 = ti * 512 + s * 128
        P.dma("sp", x32[gi % 3][:], h1[r0:r0 + 128, :], writes=["x32_%d" % (gi % 3)], key="x32_%d" % (gi % 3))

    def htrans(ti, s):
        tb = ti % 2
        q = (ti * 4 + s) % 2
        q3 = (ti * 4 + s) % 3
        for k in range(8):
            P.op("pe", lambda e, k=k: e.transpose(pT[q][:, k, :], hb[q3][:, k * 128:(k + 1) * 128], idt[:]),
                 reads=["hb%d" % q3, "idt"], writes=["pT%d_%d" % (q, k)])
        P.op("act", lambda e: e.copy(hTt[tb][:, :, s * 128:(s + 1) * 128], pT[q][:]),
             reads=["pT%d_%d" % (q, k) for k in range(8)], writes=["hTt%d_%d" % (tb, s)])
        if s == 3:
            P.dma("sp", h2T[:, :, ti * 512:(ti + 1) * 512], hTt[tb][:], reads=["hTt%d_%d" % (tb, ss) for ss in range(4)],
                  writes=["h2T_%d" % ti], key="stT_%d" % tb)

    steps = [(ti, s) for ti in range(NT // 512) for s in range(4)]
    load_tile(0)
    load_x(*steps[0])
    for idx, (ti, s) in enumerate(steps):
        if s == 0 and ti + 1 < NT // 512:
            load_tile(ti + 1)
        if idx + 1 < len(steps):
            load_x(*steps[idx + 1])
        main(ti, s)
        if idx >= 2:
            htrans(*steps[idx - 2])
    htrans(*steps[-2])
    htrans(*steps[-1])
    ph.close()


def gate_phase(nc, P, NT, W, CN, p_d, h3, h3T, y):
    ph = Phase(nc, P)
    wpg = ph.sb([128, 8, D], BF16)
    wpe = ph.sb([128, 2, D], BF16)
    bpg = ph.sb([128, D], F32)
    idt = ph.sb([128, 128], BF16)
    hTt = [ph.sb([128, 8, 512], BF16) for _ in range(2)]
    pb = [ph.sb([128, 4, PD], BF16) for _ in range(2)]
    ppT = [ph.sb([128, 2, 128], BF16) for _ in range(2)]
    x32 = [ph.sb([128, D], F32) for _ in range(3)]
    gs = [ph.sb([128, D], F32) for _ in range(2)]
    pes = [ph.sb([128, D], F32) for _ in range(2)]
    pgt2 = [ph.ps([128, 1024], F32) for _ in range(2)]
    ppe = ph.ps([128, 1024], F32)
    pT = [ph.ps([128, 2, 128], BF16) for _ in range(2)]
    for k in range(8):
        P.dma("pool", wpg[:, k, :], W["w_pg"][k * 128:(k + 1) * 128, :], writes=["wpg%d" % k], key="wpg")
    for k in range(2):
        P.dma("pool", wpe[:, k, :], W["w_pe"][k * 128:(k + 1) * 128, :], writes=["wpe%d" % k], key="wpe")
    WPG = ["wpg%d" % k for k in range(8)]
    WPE = ["wpe%d" % k for k in range(2)]
    P.dma("sp", bpg[:], bcast_rows(W["b_pg"], 128), writes=["bpg"], key="bpg")
    P.dma("sp", idt[:], CN["ident"], writes=["idt"], key="idt")
    def load_tile(ti):
        tb = ti % 2
        P.dma("sp", hTt[tb][:], h3T[:, :, ti * 512:(ti + 1) * 512], writes=["hTt%d" % tb], key="hTt%d" % tb)
        P.dma("pool", pb[tb][:], p_d[ti * 512:(ti + 1) * 512, :].rearrange("(s p) f -> p s f", p=128), writes=["pb%d" % tb], key="pb%d" % tb)

    def ptrans(ti, s):
        q = (ti * 4 + s) % 2
        tb = ti % 2

        def tp(e):
            for c in range(2):
                ins = e.transpose(pT[q][:, c, :], pb[tb][:, s, c * 128:(c + 1) * 128], idt[:])
            return ins
        P.op("pe", tp, reads=["pb%d" % tb, "idt"], writes=["pT%d" % q])
        P.op("act", lambda e: e.copy(ppT[q][:], pT[q][:]), reads=["pT%d" % q], writes=["ppT%d" % q])

    def main(ti, s):
        tb = ti % 2
        gi = ti * 4 + s
        q = gi % 2
        r0 = ti * 512 + s * 128
        pgt = pgt2[q]

        def gm(e):
            for hf in range(2):
                cs = slice(hf * 512, (hf + 1) * 512)
                for k in range(8):
                    ins = e.matmul(pgt[:, cs], hTt[tb][:, k, s * 128:(s + 1) * 128], wpg[:, k, cs], start=(k == 0), stop=(k == 7))
            return ins
        P.op("pe", gm, reads=["hTt%d" % tb] + WPG, writes=["pgt%d" % q])

        def pm(e):
            for hf in range(2):
                cs = slice(hf * 512, (hf + 1) * 512)
                for c in range(2):
                    ins = e.matmul(ppe[:, cs], ppT[q][:, c, :], wpe[:, c, cs], start=(c == 0), stop=(c == 1))
            return ins
        P.op("pe", pm, reads=["ppT%d" % q] + WPE, writes=["ppe"])
        P.op("act", lambda e: e.copy(pes[q][:], ppe[:]), reads=["ppe"], writes=["pes%d" % q])
        g = gs[q]
        gn = "gs%d" % q
        P.op("dve", lambda e: e.tensor_tensor(g[:], pgt[:], bpg[:], ALU.add), reads=["pgt%d" % q, "bpg"], writes=[gn])
        P.op("act", lambda e: e.activation(g[:], g[:], AF.Tanh, scale=0.5), reads=[gn], writes=[gn])
        P.op("dve", lambda e: e.tensor_scalar(g[:], g[:], 0.5, 0.5, ALU.mult, ALU.add), reads=[gn], writes=[gn])
        P.op("dve", lambda e: e.tensor_tensor(g[:], g[:], pes[q][:], ALU.mult), reads=[gn, "pes%d" % q], writes=[gn])
        P.op("pool", lambda e: e.tensor_tensor(g[:], g[:], x32[gi % 3][:], ALU.add), reads=[gn, "x32_%d" % (gi % 3)], writes=[gn])
        P.dma("sp", y[r0:r0 + 128, :], g[:], reads=[gn], writes=["y%d" % gi], key="sty%d" % q)

    def load_x(ti, s):
        gi = ti * 4 + s
        r0 = ti * 512 + s * 128
        P.dma("sp", x32[gi % 3][:], h3[r0:r0 + 128, :], writes=["x32_%d" % (gi % 3)], key="x32_%d" % (gi % 3))

    steps = [(ti, s) for ti in range(NT // 512) for s in range(4)]
    load_tile(0)
    load_x(*steps[0])
    ptrans(*steps[0])
    for idx, (ti, s) in enumerate(steps):
        if s == 0 and ti + 1 < NT // 512:
            load_tile(ti + 1)
        if idx + 1 < len(steps):
            load_x(*steps[idx + 1])
            ptrans(*steps[idx + 1])
        main(ti, s)
    ph.close()


W_NAMES = ["ffn1_wg", "ffn1_wu", "ffn1_wd", "ln1_g", "ln1_b", "w_in", "gla_w2f", "gla_b2f", "gla_w2b", "gla_b2b",
           "gla_gn_g", "q_norm_g", "k_norm_g", "w_out", "ln2_g", "ln2_b", "ffn2_wg", "ffn2_wu", "ffn2_wd",
           "ln3_g", "ln3_b", "w_pg", "b_pg", "w_pe"]
W_SHAPES = {"ffn1_wg": [D, DFF], "ffn1_wu": [D, DFF], "ffn1_wd": [DFF, D], "ln1_g": [D], "ln1_b": [D], "w_in": [D, DIN],
            "gla_w2f": [16, 256], "gla_b2f": [256], "gla_w2b": [16, 256], "gla_b2b": [256], "gla_gn_g": [128],
            "q_norm_g": [64], "k_norm_g": [64], "w_out": [D, D], "ln2_g": [D], "ln2_b": [D], "ffn2_wg": [D, DFF],
            "ffn2_wu": [D, DFF], "ffn2_wd": [DFF, D], "ln3_g": [D], "ln3_b": [D], "w_pg": [D, D], "b_pg": [D], "w_pe": [PD, D]}


def build(T, NSEQ, upto=99, debug=False, halfB=False):
    NT = T * NSEQ
    NTE = NT - T // 2 if halfB else NT
    nc = bass.Bass("TRN2", target_bir_lowering=False)
    x = nc.dram_tensor("x", [NT, D], F32, kind="ExternalInput").ap()
    p = nc.dram_tensor("p", [NT, PD], F32, kind="ExternalInput").ap()
    W = {n: nc.dram_tensor(n, W_SHAPES[n], F32, kind="ExternalInput").ap() for n in W_NAMES}
    CN = {"ident": nc.dram_tensor("ident", [128, 128], BF16, kind="ExternalInput").ap(),
          "ropeC": nc.dram_tensor("ropeC", [T, 64], F32, kind="ExternalInput").ap(),
          "ropeS": nc.dram_tensor("ropeS", [T, 64], F32, kind="ExternalInput").ap(),
          "glaM": nc.dram_tensor("glaM", [2, 3, 128, 128], F32, kind="ExternalInput").ap(),
          "glaInd": nc.dram_tensor("glaInd", [128, 16], F32, kind="ExternalInput").ap(),
          "glaMask": nc.dram_tensor("glaMask", [2, 128, 128], F32, kind="ExternalInput").ap()}
    y = nc.dram_tensor("y", [NTE, D], F32, kind="ExternalOutput").ap()
    sk = "ExternalOutput" if debug else "Internal"
    dt = lambda n, sh, ty: nc.dram_tensor(n, sh, ty, kind=sk).ap()
    h1 = dt("h1", [NT, D], F32)
    h1T = dt("h1T", [128, 8, NT], BF16)
    SC = {"mixA": dt("mixA", [128, 4, NT], BF16), "mixG": dt("mixG", [128, 4, NT], BF16),
          "gp": dt("gp", [NT, 1536], BF16), "la": dt("la", [NT, 512], F32), "of": dt("of", [NT, 512], F32)}
    h2 = dt("h2", [NT, D], F32)
    h2T = dt("h2T", [128, 8, NT], BF16)
    h3 = dt("h3", [NT, D], F32)
    h3T = dt("h3T", [128, 8, NT], BF16)
    with contextlib.ExitStack() as stack:
        P = Prog(nc, stack)
        ffn_phase(nc, P, "f1", NT, x, None, W["ffn1_wg"], W["ffn1_wu"], W["ffn1_wd"], W["ln1_g"], W["ln1_b"], CN["ident"],
                  h1, h1T, 0.5 / ALPHA, LN_EPS / ALPHA ** 2)
        for seq in range(NSEQ):
            hb = halfB and seq == NSEQ - 1
            if upto >= 2:
                attn_phase(nc, P, T, seq, h1T, W, CN, SC["mixA"], nqb=(T // 1024 if hb else None))
            if upto >= 3:
                gla_phase(nc, P, T, seq, h1T, W, CN, SC, half=hb)
        NT = NTE
        if upto >= 4:
            wout_phase(nc, P, NT, W, CN, SC, h1, h2, h2T)
        if upto >= 5:
            ffn_phase(nc, P, "f2", NT, h2, h2T, W["ffn2_wg"], W["ffn2_wu"], W["ffn2_wd"], W["ln3_g"], W["ln3_b"], CN["ident"],
                      h3, h3T, 0.5 / ALPHA, LN_EPS / ALPHA ** 2)
            gate_phase(nc, P, NT, W, CN, p, h3, h3T, y)
    return nc


def gla_consts():
    t = np.arange(128)
    a = t % 64
    c = t // 64
    same = (c[:, None] == c[None, :]).astype(np.float32)
    aj, ai = a[:, None], a[None, :]
    M = np.zeros((2, 3, 128, 128), np.float32)
    M[0, 0] = same * ((aj <= ai).astype(np.float32) - (aj <= 31).astype(np.float32))
    M[0, 1] = same * (aj > ai)
    M[0, 2] = same * (aj <= ai)
    M[1, 0] = same * ((aj >= ai).astype(np.float32) - (aj >= 32).astype(np.float32))
    M[1, 1] = same * (aj < ai)
    M[1, 2] = same * (aj >= ai)
    M *= -1.0 / 16
    Ind = np.zeros((128, 16), np.float32)
    Ind[t, c] = -1.0 / 16
    mask = np.zeros((2, 128, 128), np.float32)
    mask[0] = same * (aj <= ai)
    mask[1] = same * (aj > ai)
    return M, Ind, mask


def rope_tables(T):
    t = np.arange(T)
    row = (t // 64).astype(np.float32)
    col = (t % 64).astype(np.float32)
    inv = (10000.0 ** (-np.arange(0, 32, 2, dtype=np.float32) / 32)).astype(np.float32)
    ar = row[:, None] * inv[None, :]
    ac = col[:, None] * inv[None, :]
    C = np.concatenate([np.cos(ar), np.cos(ar), np.cos(ac), np.cos(ac)], axis=1).astype(np.float32)
    S = np.concatenate([-np.sin(ar), np.sin(ar), -np.sin(ac), np.sin(ac)], axis=1).astype(np.float32)
    return C, S


def consts(T, flip=False):
    C, S = rope_tables(T)
    M, Ind, mask = gla_consts()
    if flip:
        C, S = C[::-1].copy(), S[::-1].copy()
        t = np.arange(128)
        a = t % 64
        same = ((t // 64)[:, None] == (t // 64)[None, :]).astype(np.float32)
        mask = np.stack([same * (a[:, None] < a[None, :]), same * (a[:, None] >= a[None, :])]).astype(np.float32)
    return {"ident": np.eye(128, dtype=np.float32).astype(ml_dtypes.bfloat16), "ropeC": C, "ropeS": S,
            "glaM": M, "glaInd": Ind, "glaMask": mask}


_CACHE = {}


def kernel(**inputs):
    T, NSEQ, NCORE = 4096, 2, 8
    H = T // 2
    xp = np.asarray(inputs["x_prompt"], np.float32)
    xs = np.asarray(inputs["x_sample"], np.float32)
    pp = np.asarray(inputs["p_prompt"], np.float32)[0]
    ps = np.asarray(inputs["p_sample"], np.float32)[0]
    Wn = {n: np.ascontiguousarray(np.asarray(inputs[n], np.float32)[0]) for n in W_NAMES}
    Wf = dict(Wn)
    Wf["gla_w2f"], Wf["gla_w2b"] = Wn["gla_w2b"], Wn["gla_w2f"]
    Wf["gla_b2f"], Wf["gla_b2b"] = Wn["gla_b2b"], Wn["gla_b2f"]
    wi = Wn["w_in"].copy()
    wi[:, 1536:1552], wi[:, 1552:1568] = Wn["w_in"][:, 1552:1568], Wn["w_in"][:, 1536:1552]
    Wf["w_in"] = wi
    cn = [consts(T, False), consts(T, True)]
    if "nc" not in _CACHE:
        _CACHE["nc"] = build(T, NSEQ, halfB=True)
    nc = _CACHE["nc"]
    in_maps = []
    for c in range(NCORE):
        flip = c >= 4
        xa, pa, xb, pb = xp[c], pp[c], xs[c % 4], ps[c % 4]
        if flip:
            xa, pa, xb, pb = xa[::-1], pa[::-1], xb[::-1], pb[::-1]
        m = {"x": np.ascontiguousarray(np.concatenate([xa, xb], 0)),
             "p": np.ascontiguousarray(np.concatenate([pa, pb], 0))}
        m.update(Wf if flip else Wn)
        m.update(cn[1] if flip else cn[0])
        in_maps.append(m)
    res = run_bass_kernel_spmd(nc, in_maps, core_ids=list(range(NCORE)))
    ys = [np.asarray(res.results[c]["y"], np.float32) for c in range(NCORE)]
    yp = np.stack([ys[c][:T] if c < 4 else ys[c][:T][::-1] for c in range(8)], 0)
    ysm = np.stack([np.concatenate([ys[j][T:T + H], ys[j + 4][T:T + H][::-1]], 0) for j in range(4)], 0)
    return (np.ascontiguousarray(yp), np.ascontiguousarray(ysm))
```

```python
import contextlib
import math
import numpy as np
import ml_dtypes
import concourse.bass as bass
import concourse.mybir as mybir
from concourse.bass_utils import run_bass_kernel_spmd

F32 = mybir.dt.float32
BF16 = mybir.dt.bfloat16
AF = mybir.ActivationFunctionType
ALU = mybir.AluOpType
AX = mybir.AxisListType

D = 1024
DFF = 2816
NF = DFF // 128
PD = 256
DIN = 2336
ALPHA = 2.0 ** 0.25
LN_EPS = 1e-5
ENGS = ("pe", "act", "dve", "pool", "sp")


class _Op:
    __slots__ = ("eng", "fn", "deps", "is_dma", "key", "signal", "seq", "waits", "idx")

    def __init__(self, eng, fn, is_dma=False, key=None):
        self.eng = eng
        self.fn = fn
        self.deps = set()
        self.is_dma = is_dma
        self.key = key
        self.signal = False
        self.seq = None
        self.waits = None


class Prog:
    NSLOT = 40
    NPOOL = 10

    def __init__(self, nc, stack):
        self.nc = nc
        self.sem_eng = {e: stack.enter_context(nc.semaphore("s_" + e)) for e in ENGS}
        self.sem_slot = [stack.enter_context(nc.semaphore("d_%d" % i)) for i in range(self.NSLOT)]
        self.eng_cnt = {e: 0 for e in ENGS}
        self.slot_cnt = [0] * self.NSLOT
        self.known = {e: {} for e in ENGS}
        self.reset()

    def reset(self):
        self.ops = []
        self.last_w = {}
        self.readers = {}

    def _add(self, op, reads, writes):
        idx = len(self.ops)
        op.idx = idx
        for r in reads:
            w = self.last_w.get(r)
            if w is not None:
                op.deps.add(w)
        for t in writes:
            w = self.last_w.get(t)
            if w is not None:
                op.deps.add(w)
            for rd in self.readers.get(t, ()):
                op.deps.add(rd)
        op.deps.discard(idx)
        for r in reads:
            self.readers.setdefault(r, []).append(idx)
        for t in writes:
            self.last_w[t] = idx
            self.readers[t] = []
        self.ops.append(op)
        return idx

    def op(self, eng, fn, reads=(), writes=()):
        return self._add(_Op(eng, fn), reads, writes)

    def dma(self, queue, out, in_, reads=(), writes=(), key=None, **kw):
        assert key is not None
        o = _Op(queue, lambda e: e.dma_start(out=out, in_=in_, **kw), is_dma=True, key=key)
        o.signal = True
        return self._add(o, reads, writes)

    def lower(self):
        nc = self.nc
        ops = self.ops
        for o in ops:
            for d in o.deps:
                p = ops[d]
                if p.eng == "pe" and o.eng == "pe" and not p.is_dma and not o.is_dma:
                    continue
                p.signal = True
        streams = {e: [] for e in ENGS}
        for o in ops:
            streams[o.eng].append(o)
        for e in ENGS:
            lst = [o for o in streams[e] if not o.is_dma]
            if lst:
                lst[-1].signal = True
        slot_of = {}
        npool = [0]
        nsp = [0]
        for o in ops:
            if not o.signal:
                continue
            if o.is_dma:
                if o.key not in slot_of:
                    if o.eng == "pool":
                        slot_of[o.key] = npool[0]
                        npool[0] += 1
                        assert npool[0] <= self.NPOOL, "too many pool dma keys"
                    else:
                        slot_of[o.key] = self.NPOOL + nsp[0]
                        nsp[0] += 1
                        assert self.NPOOL + nsp[0] <= self.NSLOT, "too many sp dma keys"
                s = slot_of[o.key]
                self.slot_cnt[s] += 16
                o.seq = self.slot_cnt[s]
            else:
                self.eng_cnt[o.eng] += 1
                o.seq = self.eng_cnt[o.eng]

        def sem_of(p):
            return self.sem_slot[slot_of[p.key]] if p.is_dma else self.sem_eng[p.eng]

        for e in ENGS:
            known = self.known[e]
            for o in streams[e]:
                need = {}
                for d in o.deps:
                    p = ops[d]
                    if p.eng == "pe" and o.eng == "pe" and not p.is_dma and not o.is_dma:
                        continue
                    s = ("k", slot_of[p.key]) if p.is_dma else ("e", p.eng)
                    if p.seq > need.get(s, 0):
                        need[s] = p.seq
                w = []
                for s, v in need.items():
                    if known.get(s, 0) >= v:
                        continue
                    known[s] = v
                    w.append((self.sem_slot[s[1]] if s[0] == "k" else self.sem_eng[s[1]], v))
                o.waits = w
        fin = [(self.sem_slot[i], self.slot_cnt[i]) for i in range(self.NSLOT) if self.slot_cnt[i] > 0]
        fin += [(self.sem_eng[e], self.eng_cnt[e]) for e in ENGS if self.eng_cnt[e] > 0]

        with nc.Block(no_gpsimd_drain=True) as block:
            def run(e, engobj):
                for o in streams[e]:
                    for (s, v) in o.waits:
                        engobj.wait_ge(s, v)
                    ins = o.fn(engobj)
                    if o.signal:
                        ins.then_inc(sem_of(o), 16 if o.is_dma else 1)
                for (s, v) in fin:
                    engobj.wait_ge(s, v)

            @block.sync
            def _(eng):
                run("sp", eng)

            @block.tensor
            def _(eng):
                run("pe", eng)

            @block.scalar
            def _(eng):
                run("act", eng)

            @block.vector
            def _(eng):
                run("dve", eng)

            @block.gpsimd
            def _(eng):
                run("pool", eng)
        self.reset()


_CNT = [0]


class _Rec:
    def __init__(self):
        self.calls = []

    def op(self, *a, **k):
        self.calls.append(("op", a, k))

    def dma(self, *a, **k):
        self.calls.append(("dma", a, k))


def _replay(P, calls):
    for kind, a, k in calls:
        getattr(P, kind)(*a, **k)


class Phase:
    def __init__(self, nc, P):
        self.nc = nc
        self.P = P
        self.es = contextlib.ExitStack()
        self.n = 0

    def sb(self, shape, dt, name=None):
        _CNT[0] += 1
        return self.es.enter_context(self.nc.sbuf_tensor(name or ("t%d" % _CNT[0]), list(shape), dt))

    def ps(self, shape, dt, name=None):
        _CNT[0] += 1
        return self.es.enter_context(self.nc.psum_tensor(name or ("p%d" % _CNT[0]), list(shape), dt))

    def close(self):
        self.P.lower()
        self.es.close()


def bcast_rows(ap_row, n):
    return ap_row.unsqueeze(0).broadcast_to([n, ap_row.shape[0]])


def ffn_phase(nc, P, tag, NT, src32, srcT, wg_d, wu_d, wd_d, g_d, b_d, ident_d, dst32, dstT, cres, eps):
    TT = 256
    NS = TT // 128
    ph = Phase(nc, P)
    wg = ph.sb([128, 8, DFF], BF16)
    wu = ph.sb([128, 8, DFF], BF16)
    wd = ph.sb([128, NF, D], BF16)
    lng = ph.sb([128, D], F32)
    lnb = ph.sb([128, D], F32)
    idt = ph.sb([128, 128], BF16)
    xb = [ph.sb([128, NS, D], BF16) for _ in range(2)] if srcT is None else None
    xT = [ph.sb([128, 8, TT], BF16) for _ in range(2)]
    hT = ph.sb([128, NF, TT], BF16)
    sg = [ph.sb([128, TT], F32) for _ in range(2)]
    x32 = [ph.sb([128, D], F32) for _ in range(2)]
    z = [ph.sb([128, D], F32) for _ in range(2)]
    hb = [ph.sb([128, D], BF16) for _ in range(2)]
    hTt = [ph.sb([128, 8, TT], BF16) for _ in range(2)]
    stats = ph.sb([128, 2, 6], F32)
    mv = ph.sb([128, 8], F32)
    pgu = [ph.ps([128, 512], F32) for _ in range(2)]
    py = [ph.ps([128, 1024], F32) for _ in range(2)]
    pT = [ph.ps([128, 8, 128], BF16) for _ in range(2)]

    HALF = DFF // 2
    ntile = NT // TT
    P.dma("sp", idt[:], ident_d, writes=["idt"], key="idt")
    WGUH = [["wg%d_%d" % (k, h) for k in range(8)] + ["wu%d_%d" % (k, h) for k in range(8)] for h in range(2)]
    WD = ["wd%d" % f for f in range(NF)]

    def load_weights():
        for h in range(2):
            for k in range(8):
                P.dma("pool", wg[:, k, h * HALF:(h + 1) * HALF], wg_d[k * 128:(k + 1) * 128, h * HALF:(h + 1) * HALF],
                      writes=["wg%d_%d" % (k, h)], key="wg%d" % h)
                P.dma("pool", wu[:, k, h * HALF:(h + 1) * HALF], wu_d[k * 128:(k + 1) * 128, h * HALF:(h + 1) * HALF],
                      writes=["wu%d_%d" % (k, h)], key="wu%d" % h)
        for f in range(NF):
            P.dma("pool", wd[:, f, :], wd_d[f * 128:(f + 1) * 128, :], writes=["wd%d" % f], key="wd")
        P.dma("sp", lng[:], bcast_rows(g_d, 128), writes=["lng"], key="lng")
        P.dma("sp", lnb[:], bcast_rows(b_d, 128), writes=["lnb"], key="lnb")

    def load_x(ti):
        t0 = ti * TT
        tb = ti % 2
        if srcT is None:
            for s in range(NS):
                P.dma("pool", xb[tb][:, s, :], src32[t0 + s * 128:t0 + (s + 1) * 128, :], writes=["xb%d_%d" % (tb, s)], key="xb%d_%d" % (tb, s))
        else:
            P.dma("sp", xT[tb][:], srcT[:, :, t0:t0 + TT], writes=["xT%d" % tb], key="xT%d" % tb)

    def xtrans(ti):
        if srcT is not None:
            return
        tb = ti % 2
        for s in range(NS):
            pTc = pT[s % 2]
            for k in range(8):
                P.op("pe", lambda e, s=s, k=k, pTc=pTc: e.transpose(pTc[:, k, :], xb[tb][:, s, k * 128:(k + 1) * 128], idt[:]),
                     reads=["xb%d_%d" % (tb, s), "idt"], writes=["pT%d_%d" % (s % 2, k)])
            P.op("act", lambda e, s=s, pTc=pTc: e.copy(xT[tb][:, :, s * 128:(s + 1) * 128], pTc[:]),
                 reads=["pT%d_%d" % (s % 2, k) for k in range(8)], writes=["xT%d_%d" % (tb, s)])

    def xT_reads(ti):
        tb = ti % 2
        return ["xT%d_%d" % (tb, s) for s in range(NS)] if srcT is None else ["xT%d" % tb]

    def up(ti, f):
        b = f % 2
        xTc = xT[ti % 2]

        def upm(e):
            for k in range(8):
                e.matmul(pgu[b][:, 0:TT], wg[:, k, f * 128:(f + 1) * 128], xTc[:, k, :], start=(k == 0), stop=(k == 7))
            for k in range(8):
                ins = e.matmul(pgu[b][:, TT:2 * TT], wu[:, k, f * 128:(f + 1) * 128], xTc[:, k, :], start=(k == 0), stop=(k == 7))
            return ins
        P.op("pe", upm, reads=xT_reads(ti) + WGUH[0 if f < NF // 2 else 1], writes=["pgu%d" % b])
        P.op("act", lambda e: e.activation(sg[b][:], pgu[b][:, 0:TT], AF.Silu), reads=["pgu%d" % b], writes=["sg%d" % b])
        P.op("dve", lambda e: e.tensor_tensor(hT[:, f, :], sg[b][:], pgu[b][:, TT:2 * TT], ALU.mult),
             reads=["sg%d" % b, "pgu%d" % b], writes=["hT%d" % f])

    def down(ti, s, P=P):
        gi = ti * NS + s
        r0 = ti * TT + s * 128
        q = gi % 2

        def dn(e):
            for h in range(2):
                for f in range(NF):
                    ins = e.matmul(py[q][:, h * 512:(h + 1) * 512], hT[:, f, s * 128:(s + 1) * 128], wd[:, f, h * 512:(h + 1) * 512],
                                   start=(f == 0), stop=(f == NF - 1))
            return ins
        P.op("pe", dn, reads=["hT%d" % f for f in range(NF)] + WD, writes=["py%d" % q])
        P.dma("sp", x32[q][:], src32[r0:r0 + 128, :], writes=["x32_%d" % q], key="x32_%d" % q)
        P.op("dve", lambda e: e.scalar_tensor_tensor(z[q][:], py[q][:], float(cres), x32[q][:], ALU.mult, ALU.add),
             reads=["py%d" % q, "x32_%d" % q], writes=["z%d" % q])
        layer_norm(P, z[q], "z%d" % q, stats, mv, lng, lnb, eps, x32[q], "x32_%d" % q,
                   out_bf=(hb[q] if dstT is not None else None), out_bfn="hb%d" % q)
        P.dma("sp", dst32[r0:r0 + 128, :], x32[q][:], reads=["x32_%d" % q], writes=["dst32_%d" % gi], key="st32_%d" % q)

    def htrans(ti, s):
        if dstT is None:
            return
        gi = ti * NS + s
        q = gi % 2
        tb = ti % 2
        pTc = pT[q]
        for k in range(8):
            P.op("pe", lambda e, k=k: e.transpose(pTc[:, k, :], hb[q][:, k * 128:(k + 1) * 128], idt[:]),
                 reads=["hb%d" % q, "idt"], writes=["pT%d_%d" % (q, k)])
        P.op("act", lambda e: e.copy(hTt[tb][:, :, s * 128:(s + 1) * 128], pTc[:]),
             reads=["pT%d_%d" % (q, k) for k in range(8)], writes=["hTt%d_%d" % (tb, s)])
        if s == NS - 1:
            P.dma("sp", dstT[:, :, ti * TT:(ti + 1) * TT], hTt[tb][:], reads=["hTt%d_%d" % (tb, ss) for ss in range(NS)],
                  writes=["dstT_%d" % ti], key="stT_%d" % tb)

    load_x(0)
    load_weights()
    xtrans(0)
    late = []
    for ti in range(ntile):
        if ti + 1 < ntile:
            load_x(ti + 1)
        for f in range(NF):
            up(ti, f)
            if f == 3 and late:
                _replay(P, late.pop())
            if f == 10 and ti > 0:
                for s in range(NS):
                    htrans(ti - 1, s)
        if ti + 1 < ntile:
            xtrans(ti + 1)
        for s in range(NS):
            if s == NS - 1 and ti + 1 < ntile:
                r = _Rec()
                down(ti, s, r)
                k = next(n for n, c in enumerate(r.calls) if c[0] == "op" and c[1][0] == "act")
                _replay(P, r.calls[:k])
                late.append(r.calls[k:])
            else:
                down(ti, s)
    for s in range(NS):
        htrans(ntile - 1, s)
    ph.close()


def layer_norm(P, zt, zn, stats, mv, lng, lnb, eps, out, outn, out_bf=None, out_bfn=None):
    for c in range(2):
        P.op("dve", lambda e, c=c: e.bn_stats(stats[:, c, :], zt[:, c * 512:(c + 1) * 512]), reads=[zn], writes=["stats%d" % c])
    P.op("dve", lambda e: e.bn_aggr(mv[:, 0:2], stats[:]), reads=["stats0", "stats1"], writes=["mv01"])
    P.op("dve", lambda e: e.tensor_scalar(mv[:, 2:3], mv[:, 1:2], float(eps), None, ALU.add), reads=["mv01"], writes=["mv2"])
    P.op("act", lambda e: e.activation(mv[:, 3:4], mv[:, 2:3], AF.Sqrt), reads=["mv2"], writes=["mv3"])
    P.op("dve", lambda e: e.reciprocal(mv[:, 4:5], mv[:, 3:4]), reads=["mv3"], writes=["mv4"])
    P.op("dve", lambda e: e.scalar_tensor_tensor(zt[:], zt[:], mv[:, 0:1], lng[:], ALU.subtract, ALU.mult),
         reads=[zn, "mv01", "lng"], writes=[zn])
    P.op("dve", lambda e: e.scalar_tensor_tensor(out[:], zt[:], mv[:, 4:5], lnb[:], ALU.mult, ALU.add),
         reads=[zn, "mv4", "lnb"], writes=[outn])
    if out_bf is not None:
        P.op("dve", lambda e: e.scalar_tensor_tensor(out_bf[:], zt[:], mv[:, 4:5], lnb[:], ALU.mult, ALU.add),
             reads=[zn, "mv4", "lnb"], writes=[out_bfn])


def attn_phase(nc, P, T, seq, h1T, W, CN, mixA, nqb=None):
    NCH = T // 128
    outer = Phase(nc, P)
    QT = outer.sb([128, 4, T], BF16)
    KT = outer.sb([128, 2, 2, T], BF16)
    VA = outer.sb([128, NCH, 2, 65], BF16)
    VAo = outer.sb([128, NCH, 2, 128], BF16)
    negm = outer.sb([128, 1], F32)
    ones32 = outer.sb([128, 128], BF16)
    ph = Phase(nc, P)
    wq = ph.sb([128, 8, 768], BF16)
    gq = ph.sb([128, 10, 64], F32)
    gm = ph.sb([128, 4], F32)
    rC = ph.sb([128, NCH, 64], F32)
    rS = ph.sb([128, NCH, 64], F32)
    idt = ph.sb([128, 128], BF16)
    hTt = [ph.sb([128, 8, 512], BF16) for _ in range(2)]
    sq2 = [ph.sb([128, 640], F32) for _ in range(2)]
    ss2 = [ph.sb([128, 40], F32) for _ in range(2)]
    qn2 = [ph.sb([128, 10, 64], F32) for _ in range(2)]
    t12 = [ph.sb([128, 10, 64], F32) for _ in range(2)]
    t22 = [ph.sb([128, 10, 64], F32) for _ in range(2)]
    qr = [ph.sb([128, 10, 64], BF16) for _ in range(2)]
    krs = [ph.sb([128, 2, 64], BF16) for _ in range(2)]
    pq = [ph.ps([128, 1024], F32) for _ in range(2)]
    pTq = [ph.ps([128, 4, 128], BF16) for _ in range(2)]
    pTk = [ph.ps([128, 2, 128], BF16) for _ in range(2)]
    for k in range(8):
        P.dma("pool", wq[:, k, :], W["w_in"][k * 128:(k + 1) * 128, 1568:2336], writes=["wq%d" % k], key="wq")
    WQ = ["wq%d" % k for k in range(8)]
    for h in range(10):
        P.dma("sp", gq[:, h, :], bcast_rows(W["q_norm_g"] if h < 8 else W["k_norm_g"], 128), writes=["gq%d" % h], key="gq")
    GQ = ["gq%d" % h for h in range(10)]
    P.dma("sp", rC[:], CN["ropeC"].rearrange("(n p) f -> p n f", p=128), writes=["rC"], key="rC")
    P.dma("sp", rS[:], CN["ropeS"].rearrange("(n p) f -> p n f", p=128), writes=["rS"], key="rS")
    P.dma("sp", idt[:], CN["ident"], writes=["idt"], key="idt")
    P.op("pool", lambda e: e.memset(VA[:, :, :, 64:65], 1.0), writes=["VAones"])

    P.op("pool", lambda e: e.memset(VAo[:, :, :, 0:64], 0.0), writes=["VAoones"])
    P.op("pool", lambda e: e.memset(VAo[:, :, :, 0:1], 1.0), writes=["VAoones"])
    P.op("pool", lambda e: e.memset(ones32[:], 1.0), writes=["ones32"])
    P.op("pool", lambda e: e.memset(KT[:], 0.0), writes=["KTz"])
    P.op("dve", lambda e: e.tensor_reduce(gm[:, 0:1], gq[:, 0, :], AX.X, ALU.max, apply_absolute_value=True), reads=GQ, writes=["gm0"])
    P.op("dve", lambda e: e.tensor_reduce(gm[:, 1:2], gq[:, 8, :], AX.X, ALU.max, apply_absolute_value=True), reads=GQ, writes=["gm1"])
    P.op("dve", lambda e: e.tensor_tensor(gm[:, 2:3], gm[:, 0:1], gm[:, 1:2], ALU.mult), reads=["gm0", "gm1"], writes=["gm2"])
    P.op("dve", lambda e: e.tensor_scalar(negm[:], gm[:, 2:3], -8.0, None, ALU.mult), reads=["gm2"], writes=["negm"])
    loads, recs = [], []
    for ti in range(T // 512):
        hc = hTt[ti % 2]
        hn = "hTt%d" % (ti % 2)
        PP = _Rec()
        PP.dma("sp", hc[:], h1T[:, :, seq * T + ti * 512: seq * T + (ti + 1) * 512], writes=[hn], key=hn)
        loads.append(PP.calls)
        for s in range(4):
            PP = _Rec()
            recs.append(PP.calls)
            ci = ti * 4 + s
            b = ci % 2
            sq, ss, qn, t1, t2 = sq2[b], ss2[b], qn2[b], t12[b], t22[b]
            tok = slice(ci * 128, (ci + 1) * 128)

            def proj(e, s=s, b=b, hc=hc):
                for k in range(8):
                    e.matmul(pq[b][:, 0:512], hc[:, k, s * 128:(s + 1) * 128], wq[:, k, 0:512], start=(k == 0), stop=(k == 7))
                for k in range(8):
                    ins = e.matmul(pq[b][:, 512:768], hc[:, k, s * 128:(s + 1) * 128], wq[:, k, 512:768], start=(k == 0), stop=(k == 7))
                return ins
            PP.op("pe", proj, reads=[hn] + WQ, writes=["pq%d" % b])
            PP.op("act", lambda e, b=b, sq=sq, ss=ss, qn=qn, t1=t1, t2=t2: e.activation(sq[:], pq[b][:, 0:640], AF.Square), reads=["pq%d" % b], writes=["sq_" + str(b)])
            PP.op("dve", lambda e, sq=sq, ss=ss, qn=qn, t1=t1, t2=t2: e.tensor_reduce(ss[:, 0:10], sq[:].rearrange("p (h d) -> p h d", d=64), AX.X, ALU.add), reads=["sq_" + str(b)], writes=["ss0_" + str(b)])
            PP.op("dve", lambda e, sq=sq, ss=ss, qn=qn, t1=t1, t2=t2: e.tensor_scalar(ss[:, 10:20], ss[:, 0:10], 1.0 / 64, 1e-6, ALU.mult, ALU.add), reads=["ss0_" + str(b)], writes=["ss1_" + str(b)])
            PP.op("act", lambda e, sq=sq, ss=ss, qn=qn, t1=t1, t2=t2: e.activation(ss[:, 20:30], ss[:, 10:20], AF.Sqrt), reads=["ss1_" + str(b)], writes=["ss2_" + str(b)])
            PP.op("dve", lambda e, sq=sq, ss=ss, qn=qn, t1=t1, t2=t2: e.reciprocal(ss[:, 30:40], ss[:, 20:30]), reads=["ss2_" + str(b)], writes=["ss3_" + str(b)])
            PP.op("dve", lambda e, b=b, sq=sq, ss=ss, qn=qn, t1=t1, t2=t2: e.tensor_tensor(qn[:], pq[b][:, 0:640].rearrange("p (h d) -> p h d", d=64),
                                                       ss[:, 30:40].unsqueeze(2).broadcast_to([128, 10, 64]), ALU.mult),
                 reads=["pq%d" % b, "ss3_" + str(b)], writes=["qn_" + str(b)])
            PP.op("act", lambda e, b=b, ci=ci, sq=sq, ss=ss, qn=qn, t1=t1, t2=t2: e.copy(VA[:, ci, :, 0:64], pq[b][:, 640:768].rearrange("p (h d) -> p h d", d=64)),
                 reads=["pq%d" % b], writes=["VA%d" % ci])
            PP.op("act", lambda e, b=b, ci=ci: e.copy(VAo[:, ci, :, 64:128], pq[b][:, 640:768].rearrange("p (h d) -> p h d", d=64)),
                 reads=["pq%d" % b, "VAoones"], writes=["VAo%d" % ci])
            PP.op("pool", lambda e, sq=sq, ss=ss, qn=qn, t1=t1, t2=t2: e.tensor_tensor(qn[:], qn[:], gq[:], ALU.mult), reads=["qn_" + str(b)] + GQ, writes=["qn_" + str(b)])
            PP.op("pool", lambda e, ci=ci, sq=sq, ss=ss, qn=qn, t1=t1, t2=t2: e.tensor_tensor(t1[:], qn[:], rC[:, ci, :].unsqueeze(1).broadcast_to([128, 10, 64]), ALU.mult),
                 reads=["qn_" + str(b), "rC"], writes=["t1_" + str(b)])
            qv = qn[:].rearrange("p h (a c f) -> p h a c f", a=2, c=2)
            tv = t2[:].rearrange("p h (a c f) -> p h a c f", a=2, c=2)

            def rot(e, ci=ci, qv=qv, tv=tv):
                sv = rS[:, ci, :].rearrange("p (a c f) -> p a c f", a=2, c=2)
                e.tensor_tensor(tv[:, :, :, 0, :], qv[:, :, :, 1, :], sv[:, :, 0, :].unsqueeze(1).broadcast_to([128, 10, 2, 16]), ALU.mult)
                return e.tensor_tensor(tv[:, :, :, 1, :], qv[:, :, :, 0, :], sv[:, :, 1, :].unsqueeze(1).broadcast_to([128, 10, 2, 16]), ALU.mult)
            PP.op("dve", rot, reads=["qn_" + str(b), "rS"], writes=["t2_" + str(b)])
            PP.op("dve", lambda e, b=b, sq=sq, ss=ss, qn=qn, t1=t1, t2=t2: e.tensor_tensor(qr[b][:], t1[:], t2[:], ALU.add), reads=["t1_" + str(b), "t2_" + str(b)], writes=["qr%d" % b])

            def kswap(e, b=b):
                e.tensor_copy(krs[b][:, 0, :], qr[b][:, 9, :])
                return e.tensor_copy(krs[b][:, 1, :], qr[b][:, 8, :])
            PP.op("pool", kswap, reads=["qr%d" % b], writes=["krs%d" % b])

            def tq(e, b=b):
                for pr in range(4):
                    ins = e.transpose(pTq[b][:, pr, :], qr[b][:, 2 * pr:2 * pr + 2, :].rearrange("p h d -> p (h d)"), idt[:])
                return ins
            PP.op("pe", tq, reads=["qr%d" % b, "idt"], writes=["pTq%d" % b])

            def tk(e, b=b):
                e.transpose(pTk[b][:, 0, :], qr[b][:, 8:10, :].rearrange("p h d -> p (h d)"), idt[:])
                return e.transpose(pTk[b][:, 1, :], krs[b][:].rearrange("p h d -> p (h d)"), idt[:])
            PP.op("pe", tk, reads=["qr%d" % b, "krs%d" % b, "idt"], writes=["pTk%d" % b])
            PP.op("act", lambda e, b=b, tok=tok, sq=sq, ss=ss, qn=qn, t1=t1, t2=t2: e.copy(QT[:, :, tok], pTq[b][:]), reads=["pTq%d" % b], writes=["QT%d" % ci])

            def kcp(e, b=b, tok=tok):
                e.tensor_copy(KT[0:64, :, 0, tok], pTk[b][0:64, :, :])
                e.tensor_copy(KT[64:128, 0, 1, tok], pTk[b][64:128, 1, :])
                return e.tensor_copy(KT[64:128, 1, 1, tok], pTk[b][64:128, 0, :])
            PP.op("dve", kcp, reads=["pTk%d" % b, "KTz"], writes=["KT%d" % ci])
    def _split(calls):
        k = next(i for i, c in enumerate(calls) if any(str(w).startswith("krs") for w in c[2].get("writes", ())))
        return calls[:1], calls[1:k], calls[k:]
    parts = [_split(c) for c in recs]
    NR = len(parts)
    _replay(P, loads[0])
    if len(loads) > 1:
        _replay(P, loads[1])
    _replay(P, parts[0][0])
    if NR > 1:
        _replay(P, parts[1][0])
    _replay(P, parts[0][1])
    for ci in range(NR):
        nt = (ci + 2) // 4 + 1
        if (ci + 2) % 4 == 0 and nt < len(loads):
            _replay(P, loads[nt])
        if ci + 2 < NR:
            _replay(P, parts[ci + 2][0])
        if ci + 1 < NR:
            _replay(P, parts[ci + 1][1])
        _replay(P, parts[ci][2])
    ph.close()
    ph = Phase(nc, P)
    PT = [ph.sb([128, 1024], BF16) for _ in range(2)]
    rd = [ph.sb([128, 512], F32) for _ in range(2)]
    rres = [ph.sb([128, 512], F32) for _ in range(2)]
    rb16 = [ph.sb([128, 2, 512], BF16) for _ in range(2)]
    o32 = [ph.sb([128, 512], F32) for _ in range(2)]
    ob = [ph.sb([128, 4, 512], BF16) for _ in range(2)]
    pS = [ph.ps([128, 1024], F32) for _ in range(2)]
    po = [ph.ps([128, 512], F32) for _ in range(2)]
    pb = ph.ps([128, 512], F32)
    nsg = NCH // 2
    groups = [(qb, h, sgi) for qb in range(nqb if nqb is not None else T // 512) for h in range(8) for sgi in range(nsg)]

    def emit_qk_exp(i):
        qb, h, sgi = groups[i]
        b = i % 2
        pr, base, kv = h // 2, (h % 2) * 64, h // 4
        qsl = slice(qb * 512, (qb + 1) * 512)

        def qk(e):
            for j in range(2):
                c = 2 * sgi + j
                ins = e.matmul(pS[b][:, j * 512:(j + 1) * 512], KT[:, kv, h % 2, c * 128:(c + 1) * 128],
                               QT[:, pr, qsl], start=True, stop=True)
            return ins
        P.op("pe", qk, reads=[], writes=["pS%d" % b])
        P.op("act", lambda e: e.activation(PT[b][:], pS[b][:], AF.Exp, bias=negm[:, 0:1], scale=0.125),
             reads=["pS%d" % b], writes=["PT%d" % b])

    pending = []

    def emit_pv(i):
        qb, h, sgi = groups[i]
        b = i % 2
        kv = h // 4
        hb_ = (qb * 8 + h) % 2
        poc = po[hb_]
        pon = "po%d" % hb_

        odd = h % 2
        pr = h // 2
        MO = 128 if odd else 65
        dp = 0 if odd else 64
        lo = 64 if odd else 0

        def pv(e):
            for j in range(2):
                c = 2 * sgi + j
                ins = e.matmul(poc[0:MO, :], (VAo if odd else VA)[:, c, kv, :], PT[b][:, j * 512:(j + 1) * 512],
                               start=(sgi == 0 and j == 0), stop=(sgi == nsg - 1 and j == 1))
            return ins
        P.op("pe", pv, reads=["PT%d" % b], writes=[pon] if sgi in (0, nsg - 1) else [])
        if sgi == nsg - 1:
            obc = ob[qb % 2]
            obn = "ob%d" % (qb % 2)
            rdc, oc = rd[hb_], o32[hb_]
            rrc, rbc = rres[hb_], rb16[hb_]

            def dfin(e):
                e.tensor_copy(oc[lo:lo + 64, :], poc[lo:lo + 64, :])
                return e.tensor_copy(rbc[dp:dp + 1, 0, :], poc[dp:dp + 1, :])
            P.op("dve", dfin, reads=[pon], writes=["rd%d" % hb_, "o32_%d" % hb_])
            P.op("dve", lambda e: e.tensor_tensor(rrc[dp:dp + 1, :], poc[dp:dp + 1, :], rbc[dp:dp + 1, 0, :], ALU.subtract),
                 reads=["rd%d" % hb_, pon], writes=["rres%d" % hb_])
            P.op("dve", lambda e: e.tensor_copy(rbc[dp:dp + 1, 1, :], rrc[dp:dp + 1, :]), reads=["rres%d" % hb_], writes=["rb%d" % hb_])

            def fin():
                MB = lo + 64

                def bc(e):
                    e.matmul(pb[0:MB, :], ones32[dp:dp + 1, 0:MB], rbc[dp:dp + 1, 0, :], start=True, stop=False)
                    return e.matmul(pb[0:MB, :], ones32[dp:dp + 1, 0:MB], rbc[dp:dp + 1, 1, :], start=False, stop=True)
                P.op("pe", bc, reads=["rd%d" % hb_, "rb%d" % hb_], writes=["pb"])
                P.op("dve", lambda e: e.reciprocal(rrc[lo:lo + 64, :], pb[lo:lo + 64, :]), reads=["pb", "rb%d" % hb_], writes=["rec%d" % hb_])
                P.op("dve", lambda e: e.tensor_tensor(obc[lo:lo + 64, pr, :], oc[lo:lo + 64, :], rrc[lo:lo + 64, :], ALU.mult),
                     reads=["o32_%d" % hb_, "rec%d" % hb_], writes=[obn + "_%d" % h])
                if h == 7:
                    P.dma("sp", mixA[:, :, seq * T + qb * 512: seq * T + (qb + 1) * 512], obc[:],
                          reads=[obn + "_%d" % hh for hh in range(8)], writes=["mixA%d" % qb], key=obn)
            pending.append((i + min(6, nsg), fin))

    n = len(groups)
    for i in range(n + 1):
        if i < n:
            emit_qk_exp(i)
        while pending and pending[0][0] <= i:
            pending.pop(0)[1]()
        if i >= 1:
            emit_pv(i - 1)
    while pending:
        pending.pop(0)[1]()
    ph.close()
    outer.es.close()


def gla_phase(nc, P, T, seq, h1T, W, CN, SC, half=False):
    NCH = T // 128
    base = seq * T
    gp_d, la_d, of_d, mixG = SC["gp"], SC["la"], SC["of"], SC["mixG"]
    ph = Phase(nc, P)
    wq = ph.sb([128, 8, 1568], BF16)
    w2 = ph.sb([16, 2, 256], BF16)
    b2 = ph.sb([128, 512], F32)
    one1 = ph.sb([128, 1], F32)
    idt = ph.sb([128, 128], BF16)
    hTt = [ph.sb([128, 8, 512], BF16) for _ in range(2)]
    gpb = [ph.sb([128, 1536], BF16) for _ in range(2)]
    zb16 = ph.sb([128, 32], BF16)
    thb = ph.sb([128, 512], F32)
    zT = ph.sb([16, 2, 128], BF16)
    xs = ph.sb([128, 512], F32)
    e1 = ph.sb([128, 512], F32)
    la = [ph.sb([128, 512], F32) for _ in range(2)]
    pq = ph.ps([128, 2048], F32)
    pz = ph.ps([128, 2, 128], BF16)
    px = ph.ps([128, 512], F32)
    for k in range(8):
        P.dma("pool", wq[:, k, :], W["w_in"][k * 128:(k + 1) * 128, 0:1568], writes=["wq%d" % k], key="wq")
    WQ = ["wq%d" % k for k in range(8)]
    P.dma("pool", w2[:, 0, :], W["gla_w2f"], writes=["w2f"], key="w2")
    P.dma("pool", w2[:, 1, :], W["gla_w2b"], writes=["w2b"], key="w2")
    P.dma("sp", b2[:, 0:256], bcast_rows(W["gla_b2f"], 128), writes=["b2f"], key="b2")
    P.dma("sp", b2[:, 256:512], bcast_rows(W["gla_b2b"], 128), writes=["b2b"], key="b2")
    P.dma("sp", idt[:], CN["ident"], writes=["idt"], key="idt")
    P.op("pool", lambda e: e.memset(one1[:], 1.0), writes=["one1"])
    def g0_load(ti):
        P.dma("sp", hTt[ti % 2][:], h1T[:, :, base + ti * 512: base + (ti + 1) * 512], writes=["hTt%d" % (ti % 2)], key="hTt%d" % (ti % 2))
    g0_load(0)
    for ti in range(T // 512):
        hc = hTt[ti % 2]
        hn = "hTt%d" % (ti % 2)
        if ti + 1 < T // 512:
            g0_load(ti + 1)
        for s in range(4):
            ci = ti * 4 + s
            b = ci % 2
            r0 = base + ci * 128

            def proj(e, s=s, hc=hc):
                for (c0, c1) in ((0, 512), (512, 1024), (1024, 1536), (1536, 1568)):
                    for k in range(8):
                        ins = e.matmul(pq[:, c0:c1], hc[:, k, s * 128:(s + 1) * 128], wq[:, k, c0:c1], start=(k == 0), stop=(k == 7))
                return ins
            P.op("pe", proj, reads=[hn] + WQ, writes=["pq"])
            P.op("dve", lambda e: e.tensor_copy(zb16[:], pq[:, 1536:1568]), reads=["pq"], writes=["zb16"])
            P.op("act", lambda e, b=b: e.copy(gpb[b][:], pq[:, 0:1536]), reads=["pq"], writes=["gpbA%d" % b])

            def tz(e):
                e.transpose(pz[0:16, 0, :], zb16[:, 0:16], idt[:])
                return e.transpose(pz[0:16, 1, :], zb16[:, 16:32], idt[:])
            P.op("pe", tz, reads=["zb16", "idt"], writes=["pz"])
            P.op("dve", lambda e: e.tensor_copy(zT[:], pz[0:16, :, :]), reads=["pz"], writes=["zT"])

            def xm(e):
                e.matmul(px[:, 0:256], zT[:, 0, :], w2[:, 0, :], start=True, stop=True)
                return e.matmul(px[:, 256:512], zT[:, 1, :], w2[:, 1, :], start=True, stop=True)
            P.op("pe", xm, reads=["zT", "w2f", "w2b"], writes=["px"])
            P.op("dve", lambda e: e.tensor_tensor(xs[:], px[:], b2[:], ALU.add), reads=["px", "b2f", "b2b"], writes=["xs"])
            P.op("act", lambda e: e.activation(e1[:], xs[:], AF.Exp, scale=-1.0), reads=["xs"], writes=["e1"])
            P.op("act", lambda e, b=b: e.activation(thb[:], gpb[b][:, 1024:1536], AF.Tanh, scale=0.5), reads=["gpbA%d" % b], writes=["thb"])
            P.op("act", lambda e, b=b: e.activation(la[b][:], e1[:], AF.Ln, bias=one1[:, 0:1]), reads=["e1", "one1"], writes=["la%d" % b])
            P.op("dve", lambda e: e.tensor_scalar(thb[:], thb[:], 0.5, 0.5, ALU.mult, ALU.add), reads=["thb"], writes=["thb"])
            P.op("dve", lambda e, b=b: e.tensor_tensor(gpb[b][:, 1024:1536], thb[:], gpb[b][:, 1024:1536], ALU.mult),
                 reads=["thb", "gpbA%d" % b], writes=["gpb%d" % b])
            P.dma("sp", gp_d[r0:r0 + 128, :], gpb[b][:], reads=["gpb%d" % b, "gpbA%d" % b], writes=["gp%d" % ci], key="gpb%d" % b)
            P.dma("sp", la_d[r0:r0 + 128, :], la[b][:], reads=["la%d" % b], writes=["lad%d" % ci], key="la%d" % b)
    ph.close()
    for dirn in range(2):
        ph = Phase(nc, P)
        idt = ph.sb([128, 128], BF16)
        Mc = ph.sb([128, 3, 128], BF16)
        Ind = ph.sb([128, 16], BF16)
        lhi = ph.sb([128, 256], BF16)
        llo = ph.sb([128, 256], BF16)
        lres = ph.sb([128, 256], F32)
        mskT = ph.sb([128, 128], F32)
        lq = ph.sb([128, 1], F32)
        gng = ph.sb([128, 4, 128], F32)
        gpt = [ph.sb([128, 1536], BF16) for _ in range(3)]
        lat = [ph.sb([128, 256], F32) for _ in range(2)]
        E = ph.sb([128, 4, 256], F32)
        dec = [ph.sb([64, 4, 16], F32) for _ in range(2)]
        prod = [ph.sb([128, 4, 256], BF16) for _ in range(2)]
        qdT = ph.sb([64, 4, 128], BF16)
        kdT = ph.sb([64, 4, 128], BF16)
        qeT = [ph.sb([64, 4, 2, 128], BF16) for _ in range(2)]
        scT = [ph.sb([128, 4, 128], BF16) for _ in range(2)]
        S = ph.sb([64, 4, 128], F32)
        Sb = [ph.sb([64, 4, 128], BF16) for _ in range(2)]
        of32 = [ph.sb([128, 512], F32) for _ in range(3)]
        pc = ph.ps([128, 1024], F32)
        pT = ph.ps([128, 3, 4, 128], F32)
        pSc = ph.ps([128, 4, 128], F32)
        po = ph.ps([128, 4, 128], F32)
        pU = ph.ps([128, 4, 128], F32)
        if dirn == 1:
            sq = ph.sb([128, 512], F32)
            ss = ph.sb([128, 16], F32)
            er = ph.sb([128, 512], F32)
            ob16 = [ph.sb([128, 512], BF16) for _ in range(2)]
            mT = [ph.sb([128, 4, 128], BF16) for _ in range(2)]
            for h in range(4):
                P.dma("sp", gng[:, h, :], bcast_rows(W["gla_gn_g"], 128), writes=["gng%d" % h], key="gng")
        GNG = ["gng%d" % h for h in range(4)]
        P.dma("sp", idt[:], CN["ident"], writes=["idt"], key="idt")
        for m in range(3):
            P.dma("pool", Mc[:, m, :], CN["glaM"][dirn, m], writes=["Mc%d" % m], key="Mc")
        P.dma("pool", Ind[:], CN["glaInd"], writes=["Ind"], key="Ind")
        P.dma("sp", mskT[:], CN["glaMask"][dirn], writes=["mskT"], key="mskT")
        Q = [P]
        P.op("pool", lambda e: e.memset(lq[:], math.log(0.125)), writes=["lq"])
        P.op("pool", lambda e: e.memset(qeT[0][:], 0.0), writes=["qeT0"])
        P.op("pool", lambda e: e.memset(qeT[1][:], 0.0), writes=["qeT1"])
        P.op("pool", lambda e: e.memset(S[:], 0.0), writes=["S"])
        P.op("pool", lambda e: e.memset(Sb[0][:], 0.0), writes=["Sb0"])
        NOUT = NCH // 2 if half else NCH
        order = list(range(NOUT)) if dirn == 0 else list(range(NCH - 1, -1, -1))
        NU = len(order)
        c0, c1 = (0, 1) if dirn == 0 else (1, 0)

        def prep(n):
            ci = order[n]
            b = n % 2
            b3 = n % 3
            r0 = base + ci * 128
            gt = gpt[b3]
            gtn = "gpt%d" % b3
            Q[0].dma("sp", gt[:], gp_d[r0:r0 + 128, :], writes=[gtn], key=gtn)
            Q[0].dma("sp", lat[b][:], la_d[r0:r0 + 128, dirn * 256:(dirn + 1) * 256], writes=["lat%d" % b], key="lat%d" % b)
            if dirn == 1 and ci < NOUT:
                Q[0].dma("sp", of32[b3][:], of_d[r0:r0 + 128, :], writes=["of32_%d" % b3], key="ofl%d" % b3)
            Q[0].op("dve", lambda e: e.tensor_copy(lhi[:], lat[b][:]), reads=["lat%d" % b], writes=["lhi"])
            Q[0].op("dve", lambda e: e.tensor_tensor(lres[:], lat[b][:], lhi[:], ALU.subtract), reads=["lat%d" % b, "lhi"], writes=["lres"])
            Q[0].op("dve", lambda e: e.tensor_copy(llo[:], lres[:]), reads=["lres"], writes=["llo"])

            def cm(e):
                for m in range(3):
                    e.matmul(pc[:, m * 256:(m + 1) * 256], Mc[:, m, :], lhi[:], start=True, stop=False)
                    ins = e.matmul(pc[:, m * 256:(m + 1) * 256], Mc[:, m, :], llo[:], start=False, stop=True)
                return ins
            Q[0].op("pe", cm, reads=["lhi", "llo", "Mc0", "Mc1", "Mc2"], writes=["pc"])

            def dl(e):
                for h in range(4):
                    e.matmul(pc[0:64, 768 + h * 16:768 + (h + 1) * 16], lhi[:, h * 64:(h + 1) * 64], Ind[:], start=True, stop=False)
                    ins = e.matmul(pc[0:64, 768 + h * 16:768 + (h + 1) * 16], llo[:, h * 64:(h + 1) * 64], Ind[:], start=False, stop=True)
                return ins
            Q[0].op("pe", dl, reads=["lhi", "llo", "Ind"], writes=["pdl"])
            Q[0].op("act", lambda e: e.activation(E[:, 0, :], pc[:, 0:256], AF.Exp, bias=lq[:, 0:1]), reads=["pc", "pdl", "lq"], writes=["E0"])
            Q[0].op("act", lambda e: e.activation(E[:, 1, :], pc[:, 0:256], AF.Exp, scale=-1.0), reads=["pc", "pdl"], writes=["E1"])
            Q[0].op("act", lambda e: e.activation(E[:, 2, :], pc[:, 512:768], AF.Exp, bias=lq[:, 0:1]), reads=["pc", "pdl", "lq"], writes=["E2"])
            Q[0].op("act", lambda e: e.activation(E[:, 3, :], pc[:, 256:512], AF.Exp), reads=["pc", "pdl"], writes=["E3"])
            Q[0].op("act", lambda e: e.activation(dec[b][:].rearrange("p h c -> p (h c)"), pc[0:64, 768:832], AF.Exp), reads=["pdl", "pc"], writes=["dec%d" % b])
            pr = prod[b]
            Q[0].op("dve", lambda e: e.tensor_tensor(pr[:, 0, :], gt[:, 0:256], E[:, 0, :], ALU.mult), reads=[gtn, "E0"], writes=["qd%d" % b])
            Q[0].op("dve", lambda e: e.tensor_tensor(pr[:, 1, :], gt[:, 256:512], E[:, 1, :], ALU.mult), reads=[gtn, "E1"], writes=["kd%d" % b])
            Q[0].op("dve", lambda e: e.tensor_tensor(pr[:, 2, :], gt[:, 0:256], E[:, 2, :], ALU.mult), reads=[gtn, "E2"], writes=["qe%d" % b])
            Q[0].op("dve", lambda e: e.tensor_tensor(pr[:, 3, :], gt[:, 256:512], E[:, 3, :], ALU.mult), reads=[gtn, "E3"], writes=["ku%d" % b])

            def tr(e):
                for m in range(3):
                    for h in range(4):
                        ins = e.matmul(pT[0:64, m, h, :], pr[:, m, h * 64:(h + 1) * 64], idt[:], start=True, stop=True)
                return ins
            Q[0].op("pe", tr, reads=["qd%d" % b, "kd%d" % b, "qe%d" % b, "idt"], writes=["pT"])
            Q[0].op("act", lambda e: e.copy(qdT[:], pT[0:64, 0, :, :]), reads=["pT"], writes=["qdT"])
            Q[0].op("act", lambda e: e.copy(kdT[:], pT[0:64, 1, :, :]), reads=["pT"], writes=["kdT"])

            def qecp(e):
                e.copy(qeT[b][:, :, 0, 0:64], pT[0:64, 2, :, 0:64])
                return e.copy(qeT[b][:, :, 1, 64:128], pT[0:64, 2, :, 64:128])
            Q[0].op("act", qecp, reads=["pT"], writes=["qeT%d" % b])

            def sc(e):
                for h in range(4):
                    ins = e.matmul(pSc[:, h, :], kdT[:, h, :], qdT[:, h, :], start=True, stop=True)
                return ins
            Q[0].op("pe", sc, reads=["qdT", "kdT"], writes=["pSc"])
            Q[0].op("dve", lambda e: e.tensor_tensor(scT[b][:], pSc[:], mskT[:].unsqueeze(1).broadcast_to([128, 4, 128]), ALU.mult),
                 reads=["pSc", "mskT"], writes=["scT%d" % b])

        def state(n):
            ci = order[n]
            b = n % 2
            b3 = n % 3
            r0 = base + ci * 128
            gt = gpt[b3]
            gtn = "gpt%d" % b3
            pr = prod[b]

            def upd(c):
                def f(e):
                    for h in range(4):
                        ins = e.matmul(pU[0:64, h, :], pr[c * 64:(c + 1) * 64, 3, h * 64:(h + 1) * 64],
                                       gt[c * 64:(c + 1) * 64, 512 + h * 128:512 + (h + 1) * 128], start=True, stop=True)
                    return ins
                return f

            def supd(c, dst, dstn):
                def sfused(e):
                    for h in range(4):
                        ins = e.scalar_tensor_tensor(S[:, h, :], S[:, h, :], dec[b][:, h, c:c + 1], pU[0:64, h, :], ALU.mult, ALU.add)
                    return ins
                Q[0].op("dve", sfused, reads=["S", "dec%d" % b, "pU"], writes=["S"])
                Q[0].op("act", lambda e: e.copy(dst[:], S[:]), reads=["S"], writes=[dstn])
            Q[0].op("pe", upd(c0), reads=["ku%d" % b, gtn], writes=["pU"])
            supd(c0, Sb[1], "Sb1")
            if ci >= NOUT:
                Q[0].op("pe", upd(c1), reads=["ku%d" % b, gtn], writes=["pU"])
                supd(c1, Sb[0], "Sb0")
                return

            def outm(e):
                for h in range(4):
                    e.matmul(po[:, h, :], scT[b][:, h, :], gt[:, 512 + h * 128:512 + (h + 1) * 128], start=True, stop=False)
                    e.matmul(po[:, h, :], qeT[b][:, h, c0, :], Sb[0][:, h, :], start=False, stop=False)
                    ins = e.matmul(po[:, h, :], qeT[b][:, h, c1, :], Sb[1][:, h, :], start=False, stop=True)
                return ins
            Q[0].op("pe", outm, reads=["scT%d" % b, gtn, "qeT%d" % b, "Sb0", "Sb1"], writes=["po"])
            Q[0].op("pe", upd(c1), reads=["ku%d" % b, gtn], writes=["pU"])
            supd(c1, Sb[0], "Sb0")
            o = of32[b3]
            on = "of32_%d" % b3
            if dirn == 0:
                Q[0].op("act", lambda e: e.copy(o[:], po[:].rearrange("p h e -> p (h e)")), reads=["po"], writes=[on])
                Q[0].dma("sp", of_d[r0:r0 + 128, :], o[:], reads=[on], writes=["ofd%d" % ci], key="ofs%d" % b3)
            else:
                ov = o[:].rearrange("p (h e) -> p h e", e=128)
                Q[0].op("dve", lambda e: e.tensor_tensor(o[:], o[:], po[:].rearrange("p h e -> p (h e)"), ALU.add), reads=[on, "po"], writes=[on])
                Q[0].op("act", lambda e: e.activation(sq[:], o[:], AF.Square), reads=[on], writes=["sq"])
                Q[0].op("dve", lambda e: e.tensor_reduce(ss[:, 0:4], sq[:].rearrange("p (h e) -> p h e", e=128), AX.X, ALU.add), reads=["sq"], writes=["ss0"])
                Q[0].op("dve", lambda e: e.tensor_scalar(ss[:, 4:8], ss[:, 0:4], 1.0 / 128, 1e-5, ALU.mult, ALU.add), reads=["ss0"], writes=["ss1"])
                Q[0].op("act", lambda e: e.activation(ss[:, 8:12], ss[:, 4:8], AF.Sqrt), reads=["ss1"], writes=["ss2"])
                Q[0].op("dve", lambda e: e.reciprocal(ss[:, 12:16], ss[:, 8:12]), reads=["ss2"], writes=["ss3"])
                Q[0].op("dve", lambda e: e.tensor_tensor(ov, ov, ss[:, 12:16].unsqueeze(2).broadcast_to([128, 4, 128]), ALU.mult), reads=[on, "ss3"], writes=[on])
                Q[0].op("pool", lambda e: e.tensor_tensor(o[:], o[:], gng[:].rearrange("p h e -> p (h e)"), ALU.mult), reads=[on] + GNG, writes=[on])
                Q[0].op("dve", lambda e: e.tensor_tensor(ob16[b][:], o[:], gt[:, 1024:1536], ALU.mult), reads=[on, gtn], writes=["ob16_%d" % b])

        def tail(n):
            if dirn == 0 or order[n] >= NOUT:
                return
            ci = order[n]
            b = n % 2
            r0 = base + ci * 128

            def to(e):
                for c in range(4):
                    ins = e.matmul(pT[:, 0, c, :], ob16[b][:, c * 128:(c + 1) * 128], idt[:], start=True, stop=True)
                return ins
            Q[0].op("pe", to, reads=["ob16_%d" % b, "idt"], writes=["pT"])
            Q[0].op("act", lambda e: e.copy(mT[b][:], pT[:, 0, :, :]), reads=["pT"], writes=["mT%d" % b])
            Q[0].dma("sp", mixG[:, :, r0:r0 + 128], mT[b][:], reads=["mT%d" % b], writes=["mixG%d" % ci], key="mT%d" % b)

        def rec(fn, n):
            r = _Rec()
            Q[0] = r
            fn(n)
            Q[0] = P
            return r.calls

        def interleave(A, B):
            out, ia, ib, na, nb = [], 0, 0, len(A), len(B)
            while ia < na or ib < nb:
                if ib >= nb or (ia < na and ia * nb <= ib * na):
                    out.append(A[ia])
                    ia += 1
                else:
                    out.append(B[ib])
                    ib += 1
            return out

        preps = [rec(prep, n) for n in range(NU)]
        loads = [[c for c in calls if c[0] == "dma"] for calls in preps]
        comps = [[c for c in calls if c[0] != "dma"] for calls in preps]
        _replay(P, loads[0])
        if NU > 1:
            _replay(P, loads[1])
        _replay(P, comps[0])
        for n in range(NU):
            if n + 2 < NU:
                _replay(P, loads[n + 2])
            _replay(P, interleave(comps[n + 1] if n + 1 < NU else [], rec(state, n)))
            if n >= 1:
                tail(n - 1)
        tail(NU - 1)
        ph.close()


def wout_phase(nc, P, NT, W, CN, SC, h1, h2, h2T):
    mixG, mixA = SC["mixG"], SC["mixA"]
    ph = Phase(nc, P)
    wog = ph.sb([128, 4, D], BF16)
    woa = ph.sb([128, 4, D], BF16)
    lng = ph.sb([128, D], F32)
    lnb = ph.sb([128, D], F32)
    idt = ph.sb([128, 128], BF16)
    mG = [ph.sb([128, 4, 512], BF16) for _ in range(2)]
    mA = [ph.sb([128, 4, 512], BF16) for _ in range(2)]
    x32 = [ph.sb([128, D], F32) for _ in range(3)]
    z = [ph.sb([128, D], F32) for _ in range(2)]
    hb = [ph.sb([128, D], BF16) for _ in range(3)]
    hTt = [ph.sb([128, 8, 512], BF16) for _ in range(2)]
    stats = ph.sb([128, 2, 6], F32)
    mv = ph.sb([128, 8], F32)
    py = [ph.ps([128, 1024], F32) for _ in range(2)]
    pT = [ph.ps([128, 8, 128], BF16) for _ in range(2)]
    for c in range(4):
        P.dma("pool", wog[:, c, :], W["w_out"][c * 128:(c + 1) * 128, :], writes=["wog%d" % c], key="wog")
    for c in range(4):
        P.dma("pool", woa[:, c, :], W["w_out"][512 + c * 128:512 + (c + 1) * 128, :], writes=["woa%d" % c], key="woa")
    WO = ["wog%d" % c for c in range(4)] + ["woa%d" % c for c in range(4)]
    P.dma("sp", lng[:], bcast_rows(W["ln2_g"], 128), writes=["lng"], key="lng")
    P.dma("sp", lnb[:], bcast_rows(W["ln2_b"], 128), writes=["lnb"], key="lnb")
    P.dma("sp", idt[:], CN["ident"], writes=["idt"], key="idt")
    def load_tile(ti):
        t0 = ti * 512
        tb = ti % 2
        P.dma("sp", mG[tb][:], mixG[:, :, t0:t0 + 512], writes=["mG%d" % tb], key="mG%d" % tb)
        P.dma("sp", mA[tb][:], mixA[:, :, t0:t0 + 512], writes=["mA%d" % tb], key="mA%d" % tb)

    def main(ti, s):
        t0 = ti * 512
        tb = ti % 2
        gi = ti * 4 + s
        q = gi % 2
        r0 = t0 + s * 128

        def om(e):
            for hf in range(2):
                cs = slice(hf * 512, (hf + 1) * 512)
                for c in range(4):
                    e.matmul(py[q][:, cs], mG[tb][:, c, s * 128:(s + 1) * 128], wog[:, c, cs], start=(c == 0), stop=False)
                for c in range(4):
                    ins = e.matmul(py[q][:, cs], mA[tb][:, c, s * 128:(s + 1) * 128], woa[:, c, cs], start=False, stop=(c == 3))
            return ins
        P.op("pe", om, reads=["mG%d" % tb, "mA%d" % tb] + WO, writes=["py%d" % q])
        q3 = gi % 3
        xq = x32[q3]
        P.op("dve", lambda e: e.scalar_tensor_tensor(z[q][:], py[q][:], 1.0 / ALPHA, xq[:], ALU.mult, ALU.add),
             reads=["py%d" % q, "x32_%d" % q3], writes=["z%d" % q])
        layer_norm(P, z[q], "z%d" % q, stats, mv, lng, lnb, LN_EPS / ALPHA ** 2, xq, "x32_%d" % q3)
        P.dma("sp", h2[r0:r0 + 128, :], xq[:], reads=["x32_%d" % q3], writes=["h2_%d" % gi], key="st32_%d" % q3)
        P.op("act", lambda e: e.copy(hb[q3][:], xq[:]), reads=["x32_%d" % q3], writes=["hb%d" % q3])

    def load_x(ti, s):
        gi = ti * 4 + s
        r0 = ti * 512 + s * 128
        P.dma("sp", x32[gi % 3][:], h1[r0:r0 + 128, :], writes=["x32_%d" % (gi % 3)], key="x32_%d" % (gi % 3))

    def htrans(ti, s):
        tb = ti % 2
        q = (ti * 4 + s) % 2
        q3 = (ti * 4 + s) % 3
        for k in range(8):
            P.op("pe", lambda e, k=k: e.transpose(pT[q][:, k, :], hb[q3][:, k * 128:(k + 1) * 128], idt[:]),
                 reads=["hb%d" % q3, "idt"], writes=["pT%d_%d" % (q, k)])
        P.op("act", lambda e: e.copy(hTt[tb][:, :, s * 128:(s + 1) * 128], pT[q][:]),
             reads=["pT%d_%d" % (q, k) for k in range(8)], writes=["hTt%d_%d" % (tb, s)])
        if s == 3:
            P.dma("sp", h2T[:, :, ti * 512:(ti + 1) * 512], hTt[tb][:], reads=["hTt%d_%d" % (tb, ss) for ss in range(4)],
                  writes=["h2T_%d" % ti], key="stT_%d" % tb)

    steps = [(ti, s) for ti in range(NT // 512) for s in range(4)]
    load_tile(0)
    load_x(*steps[0])
    for idx, (ti, s) in enumerate(steps):
        if s == 0 and ti + 1 < NT // 512:
            load_tile(ti + 1)
        if idx + 1 < len(steps):
            load_x(*steps[idx + 1])
        main(ti, s)
        if idx >= 2:
            htrans(*steps[idx - 2])
    htrans(*steps[-2])
    htrans(*steps[-1])
    ph.close()


def gate_phase(nc, P, NT, W, CN, p_d, h3, h3T, y):
    ph = Phase(nc, P)
    wpg = ph.sb([128, 8, D], BF16)
    wpe = ph.sb([128, 2, D], BF16)
    bpg = ph.sb([128, D], F32)
    idt = ph.sb([128, 128], BF16)
    hTt = [ph.sb([128, 8, 512], BF16) for _ in range(2)]
    pb = [ph.sb([128, 4, PD], BF16) for _ in range(2)]
    ppT = [ph.sb([128, 2, 128], BF16) for _ in range(2)]
    x32 = [ph.sb([128, D], F32) for _ in range(3)]
    gs = [ph.sb([128, D], F32) for _ in range(2)]
    pes = [ph.sb([128, D], F32) for _ in range(2)]
    pgt2 = [ph.ps([128, 1024], F32) for _ in range(2)]
    ppe = ph.ps([128, 1024], F32)
    pT = [ph.ps([128, 2, 128], BF16) for _ in range(2)]
    for k in range(8):
        P.dma("pool", wpg[:, k, :], W["w_pg"][k * 128:(k + 1) * 128, :], writes=["wpg%d" % k], key="wpg")
    for k in range(2):
        P.dma("pool", wpe[:, k, :], W["w_pe"][k * 128:(k + 1) * 128, :], writes=["wpe%d" % k], key="wpe")
    WPG = ["wpg%d" % k for k in range(8)]
    WPE = ["wpe%d" % k for k in range(2)]
    P.dma("sp", bpg[:], bcast_rows(W["b_pg"], 128), writes=["bpg"], key="bpg")
    P.dma("sp", idt[:], CN["ident"], writes=["idt"], key="idt")
    def load_tile(ti):
        tb = ti % 2
        P.dma("sp", hTt[tb][:], h3T[:, :, ti * 512:(ti + 1) * 512], writes=["hTt%d" % tb], key="hTt%d" % tb)
        P.dma("pool", pb[tb][:], p_d[ti * 512:(ti + 1) * 512, :].rearrange("(s p) f -> p s f", p=128), writes=["pb%d" % tb], key="pb%d" % tb)

    def ptrans(ti, s):
        q = (ti * 4 + s) % 2
        tb = ti % 2

        def tp(e):
            for c in range(2):
                ins = e.transpose(pT[q][:, c, :], pb[tb][:, s, c * 128:(c + 1) * 128], idt[:])
            return ins
        P.op("pe", tp, reads=["pb%d" % tb, "idt"], writes=["pT%d" % q])
        P.op("act", lambda e: e.copy(ppT[q][:], pT[q][:]), reads=["pT%d" % q], writes=["ppT%d" % q])

    def main(ti, s):
        tb = ti % 2
        gi = ti * 4 + s
        q = gi % 2
        r0 = ti * 512 + s * 128
        pgt = pgt2[q]

        def gm(e):
            for hf in range(2):
                cs = slice(hf * 512, (hf + 1) * 512)
                for k in range(8):
                    ins = e.matmul(pgt[:, cs], hTt[tb][:, k, s * 128:(s + 1) * 128], wpg[:, k, cs], start=(k == 0), stop=(k == 7))
            return ins
        P.op("pe", gm, reads=["hTt%d" % tb] + WPG, writes=["pgt%d" % q])

        def pm(e):
            for hf in range(2):
                cs = slice(hf * 512, (hf + 1) * 512)
                for c in range(2):
                    ins = e.matmul(ppe[:, cs], ppT[q][:, c, :], wpe[:, c, cs], start=(c == 0), stop=(c == 1))
            return ins
        P.op("pe", pm, reads=["ppT%d" % q] + WPE, writes=["ppe"])
        P.op("act", lambda e: e.copy(pes[q][:], ppe[:]), reads=["ppe"], writes=["pes%d" % q])
        g = gs[q]
        gn = "gs%d" % q
        P.op("dve", lambda e: e.tensor_tensor(g[:], pgt[:], bpg[:], ALU.add), reads=["pgt%d" % q, "bpg"], writes=[gn])
        P.op("act", lambda e: e.activation(g[:], g[:], AF.Tanh, scale=0.5), reads=[gn], writes=[gn])
        P.op("dve", lambda e: e.tensor_scalar(g[:], g[:], 0.5, 0.5, ALU.mult, ALU.add), reads=[gn], writes=[gn])
        P.op("dve", lambda e: e.tensor_tensor(g[:], g[:], pes[q][:], ALU.mult), reads=[gn, "pes%d" % q], writes=[gn])
        P.op("pool", lambda e: e.tensor_tensor(g[:], g[:], x32[gi % 3][:], ALU.add), reads=[gn, "x32_%d" % (gi % 3)], writes=[gn])
        P.dma("sp", y[r0:r0 + 128, :], g[:], reads=[gn], writes=["y%d" % gi], key="sty%d" % q)

    def load_x(ti, s):
        gi = ti * 4 + s
        r0 = ti * 512 + s * 128
        P.dma("sp", x32[gi % 3][:], h3[r0:r0 + 128, :], writes=["x32_%d" % (gi % 3)], key="x32_%d" % (gi % 3))

    steps = [(ti, s) for ti in range(NT // 512) for s in range(4)]
    load_tile(0)
    load_x(*steps[0])
    ptrans(*steps[0])
    for idx, (ti, s) in enumerate(steps):
        if s == 0 and ti + 1 < NT // 512:
            load_tile(ti + 1)
        if idx + 1 < len(steps):
            load_x(*steps[idx + 1])
            ptrans(*steps[idx + 1])
        main(ti, s)
    ph.close()


W_NAMES = ["ffn1_wg", "ffn1_wu", "ffn1_wd", "ln1_g", "ln1_b", "w_in", "gla_w2f", "gla_b2f", "gla_w2b", "gla_b2b",
           "gla_gn_g", "q_norm_g", "k_norm_g", "w_out", "ln2_g", "ln2_b", "ffn2_wg", "ffn2_wu", "ffn2_wd",
           "ln3_g", "ln3_b", "w_pg", "b_pg", "w_pe"]
W_SHAPES = {"ffn1_wg": [D, DFF], "ffn1_wu": [D, DFF], "ffn1_wd": [DFF, D], "ln1_g": [D], "ln1_b": [D], "w_in": [D, DIN],
            "gla_w2f": [16, 256], "gla_b2f": [256], "gla_w2b": [16, 256], "gla_b2b": [256], "gla_gn_g": [128],
            "q_norm_g": [64], "k_norm_g": [64], "w_out": [D, D], "ln2_g": [D], "ln2_b": [D], "ffn2_wg": [D, DFF],
            "ffn2_wu": [D, DFF], "ffn2_wd": [DFF, D], "ln3_g": [D], "ln3_b": [D], "w_pg": [D, D], "b_pg": [D], "w_pe": [PD, D]}


def build(T, NSEQ, upto=99, debug=False, halfB=False):
    NT = T * NSEQ
    NTE = NT - T // 2 if halfB else NT
    nc = bass.Bass("TRN2", target_bir_lowering=False)
    x = nc.dram_tensor("x", [NT, D], F32, kind="ExternalInput").ap()
    p = nc.dram_tensor("p", [NT, PD], F32, kind="ExternalInput").ap()
    W = {n: nc.dram_tensor(n, W_SHAPES[n], F32, kind="ExternalInput").ap() for n in W_NAMES}
    CN = {"ident": nc.dram_tensor("ident", [128, 128], BF16, kind="ExternalInput").ap(),
          "ropeC": nc.dram_tensor("ropeC", [T, 64], F32, kind="ExternalInput").ap(),
          "ropeS": nc.dram_tensor("ropeS", [T, 64], F32, kind="ExternalInput").ap(),
          "glaM": nc.dram_tensor("glaM", [2, 3, 128, 128], F32, kind="ExternalInput").ap(),
          "glaInd": nc.dram_tensor("glaInd", [128, 16], F32, kind="ExternalInput").ap(),
          "glaMask": nc.dram_tensor("glaMask", [2, 128, 128], F32, kind="ExternalInput").ap()}
    y = nc.dram_tensor("y", [NTE, D], F32, kind="ExternalOutput").ap()
    sk = "ExternalOutput" if debug else "Internal"
    dt = lambda n, sh, ty: nc.dram_tensor(n, sh, ty, kind=sk).ap()
    h1 = dt("h1", [NT, D], F32)
    h1T = dt("h1T", [128, 8, NT], BF16)
    SC = {"mixA": dt("mixA", [128, 4, NT], BF16), "mixG": dt("mixG", [128, 4, NT], BF16),
          "gp": dt("gp", [NT, 1536], BF16), "la": dt("la", [NT, 512], F32), "of": dt("of", [NT, 512], F32)}
    h2 = dt("h2", [NT, D], F32)
    h2T = dt("h2T", [128, 8, NT], BF16)
    h3 = dt("h3", [NT, D], F32)
    h3T = dt("h3T", [128, 8, NT], BF16)
    with contextlib.ExitStack() as stack:
        P = Prog(nc, stack)
        ffn_phase(nc, P, "f1", NT, x, None, W["ffn1_wg"], W["ffn1_wu"], W["ffn1_wd"], W["ln1_g"], W["ln1_b"], CN["ident"],
                  h1, h1T, 0.5 / ALPHA, LN_EPS / ALPHA ** 2)
        for seq in range(NSEQ):
            hb = halfB and seq == NSEQ - 1
            if upto >= 2:
                attn_phase(nc, P, T, seq, h1T, W, CN, SC["mixA"], nqb=(T // 1024 if hb else None))
            if upto >= 3:
                gla_phase(nc, P, T, seq, h1T, W, CN, SC, half=hb)
        NT = NTE
        if upto >= 4:
            wout_phase(nc, P, NT, W, CN, SC, h1, h2, h2T)
        if upto >= 5:
            ffn_phase(nc, P, "f2", NT, h2, h2T, W["ffn2_wg"], W["ffn2_wu"], W["ffn2_wd"], W["ln3_g"], W["ln3_b"], CN["ident"],
                      h3, h3T, 0.5 / ALPHA, LN_EPS / ALPHA ** 2)
            gate_phase(nc, P, NT, W, CN, p, h3, h3T, y)
    return nc


def gla_consts():
    t = np.arange(128)
    a = t % 64
    c = t // 64
    same = (c[:, None] == c[None, :]).astype(np.float32)
    aj, ai = a[:, None], a[None, :]
    M = np.zeros((2, 3, 128, 128), np.float32)
    M[0, 0] = same * ((aj <= ai).astype(np.float32) - (aj <= 31).astype(np.float32))
    M[0, 1] = same * (aj > ai)
    M[0, 2] = same * (aj <= ai)
    M[1, 0] = same * ((aj >= ai).astype(np.float32) - (aj >= 32).astype(np.float32))
    M[1, 1] = same * (aj < ai)
    M[1, 2] = same * (aj >= ai)
    M *= -1.0 / 16
    Ind = np.zeros((128, 16), np.float32)
    Ind[t, c] = -1.0 / 16
    mask = np.zeros((2, 128, 128), np.float32)
    mask[0] = same * (aj <= ai)
    mask[1] = same * (aj > ai)
    return M, Ind, mask


def rope_tables(T):
    t = np.arange(T)
    row = (t // 64).astype(np.float32)
    col = (t % 64).astype(np.float32)
    inv = (10000.0 ** (-np.arange(0, 32, 2, dtype=np.float32) / 32)).astype(np.float32)
    ar = row[:, None] * inv[None, :]
    ac = col[:, None] * inv[None, :]
    C = np.concatenate([np.cos(ar), np.cos(ar), np.cos(ac), np.cos(ac)], axis=1).astype(np.float32)
    S = np.concatenate([-np.sin(ar), np.sin(ar), -np.sin(ac), np.sin(ac)], axis=1).astype(np.float32)
    return C, S


def consts(T, flip=False):
    C, S = rope_tables(T)
    M, Ind, mask = gla_consts()
    if flip:
        C, S = C[::-1].copy(), S[::-1].copy()
        t = np.arange(128)
        a = t % 64
        same = ((t // 64)[:, None] == (t // 64)[None, :]).astype(np.float32)
        mask = np.stack([same * (a[:, None] < a[None, :]), same * (a[:, None] >= a[None, :])]).astype(np.float32)
    return {"ident": np.eye(128, dtype=np.float32).astype(ml_dtypes.bfloat16), "ropeC": C, "ropeS": S,
            "glaM": M, "glaInd": Ind, "glaMask": mask}


_CACHE = {}


def kernel(**inputs):
    T, NSEQ, NCORE = 4096, 2, 8
    H = T // 2
    xp = np.asarray(inputs["x_prompt"], np.float32)
    xs = np.asarray(inputs["x_sample"], np.float32)
    pp = np.asarray(inputs["p_prompt"], np.float32)[0]
    ps = np.asarray(inputs["p_sample"], np.float32)[0]
    Wn = {n: np.ascontiguousarray(np.asarray(inputs[n], np.float32)[0]) for n in W_NAMES}
    Wf = dict(Wn)
    Wf["gla_w2f"], Wf["gla_w2b"] = Wn["gla_w2b"], Wn["gla_w2f"]
    Wf["gla_b2f"], Wf["gla_b2b"] = Wn["gla_b2b"], Wn["gla_b2f"]
    wi = Wn["w_in"].copy()
    wi[:, 1536:1552], wi[:, 1552:1568] = Wn["w_in"][:, 1552:1568], Wn["w_in"][:, 1536:1552]
    Wf["w_in"] = wi
    cn = [consts(T, False), consts(T, True)]
    if "nc" not in _CACHE:
        _CACHE["nc"] = build(T, NSEQ, halfB=True)
    nc = _CACHE["nc"]
    in_maps = []
    for c in range(NCORE):
        flip = c >= 4
        xa, pa, xb, pb = xp[c], pp[c], xs[c % 4], ps[c % 4]
        if flip:
            xa, pa, xb, pb = xa[::-1], pa[::-1], xb[::-1], pb[::-1]
        m = {"x": np.ascontiguousarray(np.concatenate([xa, xb], 0)),
             "p": np.ascontiguousarray(np.concatenate([pa, pb], 0))}
        m.update(Wf if flip else Wn)
        m.update(cn[1] if flip else cn[0])
        in_maps.append(m)
    res = run_bass_kernel_spmd(nc, in_maps, core_ids=list(range(NCORE)))
    ys = [np.asarray(res.results[c]["y"], np.float32) for c in range(NCORE)]
    yp = np.stack([ys[c][:T] if c < 4 else ys[c][:T][::-1] for c in range(8)], 0)
    ysm = np.stack([np.concatenate([ys[j][T:T + H], ys[j + 4][T:T + H][::-1]], 0) for j in range(4)], 0)
    return (np.ascontiguousarray(yp), np.ascontiguousarray(ysm))
```

```python
import contextlib
import math
import numpy as np
import ml_dtypes
import concourse.bass as bass
import concourse.mybir as mybir
from concourse.bass_utils import run_bass_kernel_spmd

F32 = mybir.dt.float32
BF16 = mybir.dt.bfloat16
AF = mybir.ActivationFunctionType
ALU = mybir.AluOpType
AX = mybir.AxisListType

D = 1024
DFF = 2816
NF = DFF // 128
PD = 256
DIN = 2336
ALPHA = 2.0 ** 0.25
LN_EPS = 1e-5
ENGS = ("pe", "act", "dve", "pool", "sp")


class _Op:
    __slots__ = ("eng", "fn", "deps", "is_dma", "key", "signal", "seq", "waits", "idx")

    def __init__(self, eng, fn, is_dma=False, key=None):
        self.eng = eng
        self.fn = fn
        self.deps = set()
        self.is_dma = is_dma
        self.key = key
        self.signal = False
        self.seq = None
        self.waits = None


class Prog:
    NSLOT = 40
    NPOOL = 10

    def __init__(self, nc, stack):
        self.nc = nc
        self.sem_eng = {e: stack.enter_context(nc.semaphore("s_" + e)) for e in ENGS}
        self.sem_slot = [stack.enter_context(nc.semaphore("d_%d" % i)) for i in range(self.NSLOT)]
        self.eng_cnt = {e: 0 for e in ENGS}
        self.slot_cnt = [0] * self.NSLOT
        self.known = {e: {} for e in ENGS}
        self.reset()

    def reset(self):
        self.ops = []
        self.last_w = {}
        self.readers = {}

    def _add(self, op, reads, writes):
        idx = len(self.ops)
        op.idx = idx
        for r in reads:
            w = self.last_w.get(r)
            if w is not None:
                op.deps.add(w)
        for t in writes:
            w = self.last_w.get(t)
            if w is not None:
                op.deps.add(w)
            for rd in self.readers.get(t, ()):
                op.deps.add(rd)
        op.deps.discard(idx)
        for r in reads:
            self.readers.setdefault(r, []).append(idx)
        for t in writes:
            self.last_w[t] = idx
            self.readers[t] = []
        self.ops.append(op)
        return idx

    def op(self, eng, fn, reads=(), writes=()):
        return self._add(_Op(eng, fn), reads, writes)

    def dma(self, queue, out, in_, reads=(), writes=(), key=None, **kw):
        assert key is not None
        o = _Op(queue, lambda e: e.dma_start(out=out, in_=in_, **kw), is_dma=True, key=key)
        o.signal = True
        return self._add(o, reads, writes)

    def lower(self):
        nc = self.nc
        ops = self.ops
        for o in ops:
            for d in o.deps:
                p = ops[d]
                if p.eng == "pe" and o.eng == "pe" and not p.is_dma and not o.is_dma:
                    continue
                p.signal = True
        streams = {e: [] for e in ENGS}
        for o in ops:
            streams[o.eng].append(o)
        for e in ENGS:
            lst = [o for o in streams[e] if not o.is_dma]
            if lst:
                lst[-1].signal = True
        slot_of = {}
        npool = [0]
        nsp = [0]
        for o in ops:
            if not o.signal:
                continue
            if o.is_dma:
                if o.key not in slot_of:
                    if o.eng == "pool":
                        slot_of[o.key] = npool[0]
                        npool[0] += 1
                        assert npool[0] <= self.NPOOL, "too many pool dma keys"
                    else:
                        slot_of[o.key] = self.NPOOL + nsp[0]
                        nsp[0] += 1
                        assert self.NPOOL + nsp[0] <= self.NSLOT, "too many sp dma keys"
                s = slot_of[o.key]
                self.slot_cnt[s] += 16
                o.seq = self.slot_cnt[s]
            else:
                self.eng_cnt[o.eng] += 1
                o.seq = self.eng_cnt[o.eng]

        def sem_of(p):
            return self.sem_slot[slot_of[p.key]] if p.is_dma else self.sem_eng[p.eng]

        for e in ENGS:
            known = self.known[e]
            for o in streams[e]:
                need = {}
                for d in o.deps:
                    p = ops[d]
                    if p.eng == "pe" and o.eng == "pe" and not p.is_dma and not o.is_dma:
                        continue
                    s = ("k", slot_of[p.key]) if p.is_dma else ("e", p.eng)
                    if p.seq > need.get(s, 0):
                        need[s] = p.seq
                w = []
                for s, v in need.items():
                    if known.get(s, 0) >= v:
                        continue
                    known[s] = v
                    w.append((self.sem_slot[s[1]] if s[0] == "k" else self.sem_eng[s[1]], v))
                o.waits = w
        fin = [(self.sem_slot[i], self.slot_cnt[i]) for i in range(self.NSLOT) if self.slot_cnt[i] > 0]
        fin += [(self.sem_eng[e], self.eng_cnt[e]) for e in ENGS if self.eng_cnt[e] > 0]

        with nc.Block(no_gpsimd_drain=True) as block:
            def run(e, engobj):
                for o in streams[e]:
                    for (s, v) in o.waits:
                        engobj.wait_ge(s, v)
                    ins = o.fn(engobj)
                    if o.signal:
                        ins.then_inc(sem_of(o), 16 if o.is_dma else 1)
                for (s, v) in fin:
                    engobj.wait_ge(s, v)

            @block.sync
            def _(eng):
                run("sp", eng)

            @block.tensor
            def _(eng):
                run("pe", eng)

            @block.scalar
            def _(eng):
                run("act", eng)

            @block.vector
            def _(eng):
                run("dve", eng)

            @block.gpsimd
            def _(eng):
                run("pool", eng)
        self.reset()


_CNT = [0]


class _Rec:
    def __init__(self):
        self.calls = []

    def op(self, *a, **k):
        self.calls.append(("op", a, k))

    def dma(self, *a, **k):
        self.calls.append(("dma", a, k))


def _replay(P, calls):
    for kind, a, k in calls:
        getattr(P, kind)(*a, **k)


class Phase:
    def __init__(self, nc, P):
        self.nc = nc
        self.P = P
        self.es = contextlib.ExitStack()
        self.n = 0

    def sb(self, shape, dt, name=None):
        _CNT[0] += 1
        return self.es.enter_context(self.nc.sbuf_tensor(name or ("t%d" % _CNT[0]), list(shape), dt))

    def ps(self, shape, dt, name=None):
        _CNT[0] += 1
        return self.es.enter_context(self.nc.psum_tensor(name or ("p%d" % _CNT[0]), list(shape), dt))

    def close(self):
        self.P.lower()
        self.es.close()


def bcast_rows(ap_row, n):
    return ap_row.unsqueeze(0).broadcast_to([n, ap_row.shape[0]])


def ffn_phase(nc, P, tag, NT, src32, srcT, wg_d, wu_d, wd_d, g_d, b_d, ident_d, dst32, dstT, cres, eps):
    TT = 256
    NS = TT // 128
    ph = Phase(nc, P)
    wg = ph.sb([128, 8, DFF], BF16)
    wu = ph.sb([128, 8, DFF], BF16)
    wd = ph.sb([128, NF, D], BF16)
    lng = ph.sb([128, D], F32)
    lnb = ph.sb([128, D], F32)
    idt = ph.sb([128, 128], BF16)
    xb = [ph.sb([128, NS, D], BF16) for _ in range(2)] if srcT is None else None
    xT = [ph.sb([128, 8, TT], BF16) for _ in range(2)]
    hT = ph.sb([128, NF, TT], BF16)
    sg = [ph.sb([128, TT], F32) for _ in range(2)]
    x32 = [ph.sb([128, D], F32) for _ in range(2)]
    z = [ph.sb([128, D], F32) for _ in range(2)]
    hb = [ph.sb([128, D], BF16) for _ in range(2)]
    hTt = [ph.sb([128, 8, TT], BF16) for _ in range(2)]
    stats = ph.sb([128, 2, 6], F32)
    mv = ph.sb([128, 8], F32)
    pgu = [ph.ps([128, 512], F32) for _ in range(2)]
    py = [ph.ps([128, 1024], F32) for _ in range(2)]
    pT = [ph.ps([128, 8, 128], BF16) for _ in range(2)]

    HALF = DFF // 2
    ntile = NT // TT
    P.dma("sp", idt[:], ident_d, writes=["idt"], key="idt")
    WGUH = [["wg%d_%d" % (k, h) for k in range(8)] + ["wu%d_%d" % (k, h) for k in range(8)] for h in range(2)]
    WD = ["wd%d" % f for f in range(NF)]

    def load_weights():
        for h in range(2):
            for k in range(8):
                P.dma("pool", wg[:, k, h * HALF:(h + 1) * HALF], wg_d[k * 128:(k + 1) * 128, h * HALF:(h + 1) * HALF],
                      writes=["wg%d_%d" % (k, h)], key="wg%d" % h)
                P.dma("pool", wu[:, k, h * HALF:(h + 1) * HALF], wu_d[k * 128:(k + 1) * 128, h * HALF:(h + 1) * HALF],
                      writes=["wu%d_%d" % (k, h)], key="wu%d" % h)
        for f in range(NF):
            P.dma("pool", wd[:, f, :], wd_d[f * 128:(f + 1) * 128, :], writes=["wd%d" % f], key="wd")
        P.dma("sp", lng[:], bcast_rows(g_d, 128), writes=["lng"], key="lng")
        P.dma("sp", lnb[:], bcast_rows(b_d, 128), writes=["lnb"], key="lnb")

    def load_x(ti):
        t0 = ti * TT
        tb = ti % 2
        if srcT is None:
            for s in range(NS):
                P.dma("pool", xb[tb][:, s, :], src32[t0 + s * 128:t0 + (s + 1) * 128, :], writes=["xb%d_%d" % (tb, s)], key="xb%d_%d" % (tb, s))
        else:
            P.dma("sp", xT[tb][:], srcT[:, :, t0:t0 + TT], writes=["xT%d" % tb], key="xT%d" % tb)

    def xtrans(ti):
        if srcT is not None:
            return
        tb = ti % 2
        for s in range(NS):
            pTc = pT[s % 2]
            for k in range(8):
                P.op("pe", lambda e, s=s, k=k, pTc=pTc: e.transpose(pTc[:, k, :], xb[tb][:, s, k * 128:(k + 1) * 128], idt[:]),
                     reads=["xb%d_%d" % (tb, s), "idt"], writes=["pT%d_%d" % (s % 2, k)])
            P.op("act", lambda e, s=s, pTc=pTc: e.copy(xT[tb][:, :, s * 128:(s + 1) * 128], pTc[:]),
                 reads=["pT%d_%d" % (s % 2, k) for k in range(8)], writes=["xT%d_%d" % (tb, s)])

    def xT_reads(ti):
        tb = ti % 2
        return ["xT%d_%d" % (tb, s) for s in range(NS)] if srcT is None else ["xT%d" % tb]

    def up(ti, f):
        b = f % 2
        xTc = xT[ti % 2]

        def upm(e):
            for k in range(8):
                e.matmul(pgu[b][:, 0:TT], wg[:, k, f * 128:(f + 1) * 128], xTc[:, k, :], start=(k == 0), stop=(k == 7))
            for k in range(8):
                ins = e.matmul(pgu[b][:, TT:2 * TT], wu[:, k, f * 128:(f + 1) * 128], xTc[:, k, :], start=(k == 0), stop=(k == 7))
            return ins
        P.op("pe", upm, reads=xT_reads(ti) + WGUH[0 if f < NF // 2 else 1], writes=["pgu%d" % b])
        P.op("act", lambda e: e.activation(sg[b][:], pgu[b][:, 0:TT], AF.Silu), reads=["pgu%d" % b], writes=["sg%d" % b])
        P.op("dve", lambda e: e.tensor_tensor(hT[:, f, :], sg[b][:], pgu[b][:, TT:2 * TT], ALU.mult),
             reads=["sg%d" % b, "pgu%d" % b], writes=["hT%d" % f])

    def down(ti, s, P=P):
        gi = ti * NS + s
        r0 = ti * TT + s * 128
        q = gi % 2

        def dn(e):
            for h in range(2):
                for f in range(NF):
                    ins = e.matmul(py[q][:, h * 512:(h + 1) * 512], hT[:, f, s * 128:(s + 1) * 128], wd[:, f, h * 512:(h + 1) * 512],
                                   start=(f == 0), stop=(f == NF - 1))
            return ins
        P.op("pe", dn, reads=["hT%d" % f for f in range(NF)] + WD, writes=["py%d" % q])
        P.dma("sp", x32[q][:], src32[r0:r0 + 128, :], writes=["x32_%d" % q], key="x32_%d" % q)
        P.op("dve", lambda e: e.scalar_tensor_tensor(z[q][:], py[q][:], float(cres), x32[q][:], ALU.mult, ALU.add),
             reads=["py%d" % q, "x32_%d" % q], writes=["z%d" % q])
        layer_norm(P, z[q], "z%d" % q, stats, mv, lng, lnb, eps, x32[q], "x32_%d" % q,
                   out_bf=(hb[q] if dstT is not None else None), out_bfn="hb%d" % q)
        P.dma("sp", dst32[r0:r0 + 128, :], x32[q][:], reads=["x32_%d" % q], writes=["dst32_%d" % gi], key="st32_%d" % q)

    def htrans(ti, s):
        if dstT is None:
            return
        gi = ti * NS + s
        q = gi % 2
        tb = ti % 2
        pTc = pT[q]
        for k in range(8):
            P.op("pe", lambda e, k=k: e.transpose(pTc[:, k, :], hb[q][:, k * 128:(k + 1) * 128], idt[:]),
                 reads=["hb%d" % q, "idt"], writes=["pT%d_%d" % (q, k)])
        P.op("act", lambda e: e.copy(hTt[tb][:, :, s * 128:(s + 1) * 128], pTc[:]),
             reads=["pT%d_%d" % (q, k) for k in range(8)], writes=["hTt%d_%d" % (tb, s)])
        if s == NS - 1:
            P.dma("sp", dstT[:, :, ti * TT:(ti + 1) * TT], hTt[tb][:], reads=["hTt%d_%d" % (tb, ss) for ss in range(NS)],
                  writes=["dstT_%d" % ti], key="stT_%d" % tb)

    load_x(0)
    load_weights()
    xtrans(0)
    late = []
    for ti in range(ntile):
        if ti + 1 < ntile:
            load_x(ti + 1)
        for f in range(NF):
            up(ti, f)
            if f >= 2 and late:
                _replay(P, [late.pop(0)])
            if f == 15 and ti > 0:
                assert not late
                for s in range(NS):
                    htrans(ti - 1, s)
        if ti + 1 < ntile:
            xtrans(ti + 1)
        for s in range(NS):
            if s == NS - 1 and ti + 1 < ntile:
                r = _Rec()
                down(ti, s, r)
                _replay(P, r.calls[:2])
                late = list(r.calls[2:])
                assert len(late) <= 13
            else:
                down(ti, s)
    for s in range(NS):
        htrans(ntile - 1, s)
    ph.close()


def layer_norm(P, zt, zn, stats, mv, lng, lnb, eps, out, outn, out_bf=None, out_bfn=None):
    for c in range(2):
        P.op("dve", lambda e, c=c: e.bn_stats(stats[:, c, :], zt[:, c * 512:(c + 1) * 512]), reads=[zn], writes=["stats%d" % c])
    P.op("dve", lambda e: e.bn_aggr(mv[:, 0:2], stats[:]), reads=["stats0", "stats1"], writes=["mv01"])
    P.op("dve", lambda e: e.tensor_scalar(mv[:, 2:3], mv[:, 1:2], float(eps), None, ALU.add), reads=["mv01"], writes=["mv2"])
    P.op("act", lambda e: e.activation(mv[:, 3:4], mv[:, 2:3], AF.Sqrt), reads=["mv2"], writes=["mv3"])
    P.op("dve", lambda e: e.reciprocal(mv[:, 4:5], mv[:, 3:4]), reads=["mv3"], writes=["mv4"])
    P.op("dve", lambda e: e.scalar_tensor_tensor(zt[:], zt[:], mv[:, 0:1], lng[:], ALU.subtract, ALU.mult),
         reads=[zn, "mv01", "lng"], writes=[zn])
    P.op("dve", lambda e: e.scalar_tensor_tensor(out[:], zt[:], mv[:, 4:5], lnb[:], ALU.mult, ALU.add),
         reads=[zn, "mv4", "lnb"], writes=[outn])
    if out_bf is not None:
        P.op("dve", lambda e: e.scalar_tensor_tensor(out_bf[:], zt[:], mv[:, 4:5], lnb[:], ALU.mult, ALU.add),
             reads=[zn, "mv4", "lnb"], writes=[out_bfn])


def attn_phase(nc, P, T, seq, h1T, W, CN, mixA, nqb=None):
    NCH = T // 128
    outer = Phase(nc, P)
    QT = outer.sb([128, 4, T], BF16)
    KT = outer.sb([128, 2, 2, T], BF16)
    VA = outer.sb([128, NCH, 2, 65], BF16)
    VAo = outer.sb([128, NCH, 2, 128], BF16)
    negm = outer.sb([128, 1], F32)
    ones32 = outer.sb([128, 128], BF16)
    ph = Phase(nc, P)
    wq = ph.sb([128, 8, 768], BF16)
    gq = ph.sb([128, 10, 64], F32)
    gm = ph.sb([128, 4], F32)
    rC = ph.sb([128, NCH, 64], F32)
    rS = ph.sb([128, NCH, 64], F32)
    idt = ph.sb([128, 128], BF16)
    hTt = [ph.sb([128, 8, 512], BF16) for _ in range(2)]
    sq2 = [ph.sb([128, 640], F32) for _ in range(2)]
    ss2 = [ph.sb([128, 40], F32) for _ in range(2)]
    qn2 = [ph.sb([128, 10, 64], F32) for _ in range(2)]
    t12 = [ph.sb([128, 10, 64], F32) for _ in range(2)]
    t22 = [ph.sb([128, 10, 64], F32) for _ in range(2)]
    qr = [ph.sb([128, 10, 64], BF16) for _ in range(2)]
    krs = [ph.sb([128, 2, 64], BF16) for _ in range(2)]
    pq = [ph.ps([128, 1024], F32) for _ in range(2)]
    pTq = [ph.ps([128, 4, 128], BF16) for _ in range(2)]
    pTk = [ph.ps([128, 2, 128], BF16) for _ in range(2)]
    for k in range(8):
        P.dma("pool", wq[:, k, :], W["w_in"][k * 128:(k + 1) * 128, 1568:2336], writes=["wq%d" % k], key="wq")
    WQ = ["wq%d" % k for k in range(8)]
    for h in range(10):
        P.dma("sp", gq[:, h, :], bcast_rows(W["q_norm_g"] if h < 8 else W["k_norm_g"], 128), writes=["gq%d" % h], key="gq")
    GQ = ["gq%d" % h for h in range(10)]
    P.dma("sp", rC[:], CN["ropeC"].rearrange("(n p) f -> p n f", p=128), writes=["rC"], key="rC")
    P.dma("sp", rS[:], CN["ropeS"].rearrange("(n p) f -> p n f", p=128), writes=["rS"], key="rS")
    P.dma("sp", idt[:], CN["ident"], writes=["idt"], key="idt")
    P.op("pool", lambda e: e.memset(VA[:, :, :, 64:65], 1.0), writes=["VAones"])

    P.op("pool", lambda e: e.memset(VAo[:, :, :, 0:64], 0.0), writes=["VAoones"])
    P.op("pool", lambda e: e.memset(VAo[:, :, :, 0:1], 1.0), writes=["VAoones"])
    P.op("pool", lambda e: e.memset(ones32[:], 1.0), writes=["ones32"])
    P.op("pool", lambda e: e.memset(KT[:], 0.0), writes=["KTz"])
    P.op("dve", lambda e: e.tensor_reduce(gm[:, 0:1], gq[:, 0, :], AX.X, ALU.max, apply_absolute_value=True), reads=GQ, writes=["gm0"])
    P.op("dve", lambda e: e.tensor_reduce(gm[:, 1:2], gq[:, 8, :], AX.X, ALU.max, apply_absolute_value=True), reads=GQ, writes=["gm1"])
    P.op("dve", lambda e: e.tensor_tensor(gm[:, 2:3], gm[:, 0:1], gm[:, 1:2], ALU.mult), reads=["gm0", "gm1"], writes=["gm2"])
    P.op("dve", lambda e: e.tensor_scalar(negm[:], gm[:, 2:3], -8.0, None, ALU.mult), reads=["gm2"], writes=["negm"])
    loads, recs = [], []
    for ti in range(T // 512):
        hc = hTt[ti % 2]
        hn = "hTt%d" % (ti % 2)
        PP = _Rec()
        PP.dma("sp", hc[:], h1T[:, :, seq * T + ti * 512: seq * T + (ti + 1) * 512], writes=[hn], key=hn)
        loads.append(PP.calls)
        for s in range(4):
            PP = _Rec()
            recs.append(PP.calls)
            ci = ti * 4 + s
            b = ci % 2
            sq, ss, qn, t1, t2 = sq2[b], ss2[b], qn2[b], t12[b], t22[b]
            tok = slice(ci * 128, (ci + 1) * 128)

            def proj(e, s=s, b=b, hc=hc):
                for k in range(8):
                    e.matmul(pq[b][:, 0:512], hc[:, k, s * 128:(s + 1) * 128], wq[:, k, 0:512], start=(k == 0), stop=(k == 7))
                for k in range(8):
                    ins = e.matmul(pq[b][:, 512:768], hc[:, k, s * 128:(s + 1) * 128], wq[:, k, 512:768], start=(k == 0), stop=(k == 7))
                return ins
            PP.op("pe", proj, reads=[hn] + WQ, writes=["pq%d" % b])
            PP.op("act", lambda e, b=b, sq=sq, ss=ss, qn=qn, t1=t1, t2=t2: e.activation(sq[:], pq[b][:, 0:640], AF.Square), reads=["pq%d" % b], writes=["sq_" + str(b)])
            PP.op("dve", lambda e, sq=sq, ss=ss, qn=qn, t1=t1, t2=t2: e.tensor_reduce(ss[:, 0:10], sq[:].rearrange("p (h d) -> p h d", d=64), AX.X, ALU.add), reads=["sq_" + str(b)], writes=["ss0_" + str(b)])
            PP.op("dve", lambda e, sq=sq, ss=ss, qn=qn, t1=t1, t2=t2: e.tensor_scalar(ss[:, 10:20], ss[:, 0:10], 1.0 / 64, 1e-6, ALU.mult, ALU.add), reads=["ss0_" + str(b)], writes=["ss1_" + str(b)])
            PP.op("act", lambda e, sq=sq, ss=ss, qn=qn, t1=t1, t2=t2: e.activation(ss[:, 20:30], ss[:, 10:20], AF.Sqrt), reads=["ss1_" + str(b)], writes=["ss2_" + str(b)])
            PP.op("dve", lambda e, sq=sq, ss=ss, qn=qn, t1=t1, t2=t2: e.reciprocal(ss[:, 30:40], ss[:, 20:30]), reads=["ss2_" + str(b)], writes=["ss3_" + str(b)])
            PP.op("dve", lambda e, b=b, sq=sq, ss=ss, qn=qn, t1=t1, t2=t2: e.tensor_tensor(qn[:], pq[b][:, 0:640].rearrange("p (h d) -> p h d", d=64),
                                                       ss[:, 30:40].unsqueeze(2).broadcast_to([128, 10, 64]), ALU.mult),
                 reads=["pq%d" % b, "ss3_" + str(b)], writes=["qn_" + str(b)])
            PP.op("act", lambda e, b=b, ci=ci, sq=sq, ss=ss, qn=qn, t1=t1, t2=t2: e.copy(VA[:, ci, :, 0:64], pq[b][:, 640:768].rearrange("p (h d) -> p h d", d=64)),
                 reads=["pq%d" % b], writes=["VA%d" % ci])
            PP.op("act", lambda e, b=b, ci=ci: e.copy(VAo[:, ci, :, 64:128], pq[b][:, 640:768].rearrange("p (h d) -> p h d", d=64)),
                 reads=["pq%d" % b, "VAoones"], writes=["VAo%d" % ci])
            PP.op("pool", lambda e, sq=sq, ss=ss, qn=qn, t1=t1, t2=t2: e.tensor_tensor(qn[:], qn[:], gq[:], ALU.mult), reads=["qn_" + str(b)] + GQ, writes=["qn_" + str(b)])
            PP.op("pool", lambda e, ci=ci, sq=sq, ss=ss, qn=qn, t1=t1, t2=t2: e.tensor_tensor(t1[:], qn[:], rC[:, ci, :].unsqueeze(1).broadcast_to([128, 10, 64]), ALU.mult),
                 reads=["qn_" + str(b), "rC"], writes=["t1_" + str(b)])
            qv = qn[:].rearrange("p h (a c f) -> p h a c f", a=2, c=2)
            tv = t2[:].rearrange("p h (a c f) -> p h a c f", a=2, c=2)

            def rot(e, ci=ci, qv=qv, tv=tv):
                sv = rS[:, ci, :].rearrange("p (a c f) -> p a c f", a=2, c=2)
                e.tensor_tensor(tv[:, :, :, 0, :], qv[:, :, :, 1, :], sv[:, :, 0, :].unsqueeze(1).broadcast_to([128, 10, 2, 16]), ALU.mult)
                return e.tensor_tensor(tv[:, :, :, 1, :], qv[:, :, :, 0, :], sv[:, :, 1, :].unsqueeze(1).broadcast_to([128, 10, 2, 16]), ALU.mult)
            PP.op("dve", rot, reads=["qn_" + str(b), "rS"], writes=["t2_" + str(b)])
            PP.op("dve", lambda e, b=b, sq=sq, ss=ss, qn=qn, t1=t1, t2=t2: e.tensor_tensor(qr[b][:], t1[:], t2[:], ALU.add), reads=["t1_" + str(b), "t2_" + str(b)], writes=["qr%d" % b])

            def kswap(e, b=b):
                e.tensor_copy(krs[b][:, 0, :], qr[b][:, 9, :])
                return e.tensor_copy(krs[b][:, 1, :], qr[b][:, 8, :])
            PP.op("pool", kswap, reads=["qr%d" % b], writes=["krs%d" % b])

            def tq(e, b=b):
                for pr in range(4):
                    ins = e.transpose(pTq[b][:, pr, :], qr[b][:, 2 * pr:2 * pr + 2, :].rearrange("p h d -> p (h d)"), idt[:])
                return ins
            PP.op("pe", tq, reads=["qr%d" % b, "idt"], writes=["pTq%d" % b])

            def tk(e, b=b):
                e.transpose(pTk[b][:, 0, :], qr[b][:, 8:10, :].rearrange("p h d -> p (h d)"), idt[:])
                return e.transpose(pTk[b][:, 1, :], krs[b][:].rearrange("p h d -> p (h d)"), idt[:])
            PP.op("pe", tk, reads=["qr%d" % b, "krs%d" % b, "idt"], writes=["pTk%d" % b])
            PP.op("act", lambda e, b=b, tok=tok, sq=sq, ss=ss, qn=qn, t1=t1, t2=t2: e.copy(QT[:, :, tok], pTq[b][:]), reads=["pTq%d" % b], writes=["QT%d" % ci])

            def kcp(e, b=b, tok=tok):
                e.tensor_copy(KT[0:64, :, 0, tok], pTk[b][0:64, :, :])
                e.tensor_copy(KT[64:128, 0, 1, tok], pTk[b][64:128, 1, :])
                return e.tensor_copy(KT[64:128, 1, 1, tok], pTk[b][64:128, 0, :])
            PP.op("dve", kcp, reads=["pTk%d" % b, "KTz"], writes=["KT%d" % ci])
    def _split(calls):
        k = next(i for i, c in enumerate(calls) if any(str(w).startswith("krs") for w in c[2].get("writes", ())))
        return calls[:1], calls[1:k], calls[k:]
    parts = [_split(c) for c in recs]
    NR = len(parts)
    _replay(P, loads[0])
    if len(loads) > 1:
        _replay(P, loads[1])
    _replay(P, parts[0][0])
    if NR > 1:
        _replay(P, parts[1][0])
    _replay(P, parts[0][1])
    for ci in range(NR):
        nt = (ci + 2) // 4 + 1
        if (ci + 2) % 4 == 0 and nt < len(loads):
            _replay(P, loads[nt])
        if ci + 2 < NR:
            _replay(P, parts[ci + 2][0])
        if ci + 1 < NR:
            _replay(P, parts[ci + 1][1])
        _replay(P, parts[ci][2])
    ph.close()
    ph = Phase(nc, P)
    PT = [ph.sb([128, 1024], BF16) for _ in range(2)]
    rd = [ph.sb([128, 512], F32) for _ in range(2)]
    rres = [ph.sb([128, 512], F32) for _ in range(2)]
    rb16 = [ph.sb([128, 2, 512], BF16) for _ in range(2)]
    o32 = [ph.sb([128, 512], F32) for _ in range(2)]
    ob = [ph.sb([128, 4, 512], BF16) for _ in range(2)]
    pS = [ph.ps([128, 1024], F32) for _ in range(2)]
    po = [ph.ps([128, 512], F32) for _ in range(2)]
    pb = ph.ps([128, 512], F32)
    nsg = NCH // 2
    groups = [(qb, h, sgi) for qb in range(nqb if nqb is not None else T // 512) for h in range(8) for sgi in range(nsg)]

    def emit_qk_exp(i):
        qb, h, sgi = groups[i]
        b = i % 2
        pr, base, kv = h // 2, (h % 2) * 64, h // 4
        qsl = slice(qb * 512, (qb + 1) * 512)

        def qk(e):
            for j in range(2):
                c = 2 * sgi + j
                ins = e.matmul(pS[b][:, j * 512:(j + 1) * 512], KT[:, kv, h % 2, c * 128:(c + 1) * 128],
                               QT[:, pr, qsl], start=True, stop=True)
            return ins
        P.op("pe", qk, reads=[], writes=["pS%d" % b])
        P.op("act", lambda e: e.activation(PT[b][:], pS[b][:], AF.Exp, bias=negm[:, 0:1], scale=0.125),
             reads=["pS%d" % b], writes=["PT%d" % b])

    pending = []

    def emit_pv(i):
        qb, h, sgi = groups[i]
        b = i % 2
        kv = h // 4
        hb_ = (qb * 8 + h) % 2
        poc = po[hb_]
        pon = "po%d" % hb_

        odd = h % 2
        pr = h // 2
        MO = 128 if odd else 65
        dp = 0 if odd else 64
        lo = 64 if odd else 0

        def pv(e):
            for j in range(2):
                c = 2 * sgi + j
                ins = e.matmul(poc[0:MO, :], (VAo if odd else VA)[:, c, kv, :], PT[b][:, j * 512:(j + 1) * 512],
                               start=(sgi == 0 and j == 0), stop=(sgi == nsg - 1 and j == 1))
            return ins
        P.op("pe", pv, reads=["PT%d" % b], writes=[pon] if sgi in (0, nsg - 1) else [])
        if sgi == nsg - 1:
            obc = ob[qb % 2]
            obn = "ob%d" % (qb % 2)
            rdc, oc = rd[hb_], o32[hb_]
            rrc, rbc = rres[hb_], rb16[hb_]

            def dfin(e):
                e.tensor_copy(oc[lo:lo + 64, :], poc[lo:lo + 64, :])
                return e.tensor_copy(rbc[dp:dp + 1, 0, :], poc[dp:dp + 1, :])
            P.op("dve", dfin, reads=[pon], writes=["rd%d" % hb_, "o32_%d" % hb_])
            P.op("dve", lambda e: e.tensor_tensor(rrc[dp:dp + 1, :], poc[dp:dp + 1, :], rbc[dp:dp + 1, 0, :], ALU.subtract),
                 reads=["rd%d" % hb_, pon], writes=["rres%d" % hb_])
            P.op("dve", lambda e: e.tensor_copy(rbc[dp:dp + 1, 1, :], rrc[dp:dp + 1, :]), reads=["rres%d" % hb_], writes=["rb%d" % hb_])

            def fin():
                MB = lo + 64

                def bc(e):
                    e.matmul(pb[0:MB, :], ones32[dp:dp + 1, 0:MB], rbc[dp:dp + 1, 0, :], start=True, stop=False)
                    return e.matmul(pb[0:MB, :], ones32[dp:dp + 1, 0:MB], rbc[dp:dp + 1, 1, :], start=False, stop=True)
                P.op("pe", bc, reads=["rd%d" % hb_, "rb%d" % hb_], writes=["pb"])
                P.op("dve", lambda e: e.reciprocal(rrc[lo:lo + 64, :], pb[lo:lo + 64, :]), reads=["pb", "rb%d" % hb_], writes=["rec%d" % hb_])
                P.op("dve", lambda e: e.tensor_tensor(obc[lo:lo + 64, pr, :], oc[lo:lo + 64, :], rrc[lo:lo + 64, :], ALU.mult),
                     reads=["o32_%d" % hb_, "rec%d" % hb_], writes=[obn + "_%d" % h])
                if h == 7:
                    P.dma("sp", mixA[:, :, seq * T + qb * 512: seq * T + (qb + 1) * 512], obc[:],
                          reads=[obn + "_%d" % hh for hh in range(8)], writes=["mixA%d" % qb], key=obn)
            pending.append((i + min(6, nsg), fin))

    n = len(groups)
    for i in range(n + 1):
        if i < n:
            emit_qk_exp(i)
        while pending and pending[0][0] <= i:
            pending.pop(0)[1]()
        if i >= 1:
            emit_pv(i - 1)
    while pending:
        pending.pop(0)[1]()
    ph.close()
    outer.es.close()


def gla_phase(nc, P, T, seq, h1T, W, CN, SC, half=False):
    NCH = T // 128
    base = seq * T
    gp_d, la_d, of_d, mixG = SC["gp"], SC["la"], SC["of"], SC["mixG"]
    ph = Phase(nc, P)
    wq = ph.sb([128, 8, 1568], BF16)
    w2 = ph.sb([16, 2, 256], BF16)
    b2 = ph.sb([128, 512], F32)
    one1 = ph.sb([128, 1], F32)
    idt = ph.sb([128, 128], BF16)
    hTt = [ph.sb([128, 8, 512], BF16) for _ in range(2)]
    gpb = [ph.sb([128, 1536], BF16) for _ in range(2)]
    zb16 = ph.sb([128, 32], BF16)
    thb = ph.sb([128, 512], F32)
    zT = ph.sb([16, 2, 128], BF16)
    xs = ph.sb([128, 512], F32)
    e1 = ph.sb([128, 512], F32)
    la = [ph.sb([128, 512], F32) for _ in range(2)]
    pq = ph.ps([128, 2048], F32)
    pz = ph.ps([128, 2, 128], BF16)
    px = ph.ps([128, 512], F32)
    for k in range(8):
        P.dma("pool", wq[:, k, :], W["w_in"][k * 128:(k + 1) * 128, 0:1568], writes=["wq%d" % k], key="wq")
    WQ = ["wq%d" % k for k in range(8)]
    P.dma("pool", w2[:, 0, :], W["gla_w2f"], writes=["w2f"], key="w2")
    P.dma("pool", w2[:, 1, :], W["gla_w2b"], writes=["w2b"], key="w2")
    P.dma("sp", b2[:, 0:256], bcast_rows(W["gla_b2f"], 128), writes=["b2f"], key="b2")
    P.dma("sp", b2[:, 256:512], bcast_rows(W["gla_b2b"], 128), writes=["b2b"], key="b2")
    P.dma("sp", idt[:], CN["ident"], writes=["idt"], key="idt")
    P.op("pool", lambda e: e.memset(one1[:], 1.0), writes=["one1"])
    def g0_load(ti):
        P.dma("sp", hTt[ti % 2][:], h1T[:, :, base + ti * 512: base + (ti + 1) * 512], writes=["hTt%d" % (ti % 2)], key="hTt%d" % (ti % 2))
    g0_load(0)
    for ti in range(T // 512):
        hc = hTt[ti % 2]
        hn = "hTt%d" % (ti % 2)
        if ti + 1 < T // 512:
            g0_load(ti + 1)
        for s in range(4):
            ci = ti * 4 + s
            b = ci % 2
            r0 = base + ci * 128

            def proj(e, s=s, hc=hc):
                for (c0, c1) in ((0, 512), (512, 1024), (1024, 1536), (1536, 1568)):
                    for k in range(8):
                        ins = e.matmul(pq[:, c0:c1], hc[:, k, s * 128:(s + 1) * 128], wq[:, k, c0:c1], start=(k == 0), stop=(k == 7))
                return ins
            P.op("pe", proj, reads=[hn] + WQ, writes=["pq"])
            P.op("dve", lambda e: e.tensor_copy(zb16[:], pq[:, 1536:1568]), reads=["pq"], writes=["zb16"])
            P.op("act", lambda e, b=b: e.copy(gpb[b][:], pq[:, 0:1536]), reads=["pq"], writes=["gpbA%d" % b])

            def tz(e):
                e.transpose(pz[0:16, 0, :], zb16[:, 0:16], idt[:])
                return e.transpose(pz[0:16, 1, :], zb16[:, 16:32], idt[:])
            P.op("pe", tz, reads=["zb16", "idt"], writes=["pz"])
            P.op("dve", lambda e: e.tensor_copy(zT[:], pz[0:16, :, :]), reads=["pz"], writes=["zT"])

            def xm(e):
                e.matmul(px[:, 0:256], zT[:, 0, :], w2[:, 0, :], start=True, stop=True)
                return e.matmul(px[:, 256:512], zT[:, 1, :], w2[:, 1, :], start=True, stop=True)
            P.op("pe", xm, reads=["zT", "w2f", "w2b"], writes=["px"])
            P.op("dve", lambda e: e.tensor_tensor(xs[:], px[:], b2[:], ALU.add), reads=["px", "b2f", "b2b"], writes=["xs"])
            P.op("act", lambda e: e.activation(e1[:], xs[:], AF.Exp, scale=-1.0), reads=["xs"], writes=["e1"])
            P.op("act", lambda e, b=b: e.activation(thb[:], gpb[b][:, 1024:1536], AF.Tanh, scale=0.5), reads=["gpbA%d" % b], writes=["thb"])
            P.op("act", lambda e, b=b: e.activation(la[b][:], e1[:], AF.Ln, bias=one1[:, 0:1]), reads=["e1", "one1"], writes=["la%d" % b])
            P.op("dve", lambda e: e.tensor_scalar(thb[:], thb[:], 0.5, 0.5, ALU.mult, ALU.add), reads=["thb"], writes=["thb"])
            P.op("dve", lambda e, b=b: e.tensor_tensor(gpb[b][:, 1024:1536], thb[:], gpb[b][:, 1024:1536], ALU.mult),
                 reads=["thb", "gpbA%d" % b], writes=["gpb%d" % b])
            P.dma("sp", gp_d[r0:r0 + 128, :], gpb[b][:], reads=["gpb%d" % b, "gpbA%d" % b], writes=["gp%d" % ci], key="gpb%d" % b)
            P.dma("sp", la_d[r0:r0 + 128, :], la[b][:], reads=["la%d" % b], writes=["lad%d" % ci], key="la%d" % b)
    ph.close()
    for dirn in range(2):
        ph = Phase(nc, P)
        idt = ph.sb([128, 128], BF16)
        Mc = ph.sb([128, 3, 128], BF16)
        Ind = ph.sb([128, 16], BF16)
        lhi = ph.sb([128, 256], BF16)
        llo = ph.sb([128, 256], BF16)
        lres = ph.sb([128, 256], F32)
        mskT = ph.sb([128, 128], F32)
        lq = ph.sb([128, 1], F32)
        gng = ph.sb([128, 4, 128], F32)
        gpt = [ph.sb([128, 1536], BF16) for _ in range(3)]
        lat = [ph.sb([128, 256], F32) for _ in range(2)]
        E = ph.sb([128, 4, 256], F32)
        dec = [ph.sb([64, 4, 16], F32) for _ in range(2)]
        prod = [ph.sb([128, 4, 256], BF16) for _ in range(2)]
        qdT = ph.sb([64, 4, 128], BF16)
        kdT = ph.sb([64, 4, 128], BF16)
        qeT = [ph.sb([64, 4, 2, 128], BF16) for _ in range(2)]
        scT = [ph.sb([128, 4, 128], BF16) for _ in range(2)]
        S = ph.sb([64, 4, 128], F32)
        Sb = [ph.sb([64, 4, 128], BF16) for _ in range(2)]
        of32 = [ph.sb([128, 512], F32) for _ in range(3)]
        pc = ph.ps([128, 1024], F32)
        pT = ph.ps([128, 3, 4, 128], F32)
        pSc = ph.ps([128, 4, 128], F32)
        po = ph.ps([128, 4, 128], F32)
        pU = ph.ps([128, 4, 128], F32)
        if dirn == 1:
            sq = ph.sb([128, 512], F32)
            ss = ph.sb([128, 16], F32)
            er = ph.sb([128, 512], F32)
            ob16 = [ph.sb([128, 512], BF16) for _ in range(2)]
            mT = [ph.sb([128, 4, 128], BF16) for _ in range(2)]
            for h in range(4):
                P.dma("sp", gng[:, h, :], bcast_rows(W["gla_gn_g"], 128), writes=["gng%d" % h], key="gng")
        GNG = ["gng%d" % h for h in range(4)]
        P.dma("sp", idt[:], CN["ident"], writes=["idt"], key="idt")
        for m in range(3):
            P.dma("pool", Mc[:, m, :], CN["glaM"][dirn, m], writes=["Mc%d" % m], key="Mc")
        P.dma("pool", Ind[:], CN["glaInd"], writes=["Ind"], key="Ind")
        P.dma("sp", mskT[:], CN["glaMask"][dirn], writes=["mskT"], key="mskT")
        Q = [P]
        P.op("pool", lambda e: e.memset(lq[:], math.log(0.125)), writes=["lq"])
        P.op("pool", lambda e: e.memset(qeT[0][:], 0.0), writes=["qeT0"])
        P.op("pool", lambda e: e.memset(qeT[1][:], 0.0), writes=["qeT1"])
        P.op("pool", lambda e: e.memset(S[:], 0.0), writes=["S"])
        P.op("pool", lambda e: e.memset(Sb[0][:], 0.0), writes=["Sb0"])
        NOUT = NCH // 2 if half else NCH
        order = list(range(NOUT)) if dirn == 0 else list(range(NCH - 1, -1, -1))
        NU = len(order)
        c0, c1 = (0, 1) if dirn == 0 else (1, 0)

        def prep(n):
            ci = order[n]
            b = n % 2
            b3 = n % 3
            r0 = base + ci * 128
            gt = gpt[b3]
            gtn = "gpt%d" % b3
            Q[0].dma("sp", gt[:], gp_d[r0:r0 + 128, :], writes=[gtn], key=gtn)
            Q[0].dma("sp", lat[b][:], la_d[r0:r0 + 128, dirn * 256:(dirn + 1) * 256], writes=["lat%d" % b], key="lat%d" % b)
            if dirn == 1 and ci < NOUT:
                Q[0].dma("sp", of32[b3][:], of_d[r0:r0 + 128, :], writes=["of32_%d" % b3], key="ofl%d" % b3)
            Q[0].op("dve", lambda e: e.tensor_copy(lhi[:], lat[b][:]), reads=["lat%d" % b], writes=["lhi"])
            Q[0].op("dve", lambda e: e.tensor_tensor(lres[:], lat[b][:], lhi[:], ALU.subtract), reads=["lat%d" % b, "lhi"], writes=["lres"])
            Q[0].op("dve", lambda e: e.tensor_copy(llo[:], lres[:]), reads=["lres"], writes=["llo"])

            def cm(e):
                for m in range(3):
                    e.matmul(pc[:, m * 256:(m + 1) * 256], Mc[:, m, :], lhi[:], start=True, stop=False)
                    ins = e.matmul(pc[:, m * 256:(m + 1) * 256], Mc[:, m, :], llo[:], start=False, stop=True)
                return ins
            Q[0].op("pe", cm, reads=["lhi", "llo", "Mc0", "Mc1", "Mc2"], writes=["pc"])

            def dl(e):
                for h in range(4):
                    e.matmul(pc[0:64, 768 + h * 16:768 + (h + 1) * 16], lhi[:, h * 64:(h + 1) * 64], Ind[:], start=True, stop=False)
                    ins = e.matmul(pc[0:64, 768 + h * 16:768 + (h + 1) * 16], llo[:, h * 64:(h + 1) * 64], Ind[:], start=False, stop=True)
                return ins
            Q[0].op("pe", dl, reads=["lhi", "llo", "Ind"], writes=["pdl"])
            Q[0].op("act", lambda e: e.activation(E[:, 0, :], pc[:, 0:256], AF.Exp, bias=lq[:, 0:1]), reads=["pc", "pdl", "lq"], writes=["E0"])
            Q[0].op("act", lambda e: e.activation(E[:, 1, :], pc[:, 0:256], AF.Exp, scale=-1.0), reads=["pc", "pdl"], writes=["E1"])
            Q[0].op("act", lambda e: e.activation(E[:, 2, :], pc[:, 512:768], AF.Exp, bias=lq[:, 0:1]), reads=["pc", "pdl", "lq"], writes=["E2"])
            Q[0].op("act", lambda e: e.activation(E[:, 3, :], pc[:, 256:512], AF.Exp), reads=["pc", "pdl"], writes=["E3"])
            Q[0].op("act", lambda e: e.activation(dec[b][:].rearrange("p h c -> p (h c)"), pc[0:64, 768:832], AF.Exp), reads=["pdl", "pc"], writes=["dec%d" % b])
            pr = prod[b]
            Q[0].op("dve", lambda e: e.tensor_tensor(pr[:, 0, :], gt[:, 0:256], E[:, 0, :], ALU.mult), reads=[gtn, "E0"], writes=["qd%d" % b])
            Q[0].op("dve", lambda e: e.tensor_tensor(pr[:, 1, :], gt[:, 256:512], E[:, 1, :], ALU.mult), reads=[gtn, "E1"], writes=["kd%d" % b])
            Q[0].op("dve", lambda e: e.tensor_tensor(pr[:, 2, :], gt[:, 0:256], E[:, 2, :], ALU.mult), reads=[gtn, "E2"], writes=["qe%d" % b])
            Q[0].op("dve", lambda e: e.tensor_tensor(pr[:, 3, :], gt[:, 256:512], E[:, 3, :], ALU.mult), reads=[gtn, "E3"], writes=["ku%d" % b])

            def tr(e):
                for m in range(3):
                    for h in range(4):
                        ins = e.matmul(pT[0:64, m, h, :], pr[:, m, h * 64:(h + 1) * 64], idt[:], start=True, stop=True)
                return ins
            Q[0].op("pe", tr, reads=["qd%d" % b, "kd%d" % b, "qe%d" % b, "idt"], writes=["pT"])
            Q[0].op("act", lambda e: e.copy(qdT[:], pT[0:64, 0, :, :]), reads=["pT"], writes=["qdT"])
            Q[0].op("act", lambda e: e.copy(kdT[:], pT[0:64, 1, :, :]), reads=["pT"], writes=["kdT"])

            def qecp(e):
                e.copy(qeT[b][:, :, 0, 0:64], pT[0:64, 2, :, 0:64])
                return e.copy(qeT[b][:, :, 1, 64:128], pT[0:64, 2, :, 64:128])
            Q[0].op("act", qecp, reads=["pT"], writes=["qeT%d" % b])

            def sc(e):
                for h in range(4):
                    ins = e.matmul(pSc[:, h, :], kdT[:, h, :], qdT[:, h, :], start=True, stop=True)
                return ins
            Q[0].op("pe", sc, reads=["qdT", "kdT"], writes=["pSc"])
            Q[0].op("dve", lambda e: e.tensor_tensor(scT[b][:], pSc[:], mskT[:].unsqueeze(1).broadcast_to([128, 4, 128]), ALU.mult),
                 reads=["pSc", "mskT"], writes=["scT%d" % b])

        def state(n):
            ci = order[n]
            b = n % 2
            b3 = n % 3
            r0 = base + ci * 128
            gt = gpt[b3]
            gtn = "gpt%d" % b3
            pr = prod[b]

            def upd(c):
                def f(e):
                    for h in range(4):
                        ins = e.matmul(pU[0:64, h, :], pr[c * 64:(c + 1) * 64, 3, h * 64:(h + 1) * 64],
                                       gt[c * 64:(c + 1) * 64, 512 + h * 128:512 + (h + 1) * 128], start=True, stop=True)
                    return ins
                return f

            def supd(c, dst, dstn):
                def sfused(e):
                    for h in range(4):
                        ins = e.scalar_tensor_tensor(S[:, h, :], S[:, h, :], dec[b][:, h, c:c + 1], pU[0:64, h, :], ALU.mult, ALU.add)
                    return ins
                Q[0].op("dve", sfused, reads=["S", "dec%d" % b, "pU"], writes=["S"])
                Q[0].op("act", lambda e: e.copy(dst[:], S[:]), reads=["S"], writes=[dstn])
            Q[0].op("pe", upd(c0), reads=["ku%d" % b, gtn], writes=["pU"])
            supd(c0, Sb[1], "Sb1")
            if ci >= NOUT:
                Q[0].op("pe", upd(c1), reads=["ku%d" % b, gtn], writes=["pU"])
                supd(c1, Sb[0], "Sb0")
                return

            def outm(e):
                for h in range(4):
                    e.matmul(po[:, h, :], scT[b][:, h, :], gt[:, 512 + h * 128:512 + (h + 1) * 128], start=True, stop=False)
                    e.matmul(po[:, h, :], qeT[b][:, h, c0, :], Sb[0][:, h, :], start=False, stop=False)
                    ins = e.matmul(po[:, h, :], qeT[b][:, h, c1, :], Sb[1][:, h, :], start=False, stop=True)
                return ins
            Q[0].op("pe", outm, reads=["scT%d" % b, gtn, "qeT%d" % b, "Sb0", "Sb1"], writes=["po"])
            Q[0].op("pe", upd(c1), reads=["ku%d" % b, gtn], writes=["pU"])
            supd(c1, Sb[0], "Sb0")
            o = of32[b3]
            on = "of32_%d" % b3
            if dirn == 0:
                Q[0].op("act", lambda e: e.copy(o[:], po[:].rearrange("p h e -> p (h e)")), reads=["po"], writes=[on])
                Q[0].dma("sp", of_d[r0:r0 + 128, :], o[:], reads=[on], writes=["ofd%d" % ci], key="ofs%d" % b3)
            else:
                ov = o[:].rearrange("p (h e) -> p h e", e=128)
                Q[0].op("dve", lambda e: e.tensor_tensor(o[:], o[:], po[:].rearrange("p h e -> p (h e)"), ALU.add), reads=[on, "po"], writes=[on])
                Q[0].op("act", lambda e: e.activation(sq[:], o[:], AF.Square), reads=[on], writes=["sq"])
                Q[0].op("dve", lambda e: e.tensor_reduce(ss[:, 0:4], sq[:].rearrange("p (h e) -> p h e", e=128), AX.X, ALU.add), reads=["sq"], writes=["ss0"])
                Q[0].op("dve", lambda e: e.tensor_scalar(ss[:, 4:8], ss[:, 0:4], 1.0 / 128, 1e-5, ALU.mult, ALU.add), reads=["ss0"], writes=["ss1"])
                Q[0].op("act", lambda e: e.activation(ss[:, 8:12], ss[:, 4:8], AF.Sqrt), reads=["ss1"], writes=["ss2"])
                Q[0].op("dve", lambda e: e.reciprocal(ss[:, 12:16], ss[:, 8:12]), reads=["ss2"], writes=["ss3"])
                Q[0].op("dve", lambda e: e.tensor_tensor(ov, ov, ss[:, 12:16].unsqueeze(2).broadcast_to([128, 4, 128]), ALU.mult), reads=[on, "ss3"], writes=[on])
                Q[0].op("pool", lambda e: e.tensor_tensor(o[:], o[:], gng[:].rearrange("p h e -> p (h e)"), ALU.mult), reads=[on] + GNG, writes=[on])
                Q[0].op("dve", lambda e: e.tensor_tensor(ob16[b][:], o[:], gt[:, 1024:1536], ALU.mult), reads=[on, gtn], writes=["ob16_%d" % b])

        def tail(n):
            if dirn == 0 or order[n] >= NOUT:
                return
            ci = order[n]
            b = n % 2
            r0 = base + ci * 128

            def to(e):
                for c in range(4):
                    ins = e.matmul(pT[:, 0, c, :], ob16[b][:, c * 128:(c + 1) * 128], idt[:], start=True, stop=True)
                return ins
            Q[0].op("pe", to, reads=["ob16_%d" % b, "idt"], writes=["pT"])
            Q[0].op("act", lambda e: e.copy(mT[b][:], pT[:, 0, :, :]), reads=["pT"], writes=["mT%d" % b])
            Q[0].dma("sp", mixG[:, :, r0:r0 + 128], mT[b][:], reads=["mT%d" % b], writes=["mixG%d" % ci], key="mT%d" % b)

        def rec(fn, n):
            r = _Rec()
            Q[0] = r
            fn(n)
            Q[0] = P
            return r.calls

        def interleave(A, B):
            out, ia, ib, na, nb = [], 0, 0, len(A), len(B)
            while ia < na or ib < nb:
                if ib >= nb or (ia < na and ia * nb <= ib * na):
                    out.append(A[ia])
                    ia += 1
                else:
                    out.append(B[ib])
                    ib += 1
            return out

        preps = [rec(prep, n) for n in range(NU)]
        loads = [[c for c in calls if c[0] == "dma"] for calls in preps]
        comps = [[c for c in calls if c[0] != "dma"] for calls in preps]
        _replay(P, loads[0])
        if NU > 1:
            _replay(P, loads[1])
        _replay(P, comps[0])
        for n in range(NU):
            if n + 2 < NU:
                _replay(P, loads[n + 2])
            _replay(P, interleave(comps[n + 1] if n + 1 < NU else [], rec(state, n)))
            if n >= 1:
                tail(n - 1)
        tail(NU - 1)
        ph.close()


def wout_phase(nc, P, NT, W, CN, SC, h1, h2, h2T):
    mixG, mixA = SC["mixG"], SC["mixA"]
    ph = Phase(nc, P)
    wog = ph.sb([128, 4, D], BF16)
    woa = ph.sb([128, 4, D], BF16)
    lng = ph.sb([128, D], F32)
    lnb = ph.sb([128, D], F32)
    idt = ph.sb([128, 128], BF16)
    mG = [ph.sb([128, 4, 512], BF16) for _ in range(2)]
    mA = [ph.sb([128, 4, 512], BF16) for _ in range(2)]
    x32 = [ph.sb([128, D], F32) for _ in range(3)]
    z = [ph.sb([128, D], F32) for _ in range(2)]
    hb = [ph.sb([128, D], BF16) for _ in range(3)]
    hTt = [ph.sb([128, 8, 512], BF16) for _ in range(2)]
    stats = ph.sb([128, 2, 6], F32)
    mv = ph.sb([128, 8], F32)
    py = [ph.ps([128, 1024], F32) for _ in range(2)]
    pT = [ph.ps([128, 8, 128], BF16) for _ in range(2)]
    for c in range(4):
        P.dma("pool", wog[:, c, :], W["w_out"][c * 128:(c + 1) * 128, :], writes=["wog%d" % c], key="wog")
    for c in range(4):
        P.dma("pool", woa[:, c, :], W["w_out"][512 + c * 128:512 + (c + 1) * 128, :], writes=["woa%d" % c], key="woa")
    WO = ["wog%d" % c for c in range(4)] + ["woa%d" % c for c in range(4)]
    P.dma("sp", lng[:], bcast_rows(W["ln2_g"], 128), writes=["lng"], key="lng")
    P.dma("sp", lnb[:], bcast_rows(W["ln2_b"], 128), writes=["lnb"], key="lnb")
    P.dma("sp", idt[:], CN["ident"], writes=["idt"], key="idt")
    def load_tile(ti):
        t0 = ti * 512
        tb = ti % 2
        P.dma("sp", mG[tb][:], mixG[:, :, t0:t0 + 512], writes=["mG%d" % tb], key="mG%d" % tb)
        P.dma("sp", mA[tb][:], mixA[:, :, t0:t0 + 512], writes=["mA%d" % tb], key="mA%d" % tb)

    def main(ti, s):
        t0 = ti * 512
        tb = ti % 2
        gi = ti * 4 + s
        q = gi % 2
        r0 = t0 + s * 128

        def om(e):
            for hf in range(2):
                cs = slice(hf * 512, (hf + 1) * 512)
                for c in range(4):
                    e.matmul(py[q][:, cs], mG[tb][:, c, s * 128:(s + 1) * 128], wog[:, c, cs], start=(c == 0), stop=False)
                for c in range(4):
                    ins = e.matmul(py[q][:, cs], mA[tb][:, c, s * 128:(s + 1) * 128], woa[:, c, cs], start=False, stop=(c == 3))
            return ins
        P.op("pe", om, reads=["mG%d" % tb, "mA%d" % tb] + WO, writes=["py%d" % q])
        q3 = gi % 3
        xq = x32[q3]
        P.op("dve", lambda e: e.scalar_tensor_tensor(z[q][:], py[q][:], 1.0 / ALPHA, xq[:], ALU.mult, ALU.add),
             reads=["py%d" % q, "x32_%d" % q3], writes=["z%d" % q])
        layer_norm(P, z[q], "z%d" % q, stats, mv, lng, lnb, LN_EPS / ALPHA ** 2, xq, "x32_%d" % q3)
        P.dma("sp", h2[r0:r0 + 128, :], xq[:], reads=["x32_%d" % q3], writes=["h2_%d" % gi], key="st32_%d" % q3)
        P.op("act", lambda e: e.copy(hb[q3][:], xq[:]), reads=["x32_%d" % q3], writes=["hb%d" % q3])

    def load_x(ti, s):
        gi = ti * 4 + s
        r0 = ti * 512 + s * 128
        P.dma("sp", x32[gi % 3][:], h1[r0:r0 + 128, :], writes=["x32_%d" % (gi % 3)], key="x32_%d" % (gi % 3))

    def htrans(ti, s):
        tb = ti % 2
        q = (ti * 4 + s) % 2
        q3 = (ti * 4 + s) % 3
        for k in range(8):
            P.op("pe", lambda e, k=k: e.transpose(pT[q][:, k, :], hb[q3][:, k * 128:(k + 1) * 128], idt[:]),
                 reads=["hb%d" % q3, "idt"], writes=["pT%d_%d" % (q, k)])
        P.op("act", lambda e: e.copy(hTt[tb][:, :, s * 128:(s + 1) * 128], pT[q][:]),
             reads=["pT%d_%d" % (q, k) for k in range(8)], writes=["hTt%d_%d" % (tb, s)])
        if s == 3:
            P.dma("sp", h2T[:, :, ti * 512:(ti + 1) * 512], hTt[tb][:], reads=["hTt%d_%d" % (tb, ss) for ss in range(4)],
                  writes=["h2T_%d" % ti], key="stT_%d" % tb)

    steps = [(ti, s) for ti in range(NT // 512) for s in range(4)]
    load_tile(0)
    load_x(*steps[0])
    for idx, (ti, s) in enumerate(steps):
        if s == 0 and ti + 1 < NT // 512:
            load_tile(ti + 1)
        if idx + 1 < len(steps):
            load_x(*steps[idx + 1])
        main(ti, s)
        if idx >= 2:
            htrans(*steps[idx - 2])
    htrans(*steps[-2])
    htrans(*steps[-1])
    ph.close()


def gate_phase(nc, P, NT, W, CN, p_d, h3, h3T, y):
    ph = Phase(nc, P)
    wpg = ph.sb([128, 8, D], BF16)
    wpe = ph.sb([128, 2, D], BF16)
    bpg = ph.sb([128, D], F32)
    idt = ph.sb([128, 128], BF16)
    hTt = [ph.sb([128, 8, 512], BF16) for _ in range(2)]
    pb = [ph.sb([128, 4, PD], BF16) for _ in range(2)]
    ppT = [ph.sb([128, 2, 128], BF16) for _ in range(2)]
    x32 = [ph.sb([128, D], F32) for _ in range(3)]
    gs = [ph.sb([128, D], F32) for _ in range(2)]
    pes = [ph.sb([128, D], F32) for _ in range(2)]
    pgt2 = [ph.ps([128, 1024], F32) for _ in range(2)]
    ppe = ph.ps([128, 1024], F32)
    pT = [ph.ps([128, 2, 128], BF16) for _ in range(2)]
    for k in range(8):
        P.dma("pool", wpg[:, k, :], W["w_pg"][k * 128:(k + 1) * 128, :], writes=["wpg%d" % k], key="wpg")
    for k in range(2):
        P.dma("pool", wpe[:, k, :], W["w_pe"][k * 128:(k + 1) * 128, :], writes=["wpe%d" % k], key="wpe")
    WPG = ["wpg%d" % k for k in range(8)]
    WPE = ["wpe%d" % k for k in range(2)]
    P.dma("sp", bpg[:], bcast_rows(W["b_pg"], 128), writes=["bpg"], key="bpg")
    P.dma("sp", idt[:], CN["ident"], writes=["idt"], key="idt")
    def load_tile(ti):
        tb = ti % 2
        P.dma("sp", hTt[tb][:], h3T[:, :, ti * 512:(ti + 1) * 512], writes=["hTt%d" % tb], key="hTt%d" % tb)
        P.dma("pool", pb[tb][:], p_d[ti * 512:(ti + 1) * 512, :].rearrange("(s p) f -> p s f", p=128), writes=["pb%d" % tb], key="pb%d" % tb)

    def ptrans(ti, s):
        q = (ti * 4 + s) % 2
        tb = ti % 2

        def tp(e):
            for c in range(2):
                ins = e.transpose(pT[q][:, c, :], pb[tb][:, s, c * 128:(c + 1) * 128], idt[:])
            return ins
        P.op("pe", tp, reads=["pb%d" % tb, "idt"], writes=["pT%d" % q])
        P.op("act", lambda e: e.copy(ppT[q][:], pT[q][:]), reads=["pT%d" % q], writes=["ppT%d" % q])

    def main(ti, s):
        tb = ti % 2
        gi = ti * 4 + s
        q = gi % 2
        r0 = ti * 512 + s * 128
        pgt = pgt2[q]

        def gm(e):
            for hf in range(2):
                cs = slice(hf * 512, (hf + 1) * 512)
                for k in range(8):
                    ins = e.matmul(pgt[:, cs], hTt[tb][:, k, s * 128:(s + 1) * 128], wpg[:, k, cs], start=(k == 0), stop=(k == 7))
            return ins
        P.op("pe", gm, reads=["hTt%d" % tb] + WPG, writes=["pgt%d" % q])

        def pm(e):
            for hf in range(2):
                cs = slice(hf * 512, (hf + 1) * 512)
                for c in range(2):
                    ins = e.matmul(ppe[:, cs], ppT[q][:, c, :], wpe[:, c, cs], start=(c == 0), stop=(c == 1))
            return ins
        P.op("pe", pm, reads=["ppT%d" % q] + WPE, writes=["ppe"])
        P.op("act", lambda e: e.copy(pes[q][:], ppe[:]), reads=["ppe"], writes=["pes%d" % q])
        g = gs[q]
        gn = "gs%d" % q
        P.op("dve", lambda e: e.tensor_tensor(g[:], pgt[:], bpg[:], ALU.add), reads=["pgt%d" % q, "bpg"], writes=[gn])
        P.op("act", lambda e: e.activation(g[:], g[:], AF.Tanh, scale=0.5), reads=[gn], writes=[gn])
        P.op("dve", lambda e: e.tensor_scalar(g[:], g[:], 0.5, 0.5, ALU.mult, ALU.add), reads=[gn], writes=[gn])
        P.op("dve", lambda e: e.tensor_tensor(g[:], g[:], pes[q][:], ALU.mult), reads=[gn, "pes%d" % q], writes=[gn])
        P.op("pool", lambda e: e.tensor_tensor(g[:], g[:], x32[gi % 3][:], ALU.add), reads=[gn, "x32_%d" % (gi % 3)], writes=[gn])
        P.dma("sp", y[r0:r0 + 128, :], g[:], reads=[gn], writes=["y%d" % gi], key="sty%d" % q)

    def load_x(ti, s):
        gi = ti * 4 + s
        r0 = ti * 512 + s * 128
        P.dma("sp", x32[gi % 3][:], h3[r0:r0 + 128, :], writes=["x32_%d" % (gi % 3)], key="x32_%d" % (gi % 3))

    steps = [(ti, s) for ti in range(NT // 512) for s in range(4)]
    load_tile(0)
    load_x(*steps[0])
    ptrans(*steps[0])
    for idx, (ti, s) in enumerate(steps):
        if s == 0 and ti + 1 < NT // 512:
            load_tile(ti + 1)
        if idx + 1 < len(steps):
            load_x(*steps[idx + 1])
            ptrans(*steps[idx + 1])
        main(ti, s)
    ph.close()


W_NAMES = ["ffn1_wg", "ffn1_wu", "ffn1_wd", "ln1_g", "ln1_b", "w_in", "gla_w2f", "gla_b2f", "gla_w2b", "gla_b2b",
           "gla_gn_g", "q_norm_g", "k_norm_g", "w_out", "ln2_g", "ln2_b", "ffn2_wg", "ffn2_wu", "ffn2_wd",
           "ln3_g", "ln3_b", "w_pg", "b_pg", "w_pe"]
W_SHAPES = {"ffn1_wg": [D, DFF], "ffn1_wu": [D, DFF], "ffn1_wd": [DFF, D], "ln1_g": [D], "ln1_b": [D], "w_in": [D, DIN],
            "gla_w2f": [16, 256], "gla_b2f": [256], "gla_w2b": [16, 256], "gla_b2b": [256], "gla_gn_g": [128],
            "q_norm_g": [64], "k_norm_g": [64], "w_out": [D, D], "ln2_g": [D], "ln2_b": [D], "ffn2_wg": [D, DFF],
            "ffn2_wu": [D, DFF], "ffn2_wd": [DFF, D], "ln3_g": [D], "ln3_b": [D], "w_pg": [D, D], "b_pg": [D], "w_pe": [PD, D]}


def build(T, NSEQ, upto=99, debug=False, halfB=False):
    NT = T * NSEQ
    NTE = NT - T // 2 if halfB else NT
    nc = bass.Bass("TRN2", target_bir_lowering=False)
    x = nc.dram_tensor("x", [NT, D], F32, kind="ExternalInput").ap()
    p = nc.dram_tensor("p", [NT, PD], F32, kind="ExternalInput").ap()
    W = {n: nc.dram_tensor(n, W_SHAPES[n], F32, kind="ExternalInput").ap() for n in W_NAMES}
    CN = {"ident": nc.dram_tensor("ident", [128, 128], BF16, kind="ExternalInput").ap(),
          "ropeC": nc.dram_tensor("ropeC", [T, 64], F32, kind="ExternalInput").ap(),
          "ropeS": nc.dram_tensor("ropeS", [T, 64], F32, kind="ExternalInput").ap(),
          "glaM": nc.dram_tensor("glaM", [2, 3, 128, 128], F32, kind="ExternalInput").ap(),
          "glaInd": nc.dram_tensor("glaInd", [128, 16], F32, kind="ExternalInput").ap(),
          "glaMask": nc.dram_tensor("glaMask", [2, 128, 128], F32, kind="ExternalInput").ap()}
    y = nc.dram_tensor("y", [NTE, D], F32, kind="ExternalOutput").ap()
    sk = "ExternalOutput" if debug else "Internal"
    dt = lambda n, sh, ty: nc.dram_tensor(n, sh, ty, kind=sk).ap()
    h1 = dt("h1", [NT, D], F32)
    h1T = dt("h1T", [128, 8, NT], BF16)
    SC = {"mixA": dt("mixA", [128, 4, NT], BF16), "mixG": dt("mixG", [128, 4, NT], BF16),
          "gp": dt("gp", [NT, 1536], BF16), "la": dt("la", [NT, 512], F32), "of": dt("of", [NT, 512], F32)}
    h2 = dt("h2", [NT, D], F32)
    h2T = dt("h2T", [128, 8, NT], BF16)
    h3 = dt("h3", [NT, D], F32)
    h3T = dt("h3T", [128, 8, NT], BF16)
    with contextlib.ExitStack() as stack:
        P = Prog(nc, stack)
        ffn_phase(nc, P, "f1", NT, x, None, W["ffn1_wg"], W["ffn1_wu"], W["ffn1_wd"], W["ln1_g"], W["ln1_b"], CN["ident"],
                  h1, h1T, 0.5 / ALPHA, LN_EPS / ALPHA ** 2)
        for seq in range(NSEQ):
            hb = halfB and seq == NSEQ - 1
            if upto >= 2:
                attn_phase(nc, P, T, seq, h1T, W, CN, SC["mixA"], nqb=(T // 1024 if hb else None))
            if upto >= 3:
                gla_phase(nc, P, T, seq, h1T, W, CN, SC, half=hb)
        NT = NTE
        if upto >= 4:
            wout_phase(nc, P, NT, W, CN, SC, h1, h2, h2T)
        if upto >= 5:
            ffn_phase(nc, P, "f2", NT, h2, h2T, W["ffn2_wg"], W["ffn2_wu"], W["ffn2_wd"], W["ln3_g"], W["ln3_b"], CN["ident"],
                      h3, h3T, 0.5 / ALPHA, LN_EPS / ALPHA ** 2)
            gate_phase(nc, P, NT, W, CN, p, h3, h3T, y)
    return nc


def gla_consts():
    t = np.arange(128)
    a = t % 64
    c = t // 64
    same = (c[:, None] == c[None, :]).astype(np.float32)
    aj, ai = a[:, None], a[None, :]
    M = np.zeros((2, 3, 128, 128), np.float32)
    M[0, 0] = same * ((aj <= ai).astype(np.float32) - (aj <= 31).astype(np.float32))
    M[0, 1] = same * (aj > ai)
    M[0, 2] = same * (aj <= ai)
    M[1, 0] = same * ((aj >= ai).astype(np.float32) - (aj >= 32).astype(np.float32))
    M[1, 1] = same * (aj < ai)
    M[1, 2] = same * (aj >= ai)
    M *= -1.0 / 16
    Ind = np.zeros((128, 16), np.float32)
    Ind[t, c] = -1.0 / 16
    mask = np.zeros((2, 128, 128), np.float32)
    mask[0] = same * (aj <= ai)
    mask[1] = same * (aj > ai)
    return M, Ind, mask


def rope_tables(T):
    t = np.arange(T)
    row = (t // 64).astype(np.float32)
    col = (t % 64).astype(np.float32)
    inv = (10000.0 ** (-np.arange(0, 32, 2, dtype=np.float32) / 32)).astype(np.float32)
    ar = row[:, None] * inv[None, :]
    ac = col[:, None] * inv[None, :]
    C = np.concatenate([np.cos(ar), np.cos(ar), np.cos(ac), np.cos(ac)], axis=1).astype(np.float32)
    S = np.concatenate([-np.sin(ar), np.sin(ar), -np.sin(ac), np.sin(ac)], axis=1).astype(np.float32)
    return C, S


def consts(T, flip=False):
    C, S = rope_tables(T)
    M, Ind, mask = gla_consts()
    if flip:
        C, S = C[::-1].copy(), S[::-1].copy()
        t = np.arange(128)
        a = t % 64
        same = ((t // 64)[:, None] == (t // 64)[None, :]).astype(np.float32)
        mask = np.stack([same * (a[:, None] < a[None, :]), same * (a[:, None] >= a[None, :])]).astype(np.float32)
    return {"ident": np.eye(128, dtype=np.float32).astype(ml_dtypes.bfloat16), "ropeC": C, "ropeS": S,
            "glaM": M, "glaInd": Ind, "glaMask": mask}


_CACHE = {}


def kernel(**inputs):
    T, NSEQ, NCORE = 4096, 2, 8
    H = T // 2
    xp = np.asarray(inputs["x_prompt"], np.float32)
    xs = np.asarray(inputs["x_sample"], np.float32)
    pp = np.asarray(inputs["p_prompt"], np.float32)[0]
    ps = np.asarray(inputs["p_sample"], np.float32)[0]
    Wn = {n: np.ascontiguousarray(np.asarray(inputs[n], np.float32)[0]) for n in W_NAMES}
    Wf = dict(Wn)
    Wf["gla_w2f"], Wf["gla_w2b"] = Wn["gla_w2b"], Wn["gla_w2f"]
    Wf["gla_b2f"], Wf["gla_b2b"] = Wn["gla_b2b"], Wn["gla_b2f"]
    wi = Wn["w_in"].copy()
    wi[:, 1536:1552], wi[:, 1552:1568] = Wn["w_in"][:, 1552:1568], Wn["w_in"][:, 1536:1552]
    Wf["w_in"] = wi
    cn = [consts(T, False), consts(T, True)]
    if "nc" not in _CACHE:
        _CACHE["nc"] = build(T, NSEQ, halfB=True)
    nc = _CACHE["nc"]
    in_maps = []
    for c in range(NCORE):
        flip = c >= 4
        xa, pa, xb, pb = xp[c], pp[c], xs[c % 4], ps[c % 4]
        if flip:
            xa, pa, xb, pb = xa[::-1], pa[::-1], xb[::-1], pb[::-1]
        m = {"x": np.ascontiguousarray(np.concatenate([xa, xb], 0)),
             "p": np.ascontiguousarray(np.concatenate([pa, pb], 0))}
        m.update(Wf if flip else Wn)
        m.update(cn[1] if flip else cn[0])
        in_maps.append(m)
    res = run_bass_kernel_spmd(nc, in_maps, core_ids=list(range(NCORE)))
    ys = [np.asarray(res.results[c]["y"], np.float32) for c in range(NCORE)]
    yp = np.stack([ys[c][:T] if c < 4 else ys[c][:T][::-1] for c in range(8)], 0)
    ysm = np.stack([np.concatenate([ys[j][T:T + H], ys[j + 4][T:T + H][::-1]], 0) for j in range(4)], 0)
    return (np.ascontiguousarray(yp), np.ascontiguousarray(ysm))
```

```python
import contextlib
import math
import numpy as np
import ml_dtypes
import concourse.bass as bass
import concourse.mybir as mybir
from concourse.bass_utils import run_bass_kernel_spmd

F32 = mybir.dt.float32
BF16 = mybir.dt.bfloat16
AF = mybir.ActivationFunctionType
ALU = mybir.AluOpType
AX = mybir.AxisListType

D = 1024
DFF = 2816
NF = DFF // 128
PD = 256
DIN = 2336
ALPHA = 2.0 ** 0.25
LN_EPS = 1e-5
ENGS = ("pe", "act", "dve", "pool", "sp")


class _Op:
    __slots__ = ("eng", "fn", "deps", "is_dma", "key", "signal", "seq", "waits", "idx")

    def __init__(self, eng, fn, is_dma=False, key=None):
        self.eng = eng
        self.fn = fn
        self.deps = set()
        self.is_dma = is_dma
        self.key = key
        self.signal = False
        self.seq = None
        self.waits = None


class Prog:
    NSLOT = 40
    NPOOL = 10

    def __init__(self, nc, stack):
        self.nc = nc
        self.sem_eng = {e: stack.enter_context(nc.semaphore("s_" + e)) for e in ENGS}
        self.sem_slot = [stack.enter_context(nc.semaphore("d_%d" % i)) for i in range(self.NSLOT)]
        self.eng_cnt = {e: 0 for e in ENGS}
        self.slot_cnt = [0] * self.NSLOT
        self.known = {e: {} for e in ENGS}
        self.reset()

    def reset(self):
        self.ops = []
        self.last_w = {}
        self.readers = {}

    def _add(self, op, reads, writes):
        idx = len(self.ops)
        op.idx = idx
        for r in reads:
            w = self.last_w.get(r)
            if w is not None:
                op.deps.add(w)
        for t in writes:
            w = self.last_w.get(t)
            if w is not None:
                op.deps.add(w)
            for rd in self.readers.get(t, ()):
                op.deps.add(rd)
        op.deps.discard(idx)
        for r in reads:
            self.readers.setdefault(r, []).append(idx)
        for t in writes:
            self.last_w[t] = idx
            self.readers[t] = []
        self.ops.append(op)
        return idx

    def op(self, eng, fn, reads=(), writes=()):
        return self._add(_Op(eng, fn), reads, writes)

    def dma(self, queue, out, in_, reads=(), writes=(), key=None, **kw):
        assert key is not None
        o = _Op(queue, lambda e: e.dma_start(out=out, in_=in_, **kw), is_dma=True, key=key)
        o.signal = True
        return self._add(o, reads, writes)

    def lower(self):
        nc = self.nc
        ops = self.ops
        for o in ops:
            for d in o.deps:
                p = ops[d]
                if p.eng == "pe" and o.eng == "pe" and not p.is_dma and not o.is_dma:
                    continue
                p.signal = True
        streams = {e: [] for e in ENGS}
        for o in ops:
            streams[o.eng].append(o)
        for e in ENGS:
            lst = [o for o in streams[e] if not o.is_dma]
            if lst:
                lst[-1].signal = True
        slot_of = {}
        npool = [0]
        nsp = [0]
        for o in ops:
            if not o.signal:
                continue
            if o.is_dma:
                if o.key not in slot_of:
                    if o.eng == "pool":
                        slot_of[o.key] = npool[0]
                        npool[0] += 1
                        assert npool[0] <= self.NPOOL, "too many pool dma keys"
                    else:
                        slot_of[o.key] = self.NPOOL + nsp[0]
                        nsp[0] += 1
                        assert self.NPOOL + nsp[0] <= self.NSLOT, "too many sp dma keys"
                s = slot_of[o.key]
                self.slot_cnt[s] += 16
                o.seq = self.slot_cnt[s]
            else:
                self.eng_cnt[o.eng] += 1
                o.seq = self.eng_cnt[o.eng]

        def sem_of(p):
            return self.sem_slot[slot_of[p.key]] if p.is_dma else self.sem_eng[p.eng]

        for e in ENGS:
            known = self.known[e]
            for o in streams[e]:
                need = {}
                for d in o.deps:
                    p = ops[d]
                    if p.eng == "pe" and o.eng == "pe" and not p.is_dma and not o.is_dma:
                        continue
                    s = ("k", slot_of[p.key]) if p.is_dma else ("e", p.eng)
                    if p.seq > need.get(s, 0):
                        need[s] = p.seq
                w = []
                for s, v in need.items():
                    if known.get(s, 0) >= v:
                        continue
                    known[s] = v
                    w.append((self.sem_slot[s[1]] if s[0] == "k" else self.sem_eng[s[1]], v))
                o.waits = w
        fin = [(self.sem_slot[i], self.slot_cnt[i]) for i in range(self.NSLOT) if self.slot_cnt[i] > 0]
        fin += [(self.sem_eng[e], self.eng_cnt[e]) for e in ENGS if self.eng_cnt[e] > 0]

        with nc.Block(no_gpsimd_drain=True) as block:
            def run(e, engobj):
                for o in streams[e]:
                    for (s, v) in o.waits:
                        engobj.wait_ge(s, v)
                    ins = o.fn(engobj)
                    if o.signal:
                        ins.then_inc(sem_of(o), 16 if o.is_dma else 1)
                for (s, v) in fin:
                    engobj.wait_ge(s, v)

            @block.sync
            def _(eng):
                run("sp", eng)

            @block.tensor
            def _(eng):
                run("pe", eng)

            @block.scalar
            def _(eng):
                run("act", eng)

            @block.vector
            def _(eng):
                run("dve", eng)

            @block.gpsimd
            def _(eng):
                run("pool", eng)
        self.reset()


_CNT = [0]


class _Rec:
    def __init__(self):
        self.calls = []

    def op(self, *a, **k):
        self.calls.append(("op", a, k))

    def dma(self, *a, **k):
        self.calls.append(("dma", a, k))


def _replay(P, calls):
    for kind, a, k in calls:
        getattr(P, kind)(*a, **k)


class Phase:
    def __init__(self, nc, P):
        self.nc = nc
        self.P = P
        self.es = contextlib.ExitStack()
        self.n = 0

    def sb(self, shape, dt, name=None):
        _CNT[0] += 1
        return self.es.enter_context(self.nc.sbuf_tensor(name or ("t%d" % _CNT[0]), list(shape), dt))

    def ps(self, shape, dt, name=None):
        _CNT[0] += 1
        return self.es.enter_context(self.nc.psum_tensor(name or ("p%d" % _CNT[0]), list(shape), dt))

    def close(self):
        self.P.lower()
        self.es.close()


def bcast_rows(ap_row, n):
    return ap_row.unsqueeze(0).broadcast_to([n, ap_row.shape[0]])


def ffn_phase(nc, P, tag, NT, src32, srcT, wg_d, wu_d, wd_d, g_d, b_d, ident_d, dst32, dstT, cres, eps):
    TT = 256
    NS = TT // 128
    ph = Phase(nc, P)
    wg = ph.sb([128, 8, DFF], BF16)
    wu = ph.sb([128, 8, DFF], BF16)
    wd = ph.sb([128, NF, D], BF16)
    lng = ph.sb([128, D], F32)
    lnb = ph.sb([128, D], F32)
    idt = ph.sb([128, 128], BF16)
    xb = [ph.sb([128, NS, D], BF16) for _ in range(2)] if srcT is None else None
    xT = [ph.sb([128, 8, TT], BF16) for _ in range(2)]
    hT = ph.sb([128, NF, TT], BF16)
    sg = [ph.sb([128, TT], F32) for _ in range(2)]
    x32 = [ph.sb([128, D], F32) for _ in range(2)]
    z = [ph.sb([128, D], F32) for _ in range(2)]
    hb = [ph.sb([128, D], BF16) for _ in range(2)]
    hTt = [ph.sb([128, 8, TT], BF16) for _ in range(2)]
    stats = ph.sb([128, 2, 6], F32)
    mv = ph.sb([128, 8], F32)
    pgu = [ph.ps([128, 512], F32) for _ in range(2)]
    py = [ph.ps([128, 1024], F32) for _ in range(2)]
    pT = [ph.ps([128, 8, 128], BF16) for _ in range(2)]

    HALF = DFF // 2
    ntile = NT // TT
    P.dma("sp", idt[:], ident_d, writes=["idt"], key="idt")
    WGUH = [["wg%d_%d" % (k, h) for k in range(8)] + ["wu%d_%d" % (k, h) for k in range(8)] for h in range(2)]
    WD = ["wd%d" % f for f in range(NF)]

    def load_weights():
        for h in range(2):
            for k in range(8):
                P.dma("pool", wg[:, k, h * HALF:(h + 1) * HALF], wg_d[k * 128:(k + 1) * 128, h * HALF:(h + 1) * HALF],
                      writes=["wg%d_%d" % (k, h)], key="wg%d" % h)
                P.dma("pool", wu[:, k, h * HALF:(h + 1) * HALF], wu_d[k * 128:(k + 1) * 128, h * HALF:(h + 1) * HALF],
                      writes=["wu%d_%d" % (k, h)], key="wu%d" % h)
        for f in range(NF):
            P.dma("pool", wd[:, f, :], wd_d[f * 128:(f + 1) * 128, :], writes=["wd%d" % f], key="wd")
        P.dma("sp", lng[:], bcast_rows(g_d, 128), writes=["lng"], key="lng")
        P.dma("sp", lnb[:], bcast_rows(b_d, 128), writes=["lnb"], key="lnb")

    def load_x(ti):
        t0 = ti * TT
        tb = ti % 2
        if srcT is None:
            for s in range(NS):
                P.dma("pool", xb[tb][:, s, :], src32[t0 + s * 128:t0 + (s + 1) * 128, :], writes=["xb%d_%d" % (tb, s)], key="xb%d_%d" % (tb, s))
        else:
            P.dma("sp", xT[tb][:], srcT[:, :, t0:t0 + TT], writes=["xT%d" % tb], key="xT%d" % tb)

    def xtrans(ti):
        if srcT is not None:
            return
        tb = ti % 2
        for s in range(NS):
            pTc = pT[s % 2]
            for k in range(8):
                P.op("pe", lambda e, s=s, k=k, pTc=pTc: e.transpose(pTc[:, k, :], xb[tb][:, s, k * 128:(k + 1) * 128], idt[:]),
                     reads=["xb%d_%d" % (tb, s), "idt"], writes=["pT%d_%d" % (s % 2, k)])
            P.op("act", lambda e, s=s, pTc=pTc: e.copy(xT[tb][:, :, s * 128:(s + 1) * 128], pTc[:]),
                 reads=["pT%d_%d" % (s % 2, k) for k in range(8)], writes=["xT%d_%d" % (tb, s)])

    def xT_reads(ti):
        tb = ti % 2
        return ["xT%d_%d" % (tb, s) for s in range(NS)] if srcT is None else ["xT%d" % tb]

    def up(ti, f):
        b = f % 2
        xTc = xT[ti % 2]

        def upm(e):
            for k in range(8):
                e.matmul(pgu[b][:, 0:TT], wg[:, k, f * 128:(f + 1) * 128], xTc[:, k, :], start=(k == 0), stop=(k == 7))
            for k in range(8):
                ins = e.matmul(pgu[b][:, TT:2 * TT], wu[:, k, f * 128:(f + 1) * 128], xTc[:, k, :], start=(k == 0), stop=(k == 7))
            return ins
        P.op("pe", upm, reads=xT_reads(ti) + WGUH[0 if f < NF // 2 else 1], writes=["pgu%d" % b])
        P.op("act", lambda e: e.activation(sg[b][:], pgu[b][:, 0:TT], AF.Silu), reads=["pgu%d" % b], writes=["sg%d" % b])
        P.op("dve", lambda e: e.tensor_tensor(hT[:, f, :], sg[b][:], pgu[b][:, TT:2 * TT], ALU.mult),
             reads=["sg%d" % b, "pgu%d" % b], writes=["hT%d" % f])

    def down(ti, s, P=P):
        gi = ti * NS + s
        r0 = ti * TT + s * 128
        q = gi % 2

        def dn(e):
            for h in range(2):
                for f in range(NF):
                    ins = e.matmul(py[q][:, h * 512:(h + 1) * 512], hT[:, f, s * 128:(s + 1) * 128], wd[:, f, h * 512:(h + 1) * 512],
                                   start=(f == 0), stop=(f == NF - 1))
            return ins
        P.op("pe", dn, reads=["hT%d" % f for f in range(NF)] + WD, writes=["py%d" % q])
        P.dma("sp", x32[q][:], src32[r0:r0 + 128, :], writes=["x32_%d" % q], key="x32_%d" % q)
        P.op("dve", lambda e: e.scalar_tensor_tensor(z[q][:], py[q][:], float(cres), x32[q][:], ALU.mult, ALU.add),
             reads=["py%d" % q, "x32_%d" % q], writes=["z%d" % q])
        layer_norm(P, z[q], "z%d" % q, stats, mv, lng, lnb, eps, x32[q], "x32_%d" % q,
                   out_bf=(hb[q] if dstT is not None else None), out_bfn="hb%d" % q)
        P.dma("sp", dst32[r0:r0 + 128, :], x32[q][:], reads=["x32_%d" % q], writes=["dst32_%d" % gi], key="st32_%d" % q)

    def htrans(ti, s):
        if dstT is None:
            return
        gi = ti * NS + s
        q = gi % 2
        tb = ti % 2
        pTc = pT[q]
        for k in range(8):
            P.op("pe", lambda e, k=k: e.transpose(pTc[:, k, :], hb[q][:, k * 128:(k + 1) * 128], idt[:]),
                 reads=["hb%d" % q, "idt"], writes=["pT%d_%d" % (q, k)])
        P.op("act", lambda e: e.copy(hTt[tb][:, :, s * 128:(s + 1) * 128], pTc[:]),
             reads=["pT%d_%d" % (q, k) for k in range(8)], writes=["hTt%d_%d" % (tb, s)])
        if s == NS - 1:
            P.dma("sp", dstT[:, :, ti * TT:(ti + 1) * TT], hTt[tb][:], reads=["hTt%d_%d" % (tb, ss) for ss in range(NS)],
                  writes=["dstT_%d" % ti], key="stT_%d" % tb)

    load_x(0)
    load_weights()
    xtrans(0)
    late = []
    for ti in range(ntile):
        if ti + 1 < ntile:
            load_x(ti + 1)
        for f in range(NF):
            up(ti, f)
            if f >= 2 and late:
                _replay(P, [late.pop(0)])
            if f == 15 and ti > 0:
                assert not late
                for s in range(NS):
                    htrans(ti - 1, s)
        if ti + 1 < ntile:
            xtrans(ti + 1)
        for s in range(NS):
            if s == NS - 1 and ti + 1 < ntile:
                r = _Rec()
                down(ti, s, r)
                _replay(P, r.calls[:2])
                late = list(r.calls[2:])
                assert len(late) <= 13
            else:
                down(ti, s)
    for s in range(NS):
        htrans(ntile - 1, s)
    ph.close()


def layer_norm(P, zt, zn, stats, mv, lng, lnb, eps, out, outn, out_bf=None, out_bfn=None):
    for c in range(2):
        P.op("dve", lambda e, c=c: e.bn_stats(stats[:, c, :], zt[:, c * 512:(c + 1) * 512]), reads=[zn], writes=["stats%d" % c])
    P.op("dve", lambda e: e.bn_aggr(mv[:, 0:2], stats[:]), reads=["stats0", "stats1"], writes=["mv01"])
    P.op("dve", lambda e: e.tensor_scalar(mv[:, 2:3], mv[:, 1:2], float(eps), None, ALU.add), reads=["mv01"], writes=["mv2"])
    P.op("act", lambda e: e.activation(mv[:, 3:4], mv[:, 2:3], AF.Sqrt), reads=["mv2"], writes=["mv3"])
    P.op("dve", lambda e: e.reciprocal(mv[:, 4:5], mv[:, 3:4]), reads=["mv3"], writes=["mv4"])
    P.op("dve", lambda e: e.scalar_tensor_tensor(zt[:], zt[:], mv[:, 0:1], lng[:], ALU.subtract, ALU.mult),
         reads=[zn, "mv01", "lng"], writes=[zn])
    P.op("dve", lambda e: e.scalar_tensor_tensor(out[:], zt[:], mv[:, 4:5], lnb[:], ALU.mult, ALU.add),
         reads=[zn, "mv4", "lnb"], writes=[outn])
    if out_bf is not None:
        P.op("dve", lambda e: e.scalar_tensor_tensor(out_bf[:], zt[:], mv[:, 4:5], lnb[:], ALU.mult, ALU.add),
             reads=[zn, "mv4", "lnb"], writes=[out_bfn])


def attn_phase(nc, P, T, seq, h1T, W, CN, mixA, nqb=None):
    NCH = T // 128
    outer = Phase(nc, P)
    QT = outer.sb([128, 4, T], BF16)
    KT = outer.sb([128, 2, 2, T], BF16)
    VA = outer.sb([128, NCH, 2, 65], BF16)
    VAo = outer.sb([128, NCH, 2, 128], BF16)
    negm = outer.sb([128, 1], F32)
    ones32 = outer.sb([128, 128], BF16)
    ph = Phase(nc, P)
    wq = ph.sb([128, 8, 768], BF16)
    gq = ph.sb([128, 10, 64], F32)
    gm = ph.sb([128, 4], F32)
    rC = ph.sb([128, NCH, 64], F32)
    rS = ph.sb([128, NCH, 64], F32)
    idt = ph.sb([128, 128], BF16)
    hTt = [ph.sb([128, 8, 512], BF16) for _ in range(2)]
    sq2 = [ph.sb([128, 640], F32) for _ in range(2)]
    ss2 = [ph.sb([128, 40], F32) for _ in range(2)]
    qn2 = [ph.sb([128, 10, 64], F32) for _ in range(2)]
    t12 = [ph.sb([128, 10, 64], F32) for _ in range(2)]
    t22 = [ph.sb([128, 10, 64], F32) for _ in range(2)]
    qr = [ph.sb([128, 10, 64], BF16) for _ in range(2)]
    krs = [ph.sb([128, 2, 64], BF16) for _ in range(2)]
    pq = [ph.ps([128, 1024], F32) for _ in range(2)]
    pTq = [ph.ps([128, 4, 128], BF16) for _ in range(2)]
    pTk = [ph.ps([128, 2, 128], BF16) for _ in range(2)]
    for k in range(8):
        P.dma("pool", wq[:, k, :], W["w_in"][k * 128:(k + 1) * 128, 1568:2336], writes=["wq%d" % k], key="wq")
    WQ = ["wq%d" % k for k in range(8)]
    for h in range(10):
        P.dma("sp", gq[:, h, :], bcast_rows(W["q_norm_g"] if h < 8 else W["k_norm_g"], 128), writes=["gq%d" % h], key="gq")
    GQ = ["gq%d" % h for h in range(10)]
    P.dma("sp", rC[:], CN["ropeC"].rearrange("(n p) f -> p n f", p=128), writes=["rC"], key="rC")
    P.dma("sp", rS[:], CN["ropeS"].rearrange("(n p) f -> p n f", p=128), writes=["rS"], key="rS")
    P.dma("sp", idt[:], CN["ident"], writes=["idt"], key="idt")
    P.op("pool", lambda e: e.memset(VA[:, :, :, 64:65], 1.0), writes=["VAones"])

    P.op("pool", lambda e: e.memset(VAo[:, :, :, 0:64], 0.0), writes=["VAoones"])
    P.op("pool", lambda e: e.memset(VAo[:, :, :, 0:1], 1.0), writes=["VAoones"])
    P.op("pool", lambda e: e.memset(ones32[:], 1.0), writes=["ones32"])
    P.op("pool", lambda e: e.memset(KT[:], 0.0), writes=["KTz"])
    P.op("dve", lambda e: e.tensor_reduce(gm[:, 0:1], gq[:, 0, :], AX.X, ALU.max, apply_absolute_value=True), reads=GQ, writes=["gm0"])
    P.op("dve", lambda e: e.tensor_reduce(gm[:, 1:2], gq[:, 8, :], AX.X, ALU.max, apply_absolute_value=True), reads=GQ, writes=["gm1"])
    P.op("dve", lambda e: e.tensor_tensor(gm[:, 2:3], gm[:, 0:1], gm[:, 1:2], ALU.mult), reads=["gm0", "gm1"], writes=["gm2"])
    P.op("dve", lambda e: e.tensor_scalar(negm[:], gm[:, 2:3], -8.0, None, ALU.mult), reads=["gm2"], writes=["negm"])
    loads, recs = [], []
    for ti in range(T // 512):
        hc = hTt[ti % 2]
        hn = "hTt%d" % (ti % 2)
        PP = _Rec()
        PP.dma("sp", hc[:], h1T[:, :, seq * T + ti * 512: seq * T + (ti + 1) * 512], writes=[hn], key=hn)
        loads.append(PP.calls)
        for s in range(4):
            PP = _Rec()
            recs.append(PP.calls)
            ci = ti * 4 + s
            b = ci % 2
            sq, ss, qn, t1, t2 = sq2[b], ss2[b], qn2[b], t12[b], t22[b]
            tok = slice(ci * 128, (ci + 1) * 128)

            def proj(e, s=s, b=b, hc=hc):
                for k in range(8):
                    e.matmul(pq[b][:, 0:512], hc[:, k, s * 128:(s + 1) * 128], wq[:, k, 0:512], start=(k == 0), stop=(k == 7))
                for k in range(8):
                    ins = e.matmul(pq[b][:, 512:768], hc[:, k, s * 128:(s + 1) * 128], wq[:, k, 512:768], start=(k == 0), stop=(k == 7))
                return ins
            PP.op("pe", proj, reads=[hn] + WQ, writes=["pq%d" % b])
            PP.op("act", lambda e, b=b, sq=sq, ss=ss, qn=qn, t1=t1, t2=t2: e.activation(sq[:], pq[b][:, 0:640], AF.Square), reads=["pq%d" % b], writes=["sq_" + str(b)])
            PP.op("dve", lambda e, sq=sq, ss=ss, qn=qn, t1=t1, t2=t2: e.tensor_reduce(ss[:, 0:10], sq[:].rearrange("p (h d) -> p h d", d=64), AX.X, ALU.add), reads=["sq_" + str(b)], writes=["ss0_" + str(b)])
            PP.op("dve", lambda e, sq=sq, ss=ss, qn=qn, t1=t1, t2=t2: e.tensor_scalar(ss[:, 10:20], ss[:, 0:10], 1.0 / 64, 1e-6, ALU.mult, ALU.add), reads=["ss0_" + str(b)], writes=["ss1_" + str(b)])
            PP.op("act", lambda e, sq=sq, ss=ss, qn=qn, t1=t1, t2=t2: e.activation(ss[:, 20:30], ss[:, 10:20], AF.Sqrt), reads=["ss1_" + str(b)], writes=["ss2_" + str(b)])
            PP.op("dve", lambda e, sq=sq, ss=ss, qn=qn, t1=t1, t2=t2: e.reciprocal(ss[:, 30:40], ss[:, 20:30]), reads=["ss2_" + str(b)], writes=["ss3_" + str(b)])
            PP.op("dve", lambda e, b=b, sq=sq, ss=ss, qn=qn, t1=t1, t2=t2: e.tensor_tensor(qn[:], pq[b][:, 0:640].rearrange("p (h d) -> p h d", d=64),
                                                       ss[:, 30:40].unsqueeze(2).broadcast_to([128, 10, 64]), ALU.mult),
                 reads=["pq%d" % b, "ss3_" + str(b)], writes=["qn_" + str(b)])
            PP.op("act", lambda e, b=b, ci=ci, sq=sq, ss=ss, qn=qn, t1=t1, t2=t2: e.copy(VA[:, ci, :, 0:64], pq[b][:, 640:768].rearrange("p (h d) -> p h d", d=64)),
                 reads=["pq%d" % b], writes=["VA%d" % ci])
            PP.op("act", lambda e, b=b, ci=ci: e.copy(VAo[:, ci, :, 64:128], pq[b][:, 640:768].rearrange("p (h d) -> p h d", d=64)),
                 reads=["pq%d" % b, "VAoones"], writes=["VAo%d" % ci])
            PP.op("pool", lambda e, sq=sq, ss=ss, qn=qn, t1=t1, t2=t2: e.tensor_tensor(qn[:], qn[:], gq[:], ALU.mult), reads=["qn_" + str(b)] + GQ, writes=["qn_" + str(b)])
            PP.op("pool", lambda e, ci=ci, sq=sq, ss=ss, qn=qn, t1=t1, t2=t2: e.tensor_tensor(t1[:], qn[:], rC[:, ci, :].unsqueeze(1).broadcast_to([128, 10, 64]), ALU.mult),
                 reads=["qn_" + str(b), "rC"], writes=["t1_" + str(b)])
            qv = qn[:].rearrange("p h (a c f) -> p h a c f", a=2, c=2)
            tv = t2[:].rearrange("p h (a c f) -> p h a c f", a=2, c=2)

            def rot(e, ci=ci, qv=qv, tv=tv):
                sv = rS[:, ci, :].rearrange("p (a c f) -> p a c f", a=2, c=2)
                e.tensor_tensor(tv[:, :, :, 0, :], qv[:, :, :, 1, :], sv[:, :, 0, :].unsqueeze(1).broadcast_to([128, 10, 2, 16]), ALU.mult)
                return e.tensor_tensor(tv[:, :, :, 1, :], qv[:, :, :, 0, :], sv[:, :, 1, :].unsqueeze(1).broadcast_to([128, 10, 2, 16]), ALU.mult)
            PP.op("dve", rot, reads=["qn_" + str(b), "rS"], writes=["t2_" + str(b)])
            PP.op("dve", lambda e, b=b, sq=sq, ss=ss, qn=qn, t1=t1, t2=t2: e.tensor_tensor(qr[b][:], t1[:], t2[:], ALU.add), reads=["t1_" + str(b), "t2_" + str(b)], writes=["qr%d" % b])

            def kswap(e, b=b):
                e.tensor_copy(krs[b][:, 0, :], qr[b][:, 9, :])
                return e.tensor_copy(krs[b][:, 1, :], qr[b][:, 8, :])
            PP.op("pool", kswap, reads=["qr%d" % b], writes=["krs%d" % b])

            def tq(e, b=b):
                for pr in range(4):
                    ins = e.transpose(pTq[b][:, pr, :], qr[b][:, 2 * pr:2 * pr + 2, :].rearrange("p h d -> p (h d)"), idt[:])
                return ins
            PP.op("pe", tq, reads=["qr%d" % b, "idt"], writes=["pTq%d" % b])

            def tk(e, b=b):
                e.transpose(pTk[b][:, 0, :], qr[b][:, 8:10, :].rearrange("p h d -> p (h d)"), idt[:])
                return e.transpose(pTk[b][:, 1, :], krs[b][:].rearrange("p h d -> p (h d)"), idt[:])
            PP.op("pe", tk, reads=["qr%d" % b, "krs%d" % b, "idt"], writes=["pTk%d" % b])
            PP.op("act", lambda e, b=b, tok=tok, sq=sq, ss=ss, qn=qn, t1=t1, t2=t2: e.copy(QT[:, :, tok], pTq[b][:]), reads=["pTq%d" % b], writes=["QT%d" % ci])

            def kcp(e, b=b, tok=tok):
                e.tensor_copy(KT[0:64, :, 0, tok], pTk[b][0:64, :, :])
                e.tensor_copy(KT[64:128, 0, 1, tok], pTk[b][64:128, 1, :])
                return e.tensor_copy(KT[64:128, 1, 1, tok], pTk[b][64:128, 0, :])
            PP.op("dve", kcp, reads=["pTk%d" % b, "KTz"], writes=["KT%d" % ci])
    def _split(calls):
        k = next(i for i, c in enumerate(calls) if any(str(w).startswith("krs") for w in c[2].get("writes", ())))
        return calls[:1], calls[1:k], calls[k:]
    parts = [_split(c) for c in recs]
    NR = len(parts)

    def _inter(A, B):
        out, ia, ib = [], 0, 0
        while ia < len(A) or ib < len(B):
            if ia < len(A):
                out.append(A[ia])
                ia += 1
            if ib < len(B):
                out.append(B[ib])
                ib += 1
        return out
    _replay(P, loads[0])
    if len(loads) > 1:
        _replay(P, loads[1])
    _replay(P, parts[0][0])
    if NR > 1:
        _replay(P, parts[1][0])
    for p in range(0, NR, 2):
        _replay(P, _inter(parts[p][1], parts[p + 1][1] if p + 1 < NR else []))
        if (p + 2) % 4 == 0 and (p + 2) // 4 + 1 < len(loads):
            _replay(P, loads[(p + 2) // 4 + 1])
        for c in (p + 2, p + 3):
            if c < NR:
                _replay(P, parts[c][0])
        for c in (p, p + 1):
            if c < NR:
                _replay(P, parts[c][2])
    ph.close()
    ph = Phase(nc, P)
    PT = [ph.sb([128, 1024], BF16) for _ in range(2)]
    rd = [ph.sb([128, 512], F32) for _ in range(2)]
    rres = [ph.sb([128, 512], F32) for _ in range(2)]
    rb16 = [ph.sb([128, 2, 512], BF16) for _ in range(2)]
    o32 = [ph.sb([128, 512], F32) for _ in range(2)]
    ob = [ph.sb([128, 4, 512], BF16) for _ in range(2)]
    pS = [ph.ps([128, 1024], F32) for _ in range(2)]
    po = [ph.ps([128, 512], F32) for _ in range(2)]
    pb = ph.ps([128, 512], F32)
    nsg = NCH // 2
    groups = [(qb, h, sgi) for qb in range(nqb if nqb is not None else T // 512) for h in range(8) for sgi in range(nsg)]

    def emit_qk_exp(i):
        qb, h, sgi = groups[i]
        b = i % 2
        pr, base, kv = h // 2, (h % 2) * 64, h // 4
        qsl = slice(qb * 512, (qb + 1) * 512)

        def qk(e):
            for j in range(2):
                c = 2 * sgi + j
                ins = e.matmul(pS[b][:, j * 512:(j + 1) * 512], KT[:, kv, h % 2, c * 128:(c + 1) * 128],
                               QT[:, pr, qsl], start=True, stop=True)
            return ins
        P.op("pe", qk, reads=[], writes=["pS%d" % b])
        P.op("act", lambda e: e.activation(PT[b][:], pS[b][:], AF.Exp, bias=negm[:, 0:1], scale=0.125),
             reads=["pS%d" % b], writes=["PT%d" % b])

    pending = []

    def emit_pv(i):
        qb, h, sgi = groups[i]
        b = i % 2
        kv = h // 4
        hb_ = (qb * 8 + h) % 2
        poc = po[hb_]
        pon = "po%d" % hb_

        odd = h % 2
        pr = h // 2
        MO = 128 if odd else 65
        dp = 0 if odd else 64
        lo = 64 if odd else 0

        def pv(e):
            for j in range(2):
                c = 2 * sgi + j
                ins = e.matmul(poc[0:MO, :], (VAo if odd else VA)[:, c, kv, :], PT[b][:, j * 512:(j + 1) * 512],
                               start=(sgi == 0 and j == 0), stop=(sgi == nsg - 1 and j == 1))
            return ins
        P.op("pe", pv, reads=["PT%d" % b], writes=[pon] if sgi in (0, nsg - 1) else [])
        if sgi == nsg - 1:
            obc = ob[qb % 2]
            obn = "ob%d" % (qb % 2)
            rdc, oc = rd[hb_], o32[hb_]
            rrc, rbc = rres[hb_], rb16[hb_]

            def dfin(e):
                e.tensor_copy(oc[lo:lo + 64, :], poc[lo:lo + 64, :])
                return e.tensor_copy(rbc[dp:dp + 1, 0, :], poc[dp:dp + 1, :])
            P.op("dve", dfin, reads=[pon], writes=["rd%d" % hb_, "o32_%d" % hb_])
            P.op("dve", lambda e: e.tensor_tensor(rrc[dp:dp + 1, :], poc[dp:dp + 1, :], rbc[dp:dp + 1, 0, :], ALU.subtract),
                 reads=["rd%d" % hb_, pon], writes=["rres%d" % hb_])
            P.op("dve", lambda e: e.tensor_copy(rbc[dp:dp + 1, 1, :], rrc[dp:dp + 1, :]), reads=["rres%d" % hb_], writes=["rb%d" % hb_])

            def fin():
                MB = lo + 64

                def bc(e):
                    e.matmul(pb[0:MB, :], ones32[dp:dp + 1, 0:MB], rbc[dp:dp + 1, 0, :], start=True, stop=False)
                    return e.matmul(pb[0:MB, :], ones32[dp:dp + 1, 0:MB], rbc[dp:dp + 1, 1, :], start=False, stop=True)
                P.op("pe", bc, reads=["rd%d" % hb_, "rb%d" % hb_], writes=["pb"])
                P.op("dve", lambda e: e.reciprocal(rrc[lo:lo + 64, :], pb[lo:lo + 64, :]), reads=["pb", "rb%d" % hb_], writes=["rec%d" % hb_])
                P.op("dve", lambda e: e.tensor_tensor(obc[lo:lo + 64, pr, :], oc[lo:lo + 64, :], rrc[lo:lo + 64, :], ALU.mult),
                     reads=["o32_%d" % hb_, "rec%d" % hb_], writes=[obn + "_%d" % h])
                if h == 7:
                    P.dma("sp", mixA[:, :, seq * T + qb * 512: seq * T + (qb + 1) * 512], obc[:],
                          reads=[obn + "_%d" % hh for hh in range(8)], writes=["mixA%d" % qb], key=obn)
            pending.append((i + min(6, nsg), fin))

    n = len(groups)
    for i in range(n + 1):
        if i < n:
            emit_qk_exp(i)
        while pending and pending[0][0] <= i:
            pending.pop(0)[1]()
        if i >= 1:
            emit_pv(i - 1)
    while pending:
        pending.pop(0)[1]()
    ph.close()
    outer.es.close()


def gla_phase(nc, P, T, seq, h1T, W, CN, SC, half=False):
    NCH = T // 128
    base = seq * T
    gp_d, la_d, of_d, mixG = SC["gp"], SC["la"], SC["of"], SC["mixG"]
    ph = Phase(nc, P)
    wq = ph.sb([128, 8, 1568], BF16)
    w2 = ph.sb([16, 2, 256], BF16)
    b2 = ph.sb([128, 512], F32)
    one1 = ph.sb([128, 1], F32)
    idt = ph.sb([128, 128], BF16)
    hTt = [ph.sb([128, 8, 512], BF16) for _ in range(2)]
    gpb = [ph.sb([128, 1536], BF16) for _ in range(2)]
    zb16 = ph.sb([128, 32], BF16)
    thb = ph.sb([128, 512], F32)
    zT = ph.sb([16, 2, 128], BF16)
    xs = ph.sb([128, 512], F32)
    e1 = ph.sb([128, 512], F32)
    la = [ph.sb([128, 512], F32) for _ in range(2)]
    pq = ph.ps([128, 2048], F32)
    pz = ph.ps([128, 2, 128], BF16)
    px = ph.ps([128, 512], F32)
    for k in range(8):
        P.dma("pool", wq[:, k, :], W["w_in"][k * 128:(k + 1) * 128, 0:1568], writes=["wq%d" % k], key="wq")
    WQ = ["wq%d" % k for k in range(8)]
    P.dma("pool", w2[:, 0, :], W["gla_w2f"], writes=["w2f"], key="w2")
    P.dma("pool", w2[:, 1, :], W["gla_w2b"], writes=["w2b"], key="w2")
    P.dma("sp", b2[:, 0:256], bcast_rows(W["gla_b2f"], 128), writes=["b2f"], key="b2")
    P.dma("sp", b2[:, 256:512], bcast_rows(W["gla_b2b"], 128), writes=["b2b"], key="b2")
    P.dma("sp", idt[:], CN["ident"], writes=["idt"], key="idt")
    P.op("pool", lambda e: e.memset(one1[:], 1.0), writes=["one1"])
    def g0_load(ti):
        P.dma("sp", hTt[ti % 2][:], h1T[:, :, base + ti * 512: base + (ti + 1) * 512], writes=["hTt%d" % (ti % 2)], key="hTt%d" % (ti % 2))
    g0_load(0)
    for ti in range(T // 512):
        hc = hTt[ti % 2]
        hn = "hTt%d" % (ti % 2)
        if ti + 1 < T // 512:
            g0_load(ti + 1)
        for s in range(4):
            ci = ti * 4 + s
            b = ci % 2
            r0 = base + ci * 128

            def proj(e, s=s, hc=hc):
                for (c0, c1) in ((0, 512), (512, 1024), (1024, 1536), (1536, 1568)):
                    for k in range(8):
                        ins = e.matmul(pq[:, c0:c1], hc[:, k, s * 128:(s + 1) * 128], wq[:, k, c0:c1], start=(k == 0), stop=(k == 7))
                return ins
            P.op("pe", proj, reads=[hn] + WQ, writes=["pq"])
            P.op("dve", lambda e: e.tensor_copy(zb16[:], pq[:, 1536:1568]), reads=["pq"], writes=["zb16"])
            P.op("act", lambda e, b=b: e.copy(gpb[b][:], pq[:, 0:1536]), reads=["pq"], writes=["gpbA%d" % b])

            def tz(e):
                e.transpose(pz[0:16, 0, :], zb16[:, 0:16], idt[:])
                return e.transpose(pz[0:16, 1, :], zb16[:, 16:32], idt[:])
            P.op("pe", tz, reads=["zb16", "idt"], writes=["pz"])
            P.op("dve", lambda e: e.tensor_copy(zT[:], pz[0:16, :, :]), reads=["pz"], writes=["zT"])

            def xm(e):
                e.matmul(px[:, 0:256], zT[:, 0, :], w2[:, 0, :], start=True, stop=True)
                return e.matmul(px[:, 256:512], zT[:, 1, :], w2[:, 1, :], start=True, stop=True)
            P.op("pe", xm, reads=["zT", "w2f", "w2b"], writes=["px"])
            P.op("dve", lambda e: e.tensor_tensor(xs[:], px[:], b2[:], ALU.add), reads=["px", "b2f", "b2b"], writes=["xs"])
            P.op("act", lambda e: e.activation(e1[:], xs[:], AF.Exp, scale=-1.0), reads=["xs"], writes=["e1"])
            P.op("act", lambda e, b=b: e.activation(thb[:], gpb[b][:, 1024:1536], AF.Tanh, scale=0.5), reads=["gpbA%d" % b], writes=["thb"])
            P.op("act", lambda e, b=b: e.activation(la[b][:], e1[:], AF.Ln, bias=one1[:, 0:1]), reads=["e1", "one1"], writes=["la%d" % b])
            P.op("dve", lambda e: e.tensor_scalar(thb[:], thb[:], 0.5, 0.5, ALU.mult, ALU.add), reads=["thb"], writes=["thb"])
            P.op("dve", lambda e, b=b: e.tensor_tensor(gpb[b][:, 1024:1536], thb[:], gpb[b][:, 1024:1536], ALU.mult),
                 reads=["thb", "gpbA%d" % b], writes=["gpb%d" % b])
            P.dma("sp", gp_d[r0:r0 + 128, :], gpb[b][:], reads=["gpb%d" % b, "gpbA%d" % b], writes=["gp%d" % ci], key="gpb%d" % b)
            P.dma("sp", la_d[r0:r0 + 128, :], la[b][:], reads=["la%d" % b], writes=["lad%d" % ci], key="la%d" % b)
    ph.close()
    for dirn in range(2):
        ph = Phase(nc, P)
        idt = ph.sb([128, 128], BF16)
        Mc = ph.sb([128, 3, 128], BF16)
        Ind = ph.sb([128, 16], BF16)
        lhi = ph.sb([128, 256], BF16)
        llo = ph.sb([128, 256], BF16)
        lres = ph.sb([128, 256], F32)
        mskT = ph.sb([128, 128], F32)
        lq = ph.sb([128, 1], F32)
        gng = ph.sb([128, 4, 128], F32)
        gpt = [ph.sb([128, 1536], BF16) for _ in range(3)]
        lat = [ph.sb([128, 256], F32) for _ in range(2)]
        E = ph.sb([128, 4, 256], F32)
        dec = [ph.sb([64, 4, 16], F32) for _ in range(2)]
        prod = [ph.sb([128, 4, 256], BF16) for _ in range(2)]
        qdT = ph.sb([64, 4, 128], BF16)
        kdT = ph.sb([64, 4, 128], BF16)
        qeT = [ph.sb([64, 4, 2, 128], BF16) for _ in range(2)]
        scT = [ph.sb([128, 4, 128], BF16) for _ in range(2)]
        S = ph.sb([64, 4, 128], F32)
        Sb = [ph.sb([64, 4, 128], BF16) for _ in range(2)]
        of32 = [ph.sb([128, 512], F32) for _ in range(3)]
        pc = ph.ps([128, 1024], F32)
        pT = ph.ps([128, 3, 4, 128], F32)
        pSc = ph.ps([128, 4, 128], F32)
        po = ph.ps([128, 4, 128], F32)
        pU = ph.ps([128, 4, 128], F32)
        if dirn == 1:
            sq = ph.sb([128, 512], F32)
            ss = ph.sb([128, 16], F32)
            er = ph.sb([128, 512], F32)
            ob16 = [ph.sb([128, 512], BF16) for _ in range(2)]
            mT = [ph.sb([128, 4, 128], BF16) for _ in range(2)]
            for h in range(4):
                P.dma("sp", gng[:, h, :], bcast_rows(W["gla_gn_g"], 128), writes=["gng%d" % h], key="gng")
        GNG = ["gng%d" % h for h in range(4)]
        P.dma("sp", idt[:], CN["ident"], writes=["idt"], key="idt")
        for m in range(3):
            P.dma("pool", Mc[:, m, :], CN["glaM"][dirn, m], writes=["Mc%d" % m], key="Mc")
        P.dma("pool", Ind[:], CN["glaInd"], writes=["Ind"], key="Ind")
        P.dma("sp", mskT[:], CN["glaMask"][dirn], writes=["mskT"], key="mskT")
        Q = [P]
        P.op("pool", lambda e: e.memset(lq[:], math.log(0.125)), writes=["lq"])
        P.op("pool", lambda e: e.memset(qeT[0][:], 0.0), writes=["qeT0"])
        P.op("pool", lambda e: e.memset(qeT[1][:], 0.0), writes=["qeT1"])
        P.op("pool", lambda e: e.memset(S[:], 0.0), writes=["S"])
        P.op("pool", lambda e: e.memset(Sb[0][:], 0.0), writes=["Sb0"])
        NOUT = NCH // 2 if half else NCH
        order = list(range(NOUT)) if dirn == 0 else list(range(NCH - 1, -1, -1))
        NU = len(order)
        c0, c1 = (0, 1) if dirn == 0 else (1, 0)

        def prep(n):
            ci = order[n]
            b = n % 2
            b3 = n % 3
            r0 = base + ci * 128
            gt = gpt[b3]
            gtn = "gpt%d" % b3
            Q[0].dma("sp", gt[:], gp_d[r0:r0 + 128, :], writes=[gtn], key=gtn)
            Q[0].dma("sp", lat[b][:], la_d[r0:r0 + 128, dirn * 256:(dirn + 1) * 256], writes=["lat%d" % b], key="lat%d" % b)
            if dirn == 1 and ci < NOUT:
                Q[0].dma("sp", of32[b3][:], of_d[r0:r0 + 128, :], writes=["of32_%d" % b3], key="ofl%d" % b3)
            Q[0].op("dve", lambda e: e.tensor_copy(lhi[:], lat[b][:]), reads=["lat%d" % b], writes=["lhi"])
            Q[0].op("dve", lambda e: e.tensor_tensor(lres[:], lat[b][:], lhi[:], ALU.subtract), reads=["lat%d" % b, "lhi"], writes=["lres"])
            Q[0].op("dve", lambda e: e.tensor_copy(llo[:], lres[:]), reads=["lres"], writes=["llo"])

            def cm(e):
                for m in range(3):
                    e.matmul(pc[:, m * 256:(m + 1) * 256], Mc[:, m, :], lhi[:], start=True, stop=False)
                    ins = e.matmul(pc[:, m * 256:(m + 1) * 256], Mc[:, m, :], llo[:], start=False, stop=True)
                return ins
            Q[0].op("pe", cm, reads=["lhi", "llo", "Mc0", "Mc1", "Mc2"], writes=["pc"])

            def dl(e):
                for h in range(4):
                    e.matmul(pc[0:64, 768 + h * 16:768 + (h + 1) * 16], lhi[:, h * 64:(h + 1) * 64], Ind[:], start=True, stop=False)
                    ins = e.matmul(pc[0:64, 768 + h * 16:768 + (h + 1) * 16], llo[:, h * 64:(h + 1) * 64], Ind[:], start=False, stop=True)
                return ins
            Q[0].op("pe", dl, reads=["lhi", "llo", "Ind"], writes=["pdl"])
            Q[0].op("act", lambda e: e.activation(E[:, 0, :], pc[:, 0:256], AF.Exp, bias=lq[:, 0:1]), reads=["pc", "pdl", "lq"], writes=["E0"])
            Q[0].op("act", lambda e: e.activation(E[:, 1, :], pc[:, 0:256], AF.Exp, scale=-1.0), reads=["pc", "pdl"], writes=["E1"])
            Q[0].op("act", lambda e: e.activation(E[:, 2, :], pc[:, 512:768], AF.Exp, bias=lq[:, 0:1]), reads=["pc", "pdl", "lq"], writes=["E2"])
            Q[0].op("act", lambda e: e.activation(E[:, 3, :], pc[:, 256:512], AF.Exp), reads=["pc", "pdl"], writes=["E3"])
            Q[0].op("act", lambda e: e.activation(dec[b][:].rearrange("p h c -> p (h c)"), pc[0:64, 768:832], AF.Exp), reads=["pdl", "pc"], writes=["dec%d" % b])
            pr = prod[b]
            Q[0].op("dve", lambda e: e.tensor_tensor(pr[:, 0, :], gt[:, 0:256], E[:, 0, :], ALU.mult), reads=[gtn, "E0"], writes=["qd%d" % b])
            Q[0].op("dve", lambda e: e.tensor_tensor(pr[:, 1, :], gt[:, 256:512], E[:, 1, :], ALU.mult), reads=[gtn, "E1"], writes=["kd%d" % b])
            Q[0].op("dve", lambda e: e.tensor_tensor(pr[:, 2, :], gt[:, 0:256], E[:, 2, :], ALU.mult), reads=[gtn, "E2"], writes=["qe%d" % b])
            Q[0].op("dve", lambda e: e.tensor_tensor(pr[:, 3, :], gt[:, 256:512], E[:, 3, :], ALU.mult), reads=[gtn, "E3"], writes=["ku%d" % b])

            def tr(e):
                for m in range(3):
                    for h in range(4):
                        ins = e.matmul(pT[0:64, m, h, :], pr[:, m, h * 64:(h + 1) * 64], idt[:], start=True, stop=True)
                return ins
            Q[0].op("pe", tr, reads=["qd%d" % b, "kd%d" % b, "qe%d" % b, "idt"], writes=["pT"])
            Q[0].op("act", lambda e: e.copy(qdT[:], pT[0:64, 0, :, :]), reads=["pT"], writes=["qdT"])
            Q[0].op("act", lambda e: e.copy(kdT[:], pT[0:64, 1, :, :]), reads=["pT"], writes=["kdT"])

            def qecp(e):
                e.copy(qeT[b][:, :, 0, 0:64], pT[0:64, 2, :, 0:64])
                return e.copy(qeT[b][:, :, 1, 64:128], pT[0:64, 2, :, 64:128])
            Q[0].op("act", qecp, reads=["pT"], writes=["qeT%d" % b])

            def sc(e):
                for h in range(4):
                    ins = e.matmul(pSc[:, h, :], kdT[:, h, :], qdT[:, h, :], start=True, stop=True)
                return ins
            Q[0].op("pe", sc, reads=["qdT", "kdT"], writes=["pSc"])
            Q[0].op("dve", lambda e: e.tensor_tensor(scT[b][:], pSc[:], mskT[:].unsqueeze(1).broadcast_to([128, 4, 128]), ALU.mult),
                 reads=["pSc", "mskT"], writes=["scT%d" % b])

        def state(n):
            ci = order[n]
            b = n % 2
            b3 = n % 3
            r0 = base + ci * 128
            gt = gpt[b3]
            gtn = "gpt%d" % b3
            pr = prod[b]

            def upd(c):
                def f(e):
                    for h in range(4):
                        ins = e.matmul(pU[0:64, h, :], pr[c * 64:(c + 1) * 64, 3, h * 64:(h + 1) * 64],
                                       gt[c * 64:(c + 1) * 64, 512 + h * 128:512 + (h + 1) * 128], start=True, stop=True)
                    return ins
                return f

            def supd(c, dst, dstn):
                def sfused(e):
                    for h in range(4):
                        ins = e.scalar_tensor_tensor(S[:, h, :], S[:, h, :], dec[b][:, h, c:c + 1], pU[0:64, h, :], ALU.mult, ALU.add)
                    return ins
                Q[0].op("dve", sfused, reads=["S", "dec%d" % b, "pU"], writes=["S"])
                Q[0].op("act", lambda e: e.copy(dst[:], S[:]), reads=["S"], writes=[dstn])
            Q[0].op("pe", upd(c0), reads=["ku%d" % b, gtn], writes=["pU"])
            supd(c0, Sb[1], "Sb1")
            if ci >= NOUT:
                Q[0].op("pe", upd(c1), reads=["ku%d" % b, gtn], writes=["pU"])
                supd(c1, Sb[0], "Sb0")
                return

            def outm(e):
                for h in range(4):
                    e.matmul(po[:, h, :], scT[b][:, h, :], gt[:, 512 + h * 128:512 + (h + 1) * 128], start=True, stop=False)
                    e.matmul(po[:, h, :], qeT[b][:, h, c0, :], Sb[0][:, h, :], start=False, stop=False)
                    ins = e.matmul(po[:, h, :], qeT[b][:, h, c1, :], Sb[1][:, h, :], start=False, stop=True)
                return ins
            Q[0].op("pe", outm, reads=["scT%d" % b, gtn, "qeT%d" % b, "Sb0", "Sb1"], writes=["po"])
            Q[0].op("pe", upd(c1), reads=["ku%d" % b, gtn], writes=["pU"])
            supd(c1, Sb[0], "Sb0")
            o = of32[b3]
            on = "of32_%d" % b3
            if dirn == 0:
                Q[0].op("act", lambda e: e.copy(o[:], po[:].rearrange("p h e -> p (h e)")), reads=["po"], writes=[on])
                Q[0].dma("sp", of_d[r0:r0 + 128, :], o[:], reads=[on], writes=["ofd%d" % ci], key="ofs%d" % b3)
            else:
                ov = o[:].rearrange("p (h e) -> p h e", e=128)
                Q[0].op("dve", lambda e: e.tensor_tensor(o[:], o[:], po[:].rearrange("p h e -> p (h e)"), ALU.add), reads=[on, "po"], writes=[on])
                Q[0].op("act", lambda e: e.activation(sq[:], o[:], AF.Square), reads=[on], writes=["sq"])
                Q[0].op("dve", lambda e: e.tensor_reduce(ss[:, 0:4], sq[:].rearrange("p (h e) -> p h e", e=128), AX.X, ALU.add), reads=["sq"], writes=["ss0"])
                Q[0].op("dve", lambda e: e.tensor_scalar(ss[:, 4:8], ss[:, 0:4], 1.0 / 128, 1e-5, ALU.mult, ALU.add), reads=["ss0"], writes=["ss1"])
                Q[0].op("act", lambda e: e.activation(ss[:, 8:12], ss[:, 4:8], AF.Sqrt), reads=["ss1"], writes=["ss2"])
                Q[0].op("dve", lambda e: e.reciprocal(ss[:, 12:16], ss[:, 8:12]), reads=["ss2"], writes=["ss3"])
                Q[0].op("dve", lambda e: e.tensor_tensor(ov, ov, ss[:, 12:16].unsqueeze(2).broadcast_to([128, 4, 128]), ALU.mult), reads=[on, "ss3"], writes=[on])
                Q[0].op("pool", lambda e: e.tensor_tensor(o[:], o[:], gng[:].rearrange("p h e -> p (h e)"), ALU.mult), reads=[on] + GNG, writes=[on])
                Q[0].op("dve", lambda e: e.tensor_tensor(ob16[b][:], o[:], gt[:, 1024:1536], ALU.mult), reads=[on, gtn], writes=["ob16_%d" % b])

        def tail(n):
            if dirn == 0 or order[n] >= NOUT:
                return
            ci = order[n]
            b = n % 2
            r0 = base + ci * 128

            def to(e):
                for c in range(4):
                    ins = e.matmul(pT[:, 0, c, :], ob16[b][:, c * 128:(c + 1) * 128], idt[:], start=True, stop=True)
                return ins
            Q[0].op("pe", to, reads=["ob16_%d" % b, "idt"], writes=["pT"])
            Q[0].op("act", lambda e: e.copy(mT[b][:], pT[:, 0, :, :]), reads=["pT"], writes=["mT%d" % b])
            Q[0].dma("sp", mixG[:, :, r0:r0 + 128], mT[b][:], reads=["mT%d" % b], writes=["mixG%d" % ci], key="mT%d" % b)

        def rec(fn, n):
            r = _Rec()
            Q[0] = r
            fn(n)
            Q[0] = P
            return r.calls

        def interleave(A, B):
            out, ia, ib, na, nb = [], 0, 0, len(A), len(B)
            while ia < na or ib < nb:
                if ib >= nb or (ia < na and ia * nb <= ib * na):
                    out.append(A[ia])
                    ia += 1
                else:
                    out.append(B[ib])
                    ib += 1
            return out

        preps = [rec(prep, n) for n in range(NU)]
        loads = [[c for c in calls if c[0] == "dma"] for calls in preps]
        comps = [[c for c in calls if c[0] != "dma"] for calls in preps]
        _replay(P, loads[0])
        if NU > 1:
            _replay(P, loads[1])
        _replay(P, comps[0])
        for n in range(NU):
            if n + 2 < NU:
                _replay(P, loads[n + 2])
            _replay(P, interleave(comps[n + 1] if n + 1 < NU else [], rec(state, n)))
            if n >= 1:
                tail(n - 1)
        tail(NU - 1)
        ph.close()


def wout_phase(nc, P, NT, W, CN, SC, h1, h2, h2T):
    mixG, mixA = SC["mixG"], SC["mixA"]
    ph = Phase(nc, P)
    wog = ph.sb([128, 4, D], BF16)
    woa = ph.sb([128, 4, D], BF16)
    lng = ph.sb([128, D], F32)
    lnb = ph.sb([128, D], F32)
    idt = ph.sb([128, 128], BF16)
    mG = [ph.sb([128, 4, 512], BF16) for _ in range(2)]
    mA = [ph.sb([128, 4, 512], BF16) for _ in range(2)]
    x32 = [ph.sb([128, D], F32) for _ in range(3)]
    z = [ph.sb([128, D], F32) for _ in range(2)]
    hb = [ph.sb([128, D], BF16) for _ in range(3)]
    hTt = [ph.sb([128, 8, 512], BF16) for _ in range(2)]
    stats = ph.sb([128, 2, 6], F32)
    mv = ph.sb([128, 8], F32)
    py = [ph.ps([128, 1024], F32) for _ in range(2)]
    pT = [ph.ps([128, 8, 128], BF16) for _ in range(2)]
    for c in range(4):
        P.dma("pool", wog[:, c, :], W["w_out"][c * 128:(c + 1) * 128, :], writes=["wog%d" % c], key="wog")
    for c in range(4):
        P.dma("pool", woa[:, c, :], W["w_out"][512 + c * 128:512 + (c + 1) * 128, :], writes=["woa%d" % c], key="woa")
    WO = ["wog%d" % c for c in range(4)] + ["woa%d" % c for c in range(4)]
    P.dma("sp", lng[:], bcast_rows(W["ln2_g"], 128), writes=["lng"], key="lng")
    P.dma("sp", lnb[:], bcast_rows(W["ln2_b"], 128), writes=["lnb"], key="lnb")
    P.dma("sp", idt[:], CN["ident"], writes=["idt"], key="idt")
    def load_tile(ti):
        t0 = ti * 512
        tb = ti % 2
        P.dma("sp", mG[tb][:], mixG[:, :, t0:t0 + 512], writes=["mG%d" % tb], key="mG%d" % tb)
        P.dma("sp", mA[tb][:], mixA[:, :, t0:t0 + 512], writes=["mA%d" % tb], key="mA%d" % tb)

    def main(ti, s):
        t0 = ti * 512
        tb = ti % 2
        gi = ti * 4 + s
        q = gi % 2
        r0 = t0 + s * 128

        def om(e):
            for hf in range(2):
                cs = slice(hf * 512, (hf + 1) * 512)
                for c in range(4):
                    e.matmul(py[q][:, cs], mG[tb][:, c, s * 128:(s + 1) * 128], wog[:, c, cs], start=(c == 0), stop=False)
                for c in range(4):
                    ins = e.matmul(py[q][:, cs], mA[tb][:, c, s * 128:(s + 1) * 128], woa[:, c, cs], start=False, stop=(c == 3))
            return ins
        P.op("pe", om, reads=["mG%d" % tb, "mA%d" % tb] + WO, writes=["py%d" % q])
        q3 = gi % 3
        xq = x32[q3]
        P.op("dve", lambda e: e.scalar_tensor_tensor(z[q][:], py[q][:], 1.0 / ALPHA, xq[:], ALU.mult, ALU.add),
             reads=["py%d" % q, "x32_%d" % q3], writes=["z%d" % q])
        layer_norm(P, z[q], "z%d" % q, stats, mv, lng, lnb, LN_EPS / ALPHA ** 2, xq, "x32_%d" % q3)
        P.dma("sp", h2[r0:r0 + 128, :], xq[:], reads=["x32_%d" % q3], writes=["h2_%d" % gi], key="st32_%d" % q3)
        P.op("act", lambda e: e.copy(hb[q3][:], xq[:]), reads=["x32_%d" % q3], writes=["hb%d" % q3])

    def load_x(ti, s):
        gi = ti * 4 + s
        r0 = ti * 512 + s * 128
        P.dma("sp", x32[gi % 3][:], h1[r0:r0 + 128, :], writes=["x32_%d" % (gi % 3)], key="x32_%d" % (gi % 3))

    def htrans(ti, s):
        tb = ti % 2
        q = (ti * 4 + s) % 2
        q3 = (ti * 4 + s) % 3
        for k in range(8):
            P.op("pe", lambda e, k=k: e.transpose(pT[q][:, k, :], hb[q3][:, k * 128:(k + 1) * 128], idt[:]),
                 reads=["hb%d" % q3, "idt"], writes=["pT%d_%d" % (q, k)])
        P.op("act", lambda e: e.copy(hTt[tb][:, :, s * 128:(s + 1) * 128], pT[q][:]),
             reads=["pT%d_%d" % (q, k) for k in range(8)], writes=["hTt%d_%d" % (tb, s)])
        if s == 3:
            P.dma("sp", h2T[:, :, ti * 512:(ti + 1) * 512], hTt[tb][:], reads=["hTt%d_%d" % (tb, ss) for ss in range(4)],
                  writes=["h2T_%d" % ti], key="stT_%d" % tb)

    steps = [(ti, s) for ti in range(NT // 512) for s in range(4)]
    load_tile(0)
    load_x(*steps[0])
    for idx, (ti, s) in enumerate(steps):
        if s == 0 and ti + 1 < NT // 512:
            load_tile(ti + 1)
        if idx + 1 < len(steps):
            load_x(*steps[idx + 1])
        main(ti, s)
        if idx >= 2:
            htrans(*steps[idx - 2])
    htrans(*steps[-2])
    htrans(*steps[-1])
    ph.close()


def gate_phase(nc, P, NT, W, CN, p_d, h3, h3T, y):
    ph = Phase(nc, P)
    wpg = ph.sb([128, 8, D], BF16)
    wpe = ph.sb([128, 2, D], BF16)
    bpg = ph.sb([128, D], F32)
    idt = ph.sb([128, 128], BF16)
    hTt = [ph.sb([128, 8, 512], BF16) for _ in range(2)]
    pb = [ph.sb([128, 4, PD], BF16) for _ in range(2)]
    ppT = [ph.sb([128, 2, 128], BF16) for _ in range(2)]
    x32 = [ph.sb([128, D], F32) for _ in range(3)]
    gs = [ph.sb([128, D], F32) for _ in range(2)]
    pes = [ph.sb([128, D], F32) for _ in range(2)]
    pgt2 = [ph.ps([128, 1024], F32) for _ in range(2)]
    ppe = ph.ps([128, 1024], F32)
    pT = [ph.ps([128, 2, 128], BF16) for _ in range(2)]
    for k in range(8):
        P.dma("pool", wpg[:, k, :], W["w_pg"][k * 128:(k + 1) * 128, :], writes=["wpg%d" % k], key="wpg")
    for k in range(2):
        P.dma("pool", wpe[:, k, :], W["w_pe"][k * 128:(k + 1) * 128, :], writes=["wpe%d" % k], key="wpe")
    WPG = ["wpg%d" % k for k in range(8)]
    WPE = ["wpe%d" % k for k in range(2)]
    P.dma("sp", bpg[:], bcast_rows(W["b_pg"], 128), writes=["bpg"], key="bpg")
    P.dma("sp", idt[:], CN["ident"], writes=["idt"], key="idt")
    def load_tile(ti):
        tb = ti % 2
        P.dma("sp", hTt[tb][:], h3T[:, :, ti * 512:(ti + 1) * 512], writes=["hTt%d" % tb], key="hTt%d" % tb)
        P.dma("pool", pb[tb][:], p_d[ti * 512:(ti + 1) * 512, :].rearrange("(s p) f -> p s f", p=128), writes=["pb%d" % tb], key="pb%d" % tb)

    def ptrans(ti, s):
        q = (ti * 4 + s) % 2
        tb = ti % 2

        def tp(e):
            for c in range(2):
                ins = e.transpose(pT[q][:, c, :], pb[tb][:, s, c * 128:(c + 1) * 128], idt[:])
            return ins
        P.op("pe", tp, reads=["pb%d" % tb, "idt"], writes=["pT%d" % q])
        P.op("act", lambda e: e.copy(ppT[q][:], pT[q][:]), reads=["pT%d" % q], writes=["ppT%d" % q])

    def main(ti, s):
        tb = ti % 2
        gi = ti * 4 + s
        q = gi % 2
        r0 = ti * 512 + s * 128
        pgt = pgt2[q]

        def gm(e):
            for hf in range(2):
                cs = slice(hf * 512, (hf + 1) * 512)
                for k in range(8):
                    ins = e.matmul(pgt[:, cs], hTt[tb][:, k, s * 128:(s + 1) * 128], wpg[:, k, cs], start=(k == 0), stop=(k == 7))
            return ins
        P.op("pe", gm, reads=["hTt%d" % tb] + WPG, writes=["pgt%d" % q])

        def pm(e):
            for hf in range(2):
                cs = slice(hf * 512, (hf + 1) * 512)
                for c in range(2):
                    ins = e.matmul(ppe[:, cs], ppT[q][:, c, :], wpe[:, c, cs], start=(c == 0), stop=(c == 1))
            return ins
        P.op("pe", pm, reads=["ppT%d" % q] + WPE, writes=["ppe"])
        P.op("act", lambda e: e.copy(pes[q][:], ppe[:]), reads=["ppe"], writes=["pes%d" % q])
        g = gs[q]
        gn = "gs%d" % q
        P.op("dve", lambda e: e.tensor_tensor(g[:], pgt[:], bpg[:], ALU.add), reads=["pgt%d" % q, "bpg"], writes=[gn])
        P.op("act", lambda e: e.activation(g[:], g[:], AF.Tanh, scale=0.5), reads=[gn], writes=[gn])
        P.op("dve", lambda e: e.tensor_scalar(g[:], g[:], 0.5, 0.5, ALU.mult, ALU.add), reads=[gn], writes=[gn])
        P.op("dve", lambda e: e.tensor_tensor(g[:], g[:], pes[q][:], ALU.mult), reads=[gn, "pes%d" % q], writes=[gn])
        P.op("pool", lambda e: e.tensor_tensor(g[:], g[:], x32[gi % 3][:], ALU.add), reads=[gn, "x32_%d" % (gi % 3)], writes=[gn])
        P.dma("sp", y[r0:r0 + 128, :], g[:], reads=[gn], writes=["y%d" % gi], key="sty%d" % q)

    def load_x(ti, s):
        gi = ti * 4 + s
        r0 = ti * 512 + s * 128
        P.dma("sp", x32[gi % 3][:], h3[r0:r0 + 128, :], writes=["x32_%d" % (gi % 3)], key="x32_%d" % (gi % 3))

    steps = [(ti, s) for ti in range(NT // 512) for s in range(4)]
    load_tile(0)
    load_x(*steps[0])
    ptrans(*steps[0])
    for idx, (ti, s) in enumerate(steps):
        if s == 0 and ti + 1 < NT // 512:
            load_tile(ti + 1)
        if idx + 1 < len(steps):
            load_x(*steps[idx + 1])
            ptrans(*steps[idx + 1])
        main(ti, s)
    ph.close()


W_NAMES = ["ffn1_wg", "ffn1_wu", "ffn1_wd", "ln1_g", "ln1_b", "w_in", "gla_w2f", "gla_b2f", "gla_w2b", "gla_b2b",
           "gla_gn_g", "q_norm_g", "k_norm_g", "w_out", "ln2_g", "ln2_b", "ffn2_wg", "ffn2_wu", "ffn2_wd",
           "ln3_g", "ln3_b", "w_pg", "b_pg", "w_pe"]
W_SHAPES = {"ffn1_wg": [D, DFF], "ffn1_wu": [D, DFF], "ffn1_wd": [DFF, D], "ln1_g": [D], "ln1_b": [D], "w_in": [D, DIN],
            "gla_w2f": [16, 256], "gla_b2f": [256], "gla_w2b": [16, 256], "gla_b2b": [256], "gla_gn_g": [128],
            "q_norm_g": [64], "k_norm_g": [64], "w_out": [D, D], "ln2_g": [D], "ln2_b": [D], "ffn2_wg": [D, DFF],
            "ffn2_wu": [D, DFF], "ffn2_wd": [DFF, D], "ln3_g": [D], "ln3_b": [D], "w_pg": [D, D], "b_pg": [D], "w_pe": [PD, D]}


def build(T, NSEQ, upto=99, debug=False, halfB=False):
    NT = T * NSEQ
    NTE = NT - T // 2 if halfB else NT
    nc = bass.Bass("TRN2", target_bir_lowering=False)
    x = nc.dram_tensor("x", [NT, D], F32, kind="ExternalInput").ap()
    p = nc.dram_tensor("p", [NT, PD], F32, kind="ExternalInput").ap()
    W = {n: nc.dram_tensor(n, W_SHAPES[n], F32, kind="ExternalInput").ap() for n in W_NAMES}
    CN = {"ident": nc.dram_tensor("ident", [128, 128], BF16, kind="ExternalInput").ap(),
          "ropeC": nc.dram_tensor("ropeC", [T, 64], F32, kind="ExternalInput").ap(),
          "ropeS": nc.dram_tensor("ropeS", [T, 64], F32, kind="ExternalInput").ap(),
          "glaM": nc.dram_tensor("glaM", [2, 3, 128, 128], F32, kind="ExternalInput").ap(),
          "glaInd": nc.dram_tensor("glaInd", [128, 16], F32, kind="ExternalInput").ap(),
          "glaMask": nc.dram_tensor("glaMask", [2, 128, 128], F32, kind="ExternalInput").ap()}
    y = nc.dram_tensor("y", [NTE, D], F32, kind="ExternalOutput").ap()
    sk = "ExternalOutput" if debug else "Internal"
    dt = lambda n, sh, ty: nc.dram_tensor(n, sh, ty, kind=sk).ap()
    h1 = dt("h1", [NT, D], F32)
    h1T = dt("h1T", [128, 8, NT], BF16)
    SC = {"mixA": dt("mixA", [128, 4, NT], BF16), "mixG": dt("mixG", [128, 4, NT], BF16),
          "gp": dt("gp", [NT, 1536], BF16), "la": dt("la", [NT, 512], F32), "of": dt("of", [NT, 512], F32)}
    h2 = dt("h2", [NT, D], F32)
    h2T = dt("h2T", [128, 8, NT], BF16)
    h3 = dt("h3", [NT, D], F32)
    h3T = dt("h3T", [128, 8, NT], BF16)
    with contextlib.ExitStack() as stack:
        P = Prog(nc, stack)
        ffn_phase(nc, P, "f1", NT, x, None, W["ffn1_wg"], W["ffn1_wu"], W["ffn1_wd"], W["ln1_g"], W["ln1_b"], CN["ident"],
                  h1, h1T, 0.5 / ALPHA, LN_EPS / ALPHA ** 2)
        for seq in range(NSEQ):
            hb = halfB and seq == NSEQ - 1
            if upto >= 2:
                attn_phase(nc, P, T, seq, h1T, W, CN, SC["mixA"], nqb=(T // 1024 if hb else None))
            if upto >= 3:
                gla_phase(nc, P, T, seq, h1T, W, CN, SC, half=hb)
        NT = NTE
        if upto >= 4:
            wout_phase(nc, P, NT, W, CN, SC, h1, h2, h2T)
        if upto >= 5:
            ffn_phase(nc, P, "f2", NT, h2, h2T, W["ffn2_wg"], W["ffn2_wu"], W["ffn2_wd"], W["ln3_g"], W["ln3_b"], CN["ident"],
                      h3, h3T, 0.5 / ALPHA, LN_EPS / ALPHA ** 2)
            gate_phase(nc, P, NT, W, CN, p, h3, h3T, y)
    return nc


def gla_consts():
    t = np.arange(128)
    a = t % 64
    c = t // 64
    same = (c[:, None] == c[None, :]).astype(np.float32)
    aj, ai = a[:, None], a[None, :]
    M = np.zeros((2, 3, 128, 128), np.float32)
    M[0, 0] = same * ((aj <= ai).astype(np.float32) - (aj <= 31).astype(np.float32))
    M[0, 1] = same * (aj > ai)
    M[0, 2] = same * (aj <= ai)
    M[1, 0] = same * ((aj >= ai).astype(np.float32) - (aj >= 32).astype(np.float32))
    M[1, 1] = same * (aj < ai)
    M[1, 2] = same * (aj >= ai)
    M *= -1.0 / 16
    Ind = np.zeros((128, 16), np.float32)
    Ind[t, c] = -1.0 / 16
    mask = np.zeros((2, 128, 128), np.float32)
    mask[0] = same * (aj <= ai)
    mask[1] = same * (aj > ai)
    return M, Ind, mask


def rope_tables(T):
    t = np.arange(T)
    row = (t // 64).astype(np.float32)
    col = (t % 64).astype(np.float32)
    inv = (10000.0 ** (-np.arange(0, 32, 2, dtype=np.float32) / 32)).astype(np.float32)
    ar = row[:, None] * inv[None, :]
    ac = col[:, None] * inv[None, :]
    C = np.concatenate([np.cos(ar), np.cos(ar), np.cos(ac), np.cos(ac)], axis=1).astype(np.float32)
    S = np.concatenate([-np.sin(ar), np.sin(ar), -np.sin(ac), np.sin(ac)], axis=1).astype(np.float32)
    return C, S


def consts(T, flip=False):
    C, S = rope_tables(T)
    M, Ind, mask = gla_consts()
    if flip:
        C, S = C[::-1].copy(), S[::-1].copy()
        t = np.arange(128)
        a = t % 64
        same = ((t // 64)[:, None] == (t // 64)[None, :]).astype(np.float32)
        mask = np.stack([same * (a[:, None] < a[None, :]), same * (a[:, None] >= a[None, :])]).astype(np.float32)
    return {"ident": np.eye(128, dtype=np.float32).astype(ml_dtypes.bfloat16), "ropeC": C, "ropeS": S,
            "glaM": M, "glaInd": Ind, "glaMask": mask}


_CACHE = {}


def kernel(**inputs):
    T, NSEQ, NCORE = 4096, 2, 8
    H = T // 2
    xp = np.asarray(inputs["x_prompt"], np.float32)
    xs = np.asarray(inputs["x_sample"], np.float32)
    pp = np.asarray(inputs["p_prompt"], np.float32)[0]
    ps = np.asarray(inputs["p_sample"], np.float32)[0]
    Wn = {n: np.ascontiguousarray(np.asarray(inputs[n], np.float32)[0]) for n in W_NAMES}
    Wf = dict(Wn)
    Wf["gla_w2f"], Wf["gla_w2b"] = Wn["gla_w2b"], Wn["gla_w2f"]
    Wf["gla_b2f"], Wf["gla_b2b"] = Wn["gla_b2b"], Wn["gla_b2f"]
    wi = Wn["w_in"].copy()
    wi[:, 1536:1552], wi[:, 1552:1568] = Wn["w_in"][:, 1552:1568], Wn["w_in"][:, 1536:1552]
    Wf["w_in"] = wi
    cn = [consts(T, False), consts(T, True)]
    if "nc" not in _CACHE:
        _CACHE["nc"] = build(T, NSEQ, halfB=True)
    nc = _CACHE["nc"]
    in_maps = []
    for c in range(NCORE):
        flip = c >= 4
        xa, pa, xb, pb = xp[c], pp[c], xs[c % 4], ps[c % 4]
        if flip:
            xa, pa, xb, pb = xa[::-1], pa[::-1], xb[::-1], pb[::-1]
        m = {"x": np.ascontiguousarray(np.concatenate([xa, xb], 0)),
             "p": np.ascontiguousarray(np.concatenate([pa, pb], 0))}
        m.update(Wf if flip else Wn)
        m.update(cn[1] if flip else cn[0])
        in_maps.append(m)
    res = run_bass_kernel_spmd(nc, in_maps, core_ids=list(range(NCORE)))
    ys = [np.asarray(res.results[c]["y"], np.float32) for c in range(NCORE)]
    yp = np.stack([ys[c][:T] if c < 4 else ys[c][:T][::-1] for c in range(8)], 0)
    ysm = np.stack([np.concatenate([ys[j][T:T + H], ys[j + 4][T:T + H][::-1]], 0) for j in range(4)], 0)
    return (np.ascontiguousarray(yp), np.ascontiguousarray(ysm))
```
